# Optimizing a Trainium2 kernel written in Bass

```python
import math
import jax, jax.numpy as jnp
from jax import lax
import numpy as np

D_MODEL = 1024
BATCH = 16
SEQ = 2048
DEPTH = 2
DEC_BATCH = 16
DEC_SEQ = 4096
PAST_LEN = 128

DN_ALPHA = (2 * DEPTH) ** 0.25
DN_BETA = (8 * DEPTH) ** -0.25
LN_EPS = 1e-5
RMS_EPS = 1e-6

MLA_HEADS = 8
MLA_Q_RANK = 768
MLA_KV_RANK = 256
MLA_NOPE = 64
MLA_ROPE = 32
MLA_V = 64
ROPE_BASE = 10000.0
Q_BLOCK = 128

GLA_HEADS = 4
GLA_DK = 64
GLA_DV = 128
GLA_GATE_RANK = 16
GLA_TAU = 16.0
GLA_CHUNK = 64

AB_WIDTHS = (MLA_Q_RANK, MLA_KV_RANK, MLA_ROPE,
             GLA_HEADS * GLA_DK, GLA_HEADS * GLA_DK, GLA_HEADS * GLA_DV, GLA_HEADS * GLA_DV,
             GLA_GATE_RANK, GLA_GATE_RANK)
AB_IN = sum(AB_WIDTHS)
AB_OUT = MLA_HEADS * MLA_V + GLA_HEADS * GLA_DV

SGU_CHUNK = 128
SGU_HIDDEN = 6 * D_MODEL
SGU_HALF = SGU_HIDDEN // 2
SGU_GROUPS = 8
SGU_GROUP_DIM = SGU_HALF // SGU_GROUPS

FFN_HIDDEN = 2816

kernel_name = 'hybrid_mla_gla_sgu_macaron_deepnorm_encoder'


def layer_norm(x, g, b):
    xf = x.astype(jnp.float32)
    mu = jnp.mean(xf, axis=-1, keepdims=True)
    var = jnp.mean(jnp.square(xf - mu), axis=-1, keepdims=True)
    return ((xf - mu) * lax.rsqrt(var + LN_EPS) * g.astype(jnp.float32) + b.astype(jnp.float32)).astype(x.dtype)


def rms_norm(x, g):
    xf = x.astype(jnp.float32)
    ms = jnp.mean(jnp.square(xf), axis=-1, keepdims=True)
    return (xf * lax.rsqrt(ms + RMS_EPS) * g.astype(jnp.float32)).astype(x.dtype)


def split_cols(t, widths):
    out, start = [], 0
    for w in widths:
        out.append(t[..., start:start + w])
        start += w
    return out


def swiglu(h, w_gu, w_down):
    gate, up = jnp.split(h @ w_gu, 2, axis=-1)
    return (jax.nn.silu(gate) * up) @ w_down


def rope_tables(seq, dtype):
    inv_freq = 1.0 / (ROPE_BASE ** (jnp.arange(0, MLA_ROPE, 2, dtype=jnp.float32) / MLA_ROPE))
    ang = jnp.arange(seq, dtype=jnp.float32)[:, None] * inv_freq[None, :]
    return jnp.cos(ang).astype(dtype), jnp.sin(ang).astype(dtype)


def apply_rope(x, cos, sin):
    x1, x2 = jnp.split(x, 2, axis=-1)
    return jnp.concatenate([x1 * cos - x2 * sin, x1 * sin + x2 * cos], axis=-1)


def mla_attention(c_q, c_kv, k_r, q_norm, w_uq, kv_norm, w_ukv):
    B, S, _ = c_q.shape
    H = MLA_HEADS
    q = (rms_norm(c_q, q_norm) @ w_uq).reshape(B, S, H, MLA_NOPE + MLA_ROPE)
    kv = (rms_norm(c_kv, kv_norm) @ w_ukv).reshape(B, S, H, MLA_NOPE + MLA_V)
    q_nope, q_rope = q[..., :MLA_NOPE], q[..., MLA_NOPE:]
    k_nope, v = kv[..., :MLA_NOPE], kv[..., MLA_NOPE:]
    cos, sin = rope_tables(S, q.dtype)
    q_rope = apply_rope(q_rope, cos[:, None, :], sin[:, None, :])
    k_rope = apply_rope(k_r, cos, sin)
    scale = (MLA_NOPE + MLA_ROPE) ** -0.5
    nb = S // Q_BLOCK
    qn_b = jnp.moveaxis(q_nope.reshape(B, nb, Q_BLOCK, H, MLA_NOPE), 1, 0)
    qr_b = jnp.moveaxis(q_rope.reshape(B, nb, Q_BLOCK, H, MLA_ROPE), 1, 0)

    def block(args):
        qn, qr = args
        s = (jnp.einsum('bqhd,bkhd->bhqk', qn, k_nope)
             + jnp.einsum('bqhr,bkr->bhqk', qr, k_rope))
        p = jax.nn.softmax(s.astype(jnp.float32) * scale, axis=-1).astype(v.dtype)
        return jnp.einsum('bhqk,bkhd->bqhd', p, v)

    o = lax.map(block, (qn_b, qr_b))
    return jnp.moveaxis(o, 0, 1).reshape(B, S, H * MLA_V)


def gla_chunked(q, k, v, log_a, strict):
    B, S, H, DK = q.shape
    DV = v.shape[-1]
    C = GLA_CHUNK
    n = S // C
    q, k, log_a = (t.reshape(B, n, C, H, DK) for t in (q, k, log_a))
    v = v.reshape(B, n, C, H, DV)
    b = lax.cumsum(log_a, axis=2)
    b_last = b[:, :, -1:]
    q_in = q * jnp.exp(b)
    k_in = k * jnp.exp(-b)
    k_st = k * jnp.exp(b_last - b)
    mask = jnp.tril(jnp.ones((C, C), dtype=bool), k=-1 if strict else 0)
    att = jnp.where(mask, jnp.einsum('bnthd,bnshd->bnhts', q_in, k_in), 0.0)
    o_intra = jnp.einsum('bnhts,bnshe->bnthe', att, v)
    d_state = jnp.einsum('bnshd,bnshe->bnhde', k_st, v)
    decay = jnp.exp(b_last[:, :, 0])

    def step(state, inp):
        ds, dec = inp
        return dec[..., None] * state + ds, state

    s0 = jnp.zeros((B, H, DK, DV), dtype=q.dtype)
    _, s_before = lax.scan(step, s0, (jnp.moveaxis(d_state, 1, 0), jnp.moveaxis(decay, 1, 0)))
    s_before = jnp.moveaxis(s_before, 0, 1)
    o_inter = jnp.einsum('bnthd,bnhde->bnthe', q_in, s_before)
    return (o_intra + o_inter).reshape(B, S, H, DV)


def gla_bidirectional(q, k, v, r, z_f, z_b, w_gate_f, b_gate_f, w_gate_b, b_gate_b, norm_g):
    B, S, _ = q.shape
    H = GLA_HEADS
    f32 = jnp.float32
    qf = q.astype(f32).reshape(B, S, H, GLA_DK) * (GLA_DK ** -0.5)
    kf = k.astype(f32).reshape(B, S, H, GLA_DK)
    vf = v.astype(f32).reshape(B, S, H, GLA_DV)
    la_f = (jax.nn.log_sigmoid((z_f @ w_gate_f + b_gate_f).astype(f32)) / GLA_TAU).reshape(B, S, H, GLA_DK)
    la_b = (jax.nn.log_sigmoid((z_b @ w_gate_b + b_gate_b).astype(f32)) / GLA_TAU).reshape(B, S, H, GLA_DK)
    flip = lambda t: jnp.flip(t, axis=1)
    o_fwd = gla_chunked(qf, kf, vf, la_f, strict=False)
    o_bwd = flip(gla_chunked(flip(qf), flip(kf), flip(vf), flip(la_b), strict=True))
    o = rms_norm(o_fwd + o_bwd, norm_g)
    return o.reshape(B, S, H * GLA_DV).astype(r.dtype) * jax.nn.silu(r)


def mixer_ab(h, w_in, mla_q_norm, mla_w_uq, mla_kv_norm, mla_w_ukv,
             gla_w_gate_f, gla_b_gate_f, gla_w_gate_b, gla_b_gate_b, gla_norm, w_out):
    c_q, c_kv, k_r, q, k, v, r, z_f, z_b = split_cols(h @ w_in, AB_WIDTHS)
    o_a = mla_attention(c_q, c_kv, k_r, mla_q_norm, mla_w_uq, mla_kv_norm, mla_w_ukv)
    o_b = gla_bidirectional(q, k, v, r, z_f, z_b, gla_w_gate_f, gla_b_gate_f,
                            gla_w_gate_b, gla_b_gate_b, gla_norm)
    return jnp.concatenate([o_a, o_b], axis=-1) @ w_out


def mixer_c(h, w_in, ln_g, ln_b, w_s, b_s, w_out):
    B, S, _ = h.shape
    n = S // SGU_CHUNK
    u, v = jnp.split(jax.nn.gelu(h @ w_in, approximate=False), 2, axis=-1)
    v = layer_norm(v, ln_g, ln_b).reshape(B, n, SGU_CHUNK, SGU_GROUPS, SGU_GROUP_DIM)
    v = jnp.einsum('gts,bnsgc->bntgc', w_s, v) + jnp.transpose(b_s)[None, None, :, :, None]
    return (u * v.reshape(B, S, SGU_HALF)) @ w_out


def post_norm(x, f, g, b):
    return layer_norm(DN_ALPHA * x + f, g, b)


def layer_even(x, ffa_gu, ffa_down, ln1_g, ln1_b, w_in, mla_q_norm, mla_w_uq, mla_kv_norm,
               mla_w_ukv, gla_w_gate_f, gla_b_gate_f, gla_w_gate_b, gla_b_gate_b, gla_norm, w_out,
               ln2_g, ln2_b, ffb_gu, ffb_down, ln3_g, ln3_b):
    x = post_norm(x, 0.5 * swiglu(x, ffa_gu, ffa_down), ln1_g, ln1_b)
    x = post_norm(x, mixer_ab(x, w_in, mla_q_norm, mla_w_uq, mla_kv_norm, mla_w_ukv,
                              gla_w_gate_f, gla_b_gate_f, gla_w_gate_b, gla_b_gate_b,
                              gla_norm, w_out), ln2_g, ln2_b)
    return post_norm(x, 0.5 * swiglu(x, ffb_gu, ffb_down), ln3_g, ln3_b)


def layer_odd(x, ffa_gu, ffa_down, ln1_g, ln1_b, sgu_w_in, sgu_ln_g, sgu_ln_b, sgu_w_s, sgu_b_s,
              sgu_w_out, ln2_g, ln2_b, ffb_gu, ffb_down, ln3_g, ln3_b):
    x = post_norm(x, 0.5 * swiglu(x, ffa_gu, ffa_down), ln1_g, ln1_b)
    x = post_norm(x, mixer_c(x, sgu_w_in, sgu_ln_g, sgu_ln_b, sgu_w_s, sgu_b_s, sgu_w_out), ln2_g, ln2_b)
    return post_norm(x, 0.5 * swiglu(x, ffb_gu, ffb_down), ln3_g, ln3_b)


def _dense(key, fan_in, fan_out, scale=1.0):
    return jax.random.normal(key, (fan_in, fan_out), jnp.float32) * (scale * fan_in ** -0.5)


def _gain(key, shape):
    return 1.0 + 0.02 * jax.random.normal(key, shape, jnp.float32)


def _bias(key, shape, scale=0.02):
    return scale * jax.random.normal(key, shape, jnp.float32)


def setup_inputs(seed: int = 0) -> dict:
    key = jax.random.key(seed)
    ks = iter(jax.random.split(key, 64))
    d = D_MODEL
    inp = {}
    inp['x_prompt'] = jax.random.normal(next(ks), (BATCH, SEQ, d), jnp.float32)
    inp['x_sample'] = jax.random.normal(next(ks), (DEC_BATCH, DEC_SEQ, d), jnp.float32)
    inp['l0_ffa_w_gu'] = _dense(next(ks), d, 2 * FFN_HIDDEN)
    inp['l0_ffa_w_down'] = _dense(next(ks), FFN_HIDDEN, d, DN_BETA)
    inp['l0_ln1_g'] = _gain(next(ks), (d,))
    inp['l0_ln1_b'] = _bias(next(ks), (d,))
    inp['l0_w_in'] = _dense(next(ks), d, AB_IN)
    inp['l0_mla_q_norm'] = _gain(next(ks), (MLA_Q_RANK,))
    inp['l0_mla_w_uq'] = _dense(next(ks), MLA_Q_RANK, MLA_HEADS * (MLA_NOPE + MLA_ROPE))
    inp['l0_mla_kv_norm'] = _gain(next(ks), (MLA_KV_RANK,))
    inp['l0_mla_w_ukv'] = _dense(next(ks), MLA_KV_RANK, MLA_HEADS * (MLA_NOPE + MLA_V))
    inp['l0_gla_w_gate_f'] = _dense(next(ks), GLA_GATE_RANK, GLA_HEADS * GLA_DK)
    inp['l0_gla_b_gate_f'] = _bias(next(ks), (GLA_HEADS * GLA_DK,), 0.1)
    inp['l0_gla_w_gate_b'] = _dense(next(ks), GLA_GATE_RANK, GLA_HEADS * GLA_DK)
    inp['l0_gla_b_gate_b'] = _bias(next(ks), (GLA_HEADS * GLA_DK,), 0.1)
    inp['l0_gla_norm'] = _gain(next(ks), (GLA_DV,))
    inp['l0_w_out'] = _dense(next(ks), AB_OUT, d, DN_BETA)
    inp['l0_ln2_g'] = _gain(next(ks), (d,))
    inp['l0_ln2_b'] = _bias(next(ks), (d,))
    inp['l0_ffb_w_gu'] = _dense(next(ks), d, 2 * FFN_HIDDEN)
    inp['l0_ffb_w_down'] = _dense(next(ks), FFN_HIDDEN, d, DN_BETA)
    inp['l0_ln3_g'] = _gain(next(ks), (d,))
    inp['l0_ln3_b'] = _bias(next(ks), (d,))
    inp['l1_ffa_w_gu'] = _dense(next(ks), d, 2 * FFN_HIDDEN)
    inp['l1_ffa_w_down'] = _dense(next(ks), FFN_HIDDEN, d, DN_BETA)
    inp['l1_ln1_g'] = _gain(next(ks), (d,))
    inp['l1_ln1_b'] = _bias(next(ks), (d,))
    inp['l1_sgu_w_in'] = _dense(next(ks), d, SGU_HIDDEN)
    inp['l1_sgu_ln_g'] = _gain(next(ks), (SGU_HALF,))
    inp['l1_sgu_ln_b'] = _bias(next(ks), (SGU_HALF,))
    inp['l1_sgu_w_s'] = jax.random.normal(next(ks), (SGU_GROUPS, SGU_CHUNK, SGU_CHUNK), jnp.float32) * (SGU_CHUNK ** -0.5)
    inp['l1_sgu_b_s'] = _gain(next(ks), (SGU_GROUPS, SGU_CHUNK))
    inp['l1_sgu_w_out'] = _dense(next(ks), SGU_HALF, d, DN_BETA)
    inp['l1_ln2_g'] = _gain(next(ks), (d,))
    inp['l1_ln2_b'] = _bias(next(ks), (d,))
    inp['l1_ffb_w_gu'] = _dense(next(ks), d, 2 * FFN_HIDDEN)
    inp['l1_ffb_w_down'] = _dense(next(ks), FFN_HIDDEN, d, DN_BETA)
    inp['l1_ln3_g'] = _gain(next(ks), (d,))
    inp['l1_ln3_b'] = _bias(next(ks), (d,))
    return inp


def reference(x_prompt, x_sample,
              l0_ffa_w_gu, l0_ffa_w_down, l0_ln1_g, l0_ln1_b, l0_w_in, l0_mla_q_norm, l0_mla_w_uq,
              l0_mla_kv_norm, l0_mla_w_ukv, l0_gla_w_gate_f, l0_gla_b_gate_f, l0_gla_w_gate_b,
              l0_gla_b_gate_b, l0_gla_norm, l0_w_out, l0_ln2_g, l0_ln2_b, l0_ffb_w_gu, l0_ffb_w_down,
              l0_ln3_g, l0_ln3_b,
              l1_ffa_w_gu, l1_ffa_w_down, l1_ln1_g, l1_ln1_b, l1_sgu_w_in, l1_sgu_ln_g, l1_sgu_ln_b,
              l1_sgu_w_s, l1_sgu_b_s, l1_sgu_w_out, l1_ln2_g, l1_ln2_b, l1_ffb_w_gu, l1_ffb_w_down,
              l1_ln3_g, l1_ln3_b):
    even_params = (l0_ffa_w_gu, l0_ffa_w_down, l0_ln1_g, l0_ln1_b, l0_w_in, l0_mla_q_norm, l0_mla_w_uq,
                   l0_mla_kv_norm, l0_mla_w_ukv, l0_gla_w_gate_f, l0_gla_b_gate_f, l0_gla_w_gate_b,
                   l0_gla_b_gate_b, l0_gla_norm, l0_w_out, l0_ln2_g, l0_ln2_b, l0_ffb_w_gu,
                   l0_ffb_w_down, l0_ln3_g, l0_ln3_b)
    odd_params = (l1_ffa_w_gu, l1_ffa_w_down, l1_ln1_g, l1_ln1_b, l1_sgu_w_in, l1_sgu_ln_g, l1_sgu_ln_b,
                  l1_sgu_w_s, l1_sgu_b_s, l1_sgu_w_out, l1_ln2_g, l1_ln2_b, l1_ffb_w_gu, l1_ffb_w_down,
                  l1_ln3_g, l1_ln3_b)
    layer_params = (even_params, odd_params)

    def trunk(x):
        for i in range(DEPTH):
            if i % 2 == 0:
                x = layer_even(x, *layer_params[i])
            else:
                x = layer_odd(x, *layer_params[i])
        return x

    y_prompt = trunk(x_prompt)
    y_sample = trunk(x_sample)
    return (y_prompt, y_sample)
```

```python
import numpy as np
import concourse.bass as bass
import concourse.mybir as mybir
from concourse.bass_utils import run_bass_kernel_spmd

F32 = mybir.dt.float32
BF16 = mybir.dt.bfloat16
AF = mybir.ActivationFunctionType
ALU = mybir.AluOpType
AX = mybir.AxisListType

D = 1024
DC = 8
FH = 2816
FHC = 22
TB = 256
ALPHA = 4 ** 0.25
LN_EPS = 1e-5
RMS_EPS = 1e-6
N_CORES = 8
import os
DBG = os.environ.get("KDBG", "")


class Buf:
    _n = 0

    def __init__(self, name=""):
        Buf._n += 1
        self.key = "b%d_%s" % (Buf._n, name)
        self.prev = {}
        self.lw = {}
        self.rd = {}
        self.sem = None
        self.dcnt = 0


def _merge(dst, src):
    for k, (s, v) in src.items():
        if k not in dst or dst[k][1] < v:
            dst[k] = (s, v)


class Eng:
    def __init__(self, key, h, sem):
        self.key = key
        self.h = h
        self.sem = sem
        self.cnt = 0
        self.seen = {}


class KB:
    def __init__(self, nc, es):
        self.nc = nc
        self.es = es
        self.eng = {}
        for key, h in (("pe", nc.tensor), ("act", nc.scalar), ("dve", nc.vector), ("pool", nc.gpsimd), ("sp", nc.sync)):
            sem = es.enter_context(nc.semaphore("sem_" + key))
            self.eng[key] = Eng(key, h, sem)
        self.dma_bufs = []
        self.free_sems = []
        self.nsem = 0
        self.nins = 0

    def dma_sem(self, b):
        if b.sem is None:
            if self.free_sems:
                b.sem, b.dcnt, b.semkey = self.free_sems.pop()
            else:
                self.nsem += 1
                b.semkey = "dsem%d" % self.nsem
                b.sem = self.es.enter_context(self.nc.semaphore(b.semkey))
                b.dcnt = 0
            self.dma_bufs.append(b)
        return b.sem

    def recycle(self):
        for b in self.dma_bufs:
            self.free_sems.append((b.sem, b.dcnt, b.semkey))
            b.sem = None
        self.dma_bufs = []

    def emit(self, eng, fn, reads=(), writes=(), pwrites=(), dma_buf=None, signal=True):
        E = self.eng[eng]
        deps = {}
        for b in reads:
            _merge(deps, b.lw)
        for b in writes:
            p = {}
            _merge(p, b.lw)
            _merge(p, b.rd)
            b.prev = p
            b.lw = {}
            b.rd = {}
        for b in list(writes) + list(pwrites):
            _merge(deps, b.prev)
        for k, (s, v) in deps.items():
            if k == "pe" and eng == "pe":
                continue
            if E.seen.get(k, 0) >= v:
                continue
            E.h.wait_ge(s, v)
            E.seen[k] = v
        ins = fn(E.h)
        self.nins += 1
        if dma_buf is not None:
            sem = self.dma_sem(dma_buf)
            dma_buf.dcnt += 16
            ins.then_inc(sem, 16)
            ev = (dma_buf.semkey, sem, dma_buf.dcnt)
        elif signal:
            E.cnt += 1
            ins.then_inc(E.sem, 1)
            ev = (E.key, E.sem, E.cnt)
        else:
            ev = (E.key, E.sem, E.cnt + 1)
        d = {ev[0]: (ev[1], ev[2])}
        for b in reads:
            _merge(b.rd, d)
        for b in list(writes) + list(pwrites):
            _merge(b.lw, d)
        return ins

    def barrier(self):
        for E in self.eng.values():
            for E2 in self.eng.values():
                if E2 is E or E2.cnt == 0:
                    continue
                if E.seen.get(E2.key, 0) >= E2.cnt:
                    continue
                E.h.wait_ge(E2.sem, E2.cnt)
                E.seen[E2.key] = E2.cnt
            for b in self.dma_bufs:
                if b.dcnt == 0 or E.seen.get(b.semkey, 0) >= b.dcnt:
                    continue
                E.h.wait_ge(b.sem, b.dcnt)
                E.seen[b.semkey] = b.dcnt


class Ctx:
    pass


_sbn = [0]


def sb(es, nc, name, shape, dt):
    _sbn[0] += 1
    return es.enter_context(nc.sbuf_tensor("s%d_%s" % (_sbn[0], name), shape, dt))


def load_weight(kb, cx, W3, dst, dstbuf, KC, N, tag, scale=None):
    nc = kb.nc
    CH = 1536
    i = 0
    first = True
    for k in range(KC):
        for c0 in range(0, N, CH):
            w = min(CH, N - c0)
            st, stb = cx.stage[i % len(cx.stage)]
            kb.emit("sp", lambda h, st=st, k=k, c0=c0, w=w: h.dma_start(out=st[:, 0:w], in_=W3[k * 128:(k + 1) * 128, c0:c0 + w]),
                    writes=[stb], dma_buf=stb)
            e = ("act", "dve", "pool")[i % 3]
            o = dst[:, k, c0:c0 + w]
            if e == "act":
                f = lambda h, st=st, o=o, w=w: h.activation(out=o, in_=st[:, 0:w], func=AF.Copy)
            else:
                f = lambda h, st=st, o=o, w=w: h.tensor_copy(o, st[:, 0:w])
            if first:
                kb.emit(e, f, reads=[stb], writes=[dstbuf])
                first = False
            else:
                kb.emit(e, f, reads=[stb], pwrites=[dstbuf])
            i += 1


def phase_f1(kb, cx, es0, src_mode, Xsrc, XSout, HS, Wgu, NB):
    nc = kb.nc
    import contextlib
    with contextlib.ExitStack() as es:
        wgu = sb(es, nc, "wgu", [128, DC, 2 * FH], BF16)
        wgub = Buf("wgu")
        if "w" not in DBG:
            load_weight(kb, cx, Wgu, wgu, wgub, DC, 2 * FH, "gu")
        NX = 3
        xT = [sb(es, nc, "xT%d" % i, [128, DC, TB], F32) for i in range(NX)]
        xTb = [Buf("xT%d" % i) for i in range(NX)]
        xb = [sb(es, nc, "xb%d" % i, [128, DC, TB], BF16) for i in range(NX)]
        xbb = [Buf("xb%d" % i) for i in range(NX)]
        hh = [sb(es, nc, "hh%d" % i, [128, FHC, TB], BF16) for i in range(2)]
        hhb = [Buf("hh%d" % i) for i in range(2)]
        sg = [sb(es, nc, "sg%d" % i, [128, TB], F32) for i in range(3)]
        sgb = [Buf("sg%d" % i) for i in range(3)]
        if src_mode == "tm":
            xtm = [sb(es, nc, "xtm%d" % i, [128, D], F32) for i in range(4)]
            xtmb = [Buf("xtm%d" % i) for i in range(4)]
        gu_ps = []
        for i in range(2):
            gu_ps.append((cx.ps[i][:, 0:TB], Buf("gps%d" % i), cx.ps[2 + i][:, 0:TB], Buf("ups%d" % i)))
        tp_ps = [(cx.ps[4 + i], Buf("tps%d" % i)) for i in range(4)]
        gi = 0
        si = 0
        ti = 0
        def tm_load(b):
            for i in range(TB // 128):
                t, tb_ = xtm[(2 * b + i) % 4], xtmb[(2 * b + i) % 4]
                r0 = b * TB + i * 128
                kb.emit("sp", lambda h, t=t, r0=r0: h.dma_start(out=t[:], in_=Xsrc[r0:r0 + 128, :]),
                        writes=[tb_], dma_buf=tb_)

        def tm_transpose(b):
            nonlocal ti
            X, Xb_ = xT[b % NX], xTb[b % NX]
            XB, XBb = xb[b % NX], xbb[b % NX]
            tl = [(xtm[(2 * b + i) % 4], xtmb[(2 * b + i) % 4]) for i in range(TB // 128)]
            first = True
            for c2 in range(DC // 2):
                P, Pb = tp_ps[ti % 4]
                ti += 1
                n = 0
                for cc in range(2):
                    c = c2 * 2 + cc
                    for i, (t, tb_) in enumerate(tl):
                        o = P[:, cc * TB + i * 128: cc * TB + (i + 1) * 128]
                        last = (n == 3)
                        f = lambda h, o=o, t=t, c=c: h.transpose(o, t[:, c * 128:(c + 1) * 128], cx.ident[:])
                        if n == 0:
                            kb.emit("pe", f, reads=[tb_, cx.identb], writes=[Pb], signal=last)
                        else:
                            kb.emit("pe", f, reads=[tb_, cx.identb], pwrites=[Pb], signal=last)
                        n += 1
                o32 = X[:, c2 * 2:c2 * 2 + 2, :]
                o16 = XB[:, c2 * 2:c2 * 2 + 2, :]
                pin = P[:, 0:2 * TB].rearrange("p (c t) -> p c t", c=2)
                kw = dict(writes=[Xb_]) if first else dict(pwrites=[Xb_])
                kb.emit("act", lambda h, o32=o32, pin=pin: h.activation(out=o32, in_=pin, func=AF.Copy),
                        reads=[Pb], **kw)
                kw = dict(writes=[XBb]) if first else dict(pwrites=[XBb])
                kb.emit("dve", lambda h, o16=o16, o32=o32: h.tensor_copy(o16, o32), reads=[Xb_], **kw)
                first = False
            dst = XSout[b].rearrange("p (c t) -> p c t", c=DC)
            kb.emit("sp", lambda h, dst=dst, X=X: h.dma_start(out=dst, in_=X[:]), reads=[Xb_],
                    writes=[cx.dram(XSout, b)], dma_buf=Xb_)

        def fm_load(b):
            X, Xb_ = xT[b % NX], xTb[b % NX]
            XB, XBb = xb[b % NX], xbb[b % NX]
            src = Xsrc[b].rearrange("p (c t) -> p c t", c=DC)
            kb.emit("sp", lambda h, src=src, X=X: h.dma_start(out=X[:], in_=src), reads=[cx.dram(Xsrc, b)],
                    writes=[Xb_], dma_buf=Xb_)
            kb.emit("pool", lambda h, X=X, XB=XB: h.tensor_copy(XB[:], X[:]), reads=[Xb_], writes=[XBb])

        if src_mode == "tm":
            tm_load(0)
            tm_transpose(0)
            if NB > 1:
                tm_load(1)
        else:
            fm_load(0)
        for b in range(NB):
            X, Xb_ = xT[b % NX], xTb[b % NX]
            XB, XBb = xb[b % NX], xbb[b % NX]
            if src_mode == "fm" and b + 1 < NB:
                fm_load(b + 1)
            H, Hb = hh[b % 2], hhb[b % 2]
            for j in range(FHC if "c" not in DBG else 0):
                if src_mode == "tm" and j == FHC // 2 and b + 1 < NB:
                    tm_transpose(b + 1)
                    if b + 2 < NB:
                        tm_load(b + 2)
                G, Gb, U, Ub = gu_ps[gi % 2]
                gi += 1
                for k in range(DC):
                    f = lambda h, G=G, j=j, k=k, XB=XB: h.matmul(G, wgu[:, k, j * 128:(j + 1) * 128], XB[:, k, :],
                                                                 start=(k == 0), stop=(k == DC - 1))
                    if k == 0:
                        kb.emit("pe", f, reads=[wgub, XBb], writes=[Gb], signal=False)
                    else:
                        kb.emit("pe", f, reads=[wgub, XBb], pwrites=[Gb], signal=(k == DC - 1))
                for k in range(DC):
                    f = lambda h, U=U, j=j, k=k, XB=XB: h.matmul(U, wgu[:, k, FH + j * 128:FH + (j + 1) * 128],
                                                                 XB[:, k, :], start=(k == 0), stop=(k == DC - 1))
                    if k == 0:
                        kb.emit("pe", f, reads=[wgub, XBb], writes=[Ub], signal=False)
                    else:
                        kb.emit("pe", f, reads=[wgub, XBb], pwrites=[Ub], signal=(k == DC - 1))
                S, Sb = sg[si % 3], sgb[si % 3]
                si += 1
                kb.emit("act", lambda h, S=S, G=G: h.activation(out=S[:], in_=G, func=AF.Silu), reads=[Gb], writes=[Sb])
                o = H[:, j, :]
                f = lambda h, o=o, S=S, U=U: h.tensor_tensor(o, S[:], U, ALU.mult)
                if j == 0:
                    kb.emit("dve", f, reads=[Sb, Ub], writes=[Hb])
                else:
                    kb.emit("dve", f, reads=[Sb, Ub], pwrites=[Hb])
            if src_mode == "tm" and "c" in DBG and b + 1 < NB:
                tm_transpose(b + 1)
                if b + 2 < NB:
                    tm_load(b + 2)
            dst = HS[b][:, 0:FHC * TB].rearrange("p (c t) -> p c t", c=FHC)
            if "h" not in DBG:
                kb.emit("sp", lambda h, dst=dst, H=H: h.dma_start(out=dst, in_=H[:]), reads=[Hb],
                        writes=[cx.dram(HS, b)], dma_buf=Hb)
        kb.barrier()
        kb.recycle()


def phase_f2(kb, cx, AS, KC, Wd, csc, lng, lnb, XSin, dst_mode, Xdst, NB, tagp):
    nc = kb.nc
    import contextlib
    with contextlib.ExitStack() as es:
        wd = sb(es, nc, "wd", [128, KC, D], BF16)
        wdb = Buf("wd")
        load_weight(kb, cx, Wd, wd, wdb, KC, D, "wd")
        gbg = sb(es, nc, "lng", [128, DC], F32)
        gbv = sb(es, nc, "lnb", [128, DC], F32)
        gbb = Buf("lngb")
        kb.emit("sp", lambda h: h.dma_start(out=gbg[:], in_=lng[:, :]), writes=[gbb], dma_buf=gbb)
        kb.emit("sp", lambda h: h.dma_start(out=gbv[:], in_=lnb[:, :]), pwrites=[gbb], dma_buf=gbb)
        A = [sb(es, nc, "A%d" % i, [128, KC, TB], BF16) for i in range(3)]
        Ab = [Buf("A%d" % i) for i in range(3)]
        Z = [sb(es, nc, "Z%d" % i, [128, DC, TB], F32) for i in range(3)]
        Zb = [[Buf("Z%d_%d" % (i, m)) for m in range(DC)] for i in range(3)]
        zb16 = [sb(es, nc, "zb%d" % i, [128, TB], BF16) for i in range(3)]
        zb16b = [Buf("zb%d" % i) for i in range(3)]
        zsq = [sb(es, nc, "zsq%d" % i, [128, TB], BF16) for i in range(3)]
        zsqb = [Buf("zsq%d" % i) for i in range(3)]
        mean = sb(es, nc, "mean", [128, TB], F32)
        meanb = Buf("mean")
        msq = sb(es, nc, "msq", [128, TB], F32)
        msqb = Buf("msq")
        rstd = sb(es, nc, "rstd", [128, TB], F32)
        rstdb = Buf("rstd")
        tt = [sb(es, nc, "tt%d" % i, [128, TB], F32) for i in range(4)]
        ttb = [Buf("tt%d" % i) for i in range(4)]
        if dst_mode == "tm":
            ytm = [sb(es, nc, "ytm%d" % i, [128, D], F32) for i in range(2)]
            ytmb = [Buf("ytm%d" % i) for i in range(2)]
        if dst_mode == "tm":
            ybanks, s1banks, s2banks = (0, 1), (2, 4), (3, 5)
        else:
            ybanks, s1banks, s2banks = (0, 1, 4, 5), (2, 6), (3, 7)
        NY = len(ybanks)
        y_ps = [(cx.ps[i][:, 0:TB], Buf("yps%d" % i)) for i in ybanks]
        s1s = [(cx.ps[i][:, 0:TB], Buf("s1_%d" % i)) for i in s1banks]
        s2s = [(cx.ps[i][:, 0:TB], Buf("s2_%d" % i)) for i in s2banks]
        tp_ps = [(cx.ps[6 + i], Buf("tps%d" % i)) for i in range(2)]
        LAG = 3
        yi = 0
        ti = 0
        eps = LN_EPS / (ALPHA * ALPHA)
        cz = csc / ALPHA
        def load(b):
            Ai, Aib = A[b % 3], Ab[b % 3]
            Zi, Zib = Z[b % 3], Zb[b % 3]
            src = AS[b][:, 0:KC * TB].rearrange("p (c t) -> p c t", c=KC)
            kb.emit("sp", lambda h, src=src, Ai=Ai: h.dma_start(out=Ai[:], in_=src), reads=[cx.dram(AS, b)],
                    writes=[Aib], dma_buf=Aib)
            srcx = XSin[b].rearrange("p (c t) -> p c t", c=DC)
            kb.emit("sp", lambda h, srcx=srcx, Zi=Zi: h.dma_start(out=Zi[:], in_=srcx), reads=[cx.dram(XSin, b)],
                    writes=Zib, dma_buf=Zib[0])

        pend = [None]

        def chunk(b, m):
            nonlocal yi
            Ai, Aib = A[b % 3], Ab[b % 3]
            Zi, Zib = Z[b % 3], Zb[b % 3]
            s1, s2 = s1s[b % 2], s2s[b % 2]
            pend_stats = pend[0]
            if True:
                Y, Yb = y_ps[yi % NY]
                yi += 1
                for k in range(KC):
                    f = lambda h, Y=Y, m=m, k=k, Ai=Ai: h.matmul(Y, wd[:, k, m * 128:(m + 1) * 128], Ai[:, k, :],
                                                                 start=(k == 0), stop=(k == KC - 1))
                    if k == 0:
                        kb.emit("pe", f, reads=[wdb, Aib], writes=[Yb], signal=(KC == 1))
                    else:
                        kb.emit("pe", f, reads=[wdb, Aib], pwrites=[Yb], signal=(k == KC - 1))
                zc = Zi[:, m, :]
                kb.emit("dve", lambda h, zc=zc, Y=Y: h.scalar_tensor_tensor(zc, Y, float(cz), zc, ALU.mult, ALU.add),
                        reads=[Yb, Zib[m]], pwrites=[Zib[m]])
                z16, z16b = zb16[m % 3], zb16b[m % 3]
                zq, zqb = zsq[m % 3], zsqb[m % 3]
                kb.emit("act", lambda h, z16=z16, zc=zc: h.activation(out=z16[:], in_=zc, func=AF.Copy),
                        reads=[Zib[m]], writes=[z16b])
                kb.emit("act", lambda h, zq=zq, zc=zc: h.activation(out=zq[:], in_=zc, func=AF.Square),
                        reads=[Zib[m]], writes=[zqb])
                def stats(z16=z16, z16b=z16b, zq=zq, zqb=zqb, m=m):
                    f1 = lambda h: h.matmul(s1[0], cx.ones[:], z16[:], start=(m == 0), stop=(m == DC - 1))
                    f2_ = lambda h: h.matmul(s2[0], cx.ones[:], zq[:], start=(m == 0), stop=(m == DC - 1))
                    if m == 0:
                        kb.emit("pe", f1, reads=[z16b, cx.onesb], writes=[s1[1]])
                        kb.emit("pe", f2_, reads=[zqb, cx.onesb], writes=[s2[1]])
                    else:
                        kb.emit("pe", f1, reads=[z16b, cx.onesb], pwrites=[s1[1]])
                        kb.emit("pe", f2_, reads=[zqb, cx.onesb], pwrites=[s2[1]])
                if pend_stats is not None:
                    pend_stats()
                if m == DC - 1:
                    stats()
                    pend[0] = None
                else:
                    pend[0] = stats

        def epilogue(b):
            nonlocal ti
            Zi, Zib = Z[b % 3], Zb[b % 3]
            s1, s2 = s1s[b % 2], s2s[b % 2]
            kb.emit("act", lambda h: h.activation(out=mean[:], in_=s1[0], func=AF.Copy, scale=1.0 / D),
                    reads=[s1[1]], writes=[meanb])
            kb.emit("dve", lambda h: h.tensor_tensor(msq[:], mean[:], mean[:], ALU.mult), reads=[meanb], writes=[msqb])
            kb.emit("dve", lambda h: h.scalar_tensor_tensor(rstd[:], s2[0], 1.0 / D, msq[:], ALU.mult, ALU.subtract),
                    reads=[s2[1], msqb], writes=[rstdb])
            kb.emit("dve", lambda h: h.tensor_scalar(rstd[:], rstd[:], float(eps), None, ALU.add), reads=[rstdb], writes=[rstdb])
            kb.emit("act", lambda h: h.activation(out=rstd[:], in_=rstd[:], func=AF.Sqrt), reads=[rstdb], writes=[rstdb])
            kb.emit("dve", lambda h: h.reciprocal(rstd[:], rstd[:]), reads=[rstdb], writes=[rstdb])
            for m in range(DC):
                zc = Zi[:, m, :]
                T1, T1b = tt[m % 4], ttb[m % 4]
                kb.emit("pool", lambda h, T1=T1, zc=zc: h.tensor_tensor(T1[:], zc, mean[:], ALU.subtract),
                        reads=[Zib[m], meanb], writes=[T1b])
                kb.emit("pool", lambda h, T1=T1: h.tensor_tensor(T1[:], T1[:], rstd[:], ALU.mult),
                        reads=[T1b, rstdb], writes=[T1b])
                kb.emit("act", lambda h, T1=T1, zc=zc, m=m: h.activation(out=zc, in_=T1[:], func=AF.Identity,
                                                                         scale=gbg[:, m:m + 1], bias=gbv[:, m:m + 1]),
                        reads=[T1b, gbb], pwrites=[Zib[m]])
            if dst_mode == "fm":
                dst = Xdst[b].rearrange("p (c t) -> p c t", c=DC)
                kb.emit("sp", lambda h, dst=dst, Zi=Zi: h.dma_start(out=dst, in_=Zi[:]), reads=Zib,
                        writes=[cx.dram(Xdst, b)], dma_buf=Zib[0])
            else:
                for i in range(TB // 128):
                    Yt, Ytb = ytm[i % 2], ytmb[i % 2]
                    for q in range(2):
                        P, Pb = tp_ps[ti % 2]
                        ti += 1
                        for cc in range(4):
                            c = q * 4 + cc
                            o = P[:, cc * 128:(cc + 1) * 128]
                            f = lambda h, o=o, c=c, i=i, Zi=Zi: h.transpose(o, Zi[:, c, i * 128:(i + 1) * 128], cx.ident[:])
                            if cc == 0:
                                kb.emit("pe", f, reads=[Zib[c], cx.identb], writes=[Pb], signal=False)
                            else:
                                kb.emit("pe", f, reads=[Zib[c], cx.identb], pwrites=[Pb], signal=(cc == 3))
                        o = Yt[:, q * 512:(q + 1) * 512]
                        if q == 0:
                            kb.emit("act", lambda h, o=o, P=P: h.activation(out=o, in_=P[:], func=AF.Copy), reads=[Pb],
                                    writes=[Ytb])
                        else:
                            kb.emit("dve", lambda h, o=o, P=P: h.tensor_copy(o, P[:]), reads=[Pb], pwrites=[Ytb])
                    r0 = b * TB + i * 128
                    kb.emit("sp", lambda h, Yt=Yt, r0=r0: h.dma_start(out=Xdst[r0:r0 + 128, :], in_=Yt[:]),
                            reads=[Ytb], pwrites=[cx.outbuf], dma_buf=Ytb)
        load(0)
        for b in range(NB):
            if b + 1 < NB:
                load(b + 1)
            for m in range(DC):
                chunk(b, m)
                if m == LAG - 1 and b > 0:
                    epilogue(b - 1)
        epilogue(NB - 1)
        kb.barrier()
        kb.recycle()


class DS:
    def __init__(self, nc, name, NB, W, dt):
        self.ap = nc.dram_tensor(name, [NB * 128, W], dt, kind="Internal").ap()
        self.bufs = [Buf("%s_%d" % (name, b)) for b in range(NB)]

    def __getitem__(self, b):
        return self.ap[b * 128:(b + 1) * 128, :]


def make_ctx(nc, kb, es):
    cx = Ctx()
    cx.dram = lambda ds, b: ds.bufs[b]
    cx.ps = [es.enter_context(nc.psum_tensor("ps%d" % i, [128, 512], F32)) for i in range(8)]
    cx.psb = [Buf("psb%d" % i) for i in range(8)]
    cx.stage = [(sb(es, nc, "stage%d" % i, [128, 1536], F32), Buf("stage%d" % i)) for i in range(2)]
    cx.ident = sb(es, nc, "ident", [128, 128], F32)
    cx.identb = Buf("ident")
    cx.ones = sb(es, nc, "ones", [128, 128], BF16)
    cx.onesb = Buf("ones")
    cx.outbuf = Buf("out")
    ident_d = nc.dram_tensor("c_ident", [128, 128], F32, kind="ExternalInput").ap()
    kb.emit("sp", lambda h: h.dma_start(out=cx.ident[:], in_=ident_d[:, :]), writes=[cx.identb], dma_buf=cx.identb)
    kb.emit("pool", lambda h: h.memset(cx.ones[:], 1.0), writes=[cx.onesb])
    return cx


def finish(kb, cx):
    E = kb.eng["sp"]
    for k, (s, v) in cx.outbuf.lw.items():
        if E.seen.get(k, 0) < v:
            E.h.wait_ge(s, v)
            E.seen[k] = v
    kb.barrier()


def const_inputs():
    return {"c_ident": np.eye(128, dtype=np.float32)}


SGH = 3072
SGC = 24


def phase_c1(kb, cx, XSin, HS, Win, wsT_d, bsbc_d, lng_d, lnb_d, NB):
    nc = kb.nc
    import contextlib
    with contextlib.ExitStack() as es:
        win = sb(es, nc, "win", [128, DC, 2 * SGH], BF16)
        winb = Buf("win")
        load_weight(kb, cx, Win, win, winb, DC, 2 * SGH, "sguin")
        wsT = sb(es, nc, "wsT", [128, 1, 1024], BF16)
        wsTb = Buf("wsT")
        load_weight(kb, cx, wsT_d, wsT, wsTb, 1, 1024, "wsT")
        bsbc = sb(es, nc, "bsbc", [128, 1024], F32)
        lng = sb(es, nc, "slng", [128, SGC], F32)
        lnb = sb(es, nc, "slnb", [128, SGC], F32)
        cb = Buf("sgconst")
        kb.emit("sp", lambda h: h.dma_start(out=bsbc[:], in_=bsbc_d[:, :]), writes=[cb], dma_buf=cb)
        kb.emit("sp", lambda h: h.dma_start(out=lng[:], in_=lng_d[:, :]), pwrites=[cb], dma_buf=cb)
        kb.emit("sp", lambda h: h.dma_start(out=lnb[:], in_=lnb_d[:, :]), pwrites=[cb], dma_buf=cb)
        bias2 = sb(es, nc, "bias2", [128, SGC, 128], F32)
        bias2b = Buf("bias2")
        for g in range(8):
            P, Pb = cx.ps[g % 2][:, 0:128], cx.psb[g % 2]
            kb.emit("pe", lambda h, P=P, g=g: h.matmul(P, cx.ones[:], wsT[:, 0, g * 128:(g + 1) * 128], start=True, stop=True),
                    reads=[wsTb, cx.onesb], writes=[Pb])
            for cc in range(3):
                c = g * 3 + cc
                f = lambda h, P=P, c=c, g=g: h.scalar_tensor_tensor(bias2[:, c, :], P, lnb[:, c:c + 1],
                                                                    bsbc[:, g * 128:(g + 1) * 128], ALU.mult, ALU.add)
                if c == 0:
                    kb.emit("dve", f, reads=[Pb, cb], writes=[bias2b])
                else:
                    kb.emit("dve", f, reads=[Pb, cb], pwrites=[bias2b])
        X = sb(es, nc, "cX", [128, DC, TB], F32)
        Xb_ = Buf("cX")
        xb = [sb(es, nc, "cxb%d" % i, [128, DC, TB], BF16) for i in range(2)]
        xbb = [Buf("cxb%d" % i) for i in range(2)]
        ufm = sb(es, nc, "ufm", [128, SGC, TB], BF16)
        ufmb = [Buf("ufm%d" % c) for c in range(SGC)]
        vtm = [sb(es, nc, "vtm%d" % i, [128, SGH], F32) for i in range(2)]
        vtmb = [Buf("vtm%d" % i) for i in range(2)]
        vn = sb(es, nc, "vn", [128, SGH], BF16)
        vnb = Buf("vn")
        uv = sb(es, nc, "uv", [128, SGC, TB], BF16)
        uvb = Buf("uv")
        st = [sb(es, nc, "sgst%d" % i, [128, 8], F32) for i in range(2)]
        stb = [Buf("sgst%d" % i) for i in range(2)]
        t1 = [sb(es, nc, "sgt%d" % i, [128, 128], F32) for i in range(2)]
        t1b = [Buf("sgt%d" % i) for i in range(2)]
        ui = 0
        vi = 0
        si = 0
        ti = 0
        def load(b):
            XB, XBb = xb[b % 2], xbb[b % 2]
            src = XSin[b].rearrange("p (c t) -> p c t", c=DC)
            kb.emit("sp", lambda h, src=src: h.dma_start(out=X[:], in_=src), reads=[cx.dram(XSin, b)], writes=[Xb_], dma_buf=Xb_)
            kb.emit("pool", lambda h, XB=XB: h.tensor_copy(XB[:], X[:]), reads=[Xb_], writes=[XBb])

        load(0)
        for b in range(NB):
            XB, XBb = xb[b % 2], xbb[b % 2]
            if b + 1 < NB:
                load(b + 1)
            for c in range(SGC):
                P, Pb = cx.ps[ui % 2][:, 0:TB], cx.psb[ui % 2]
                ui += 1
                for k in range(DC):
                    f = lambda h, P=P, k=k, c=c, XB=XB: h.matmul(P, win[:, k, c * 128:(c + 1) * 128], XB[:, k, :],
                                                                 start=(k == 0), stop=(k == DC - 1))
                    if k == 0:
                        kb.emit("pe", f, reads=[winb, XBb], writes=[Pb], signal=False)
                    else:
                        kb.emit("pe", f, reads=[winb, XBb], pwrites=[Pb], signal=(k == DC - 1))
                kb.emit("act", lambda h, c=c, P=P: h.activation(out=ufm[:, c, :], in_=P, func=AF.Gelu), reads=[Pb], writes=[ufmb[c]])
            for i in range(TB // 128):
                tsl = slice(i * 128, (i + 1) * 128)
                V, Vb = vtm[ti % 2], vtmb[ti % 2]
                S, Sb = st[ti % 2], stb[ti % 2]
                ti += 1
                for q in range(6):
                    P, Pb = cx.ps[2 + vi % 3], cx.psb[2 + vi % 3]
                    vi += 1
                    for k in range(DC):
                        f = lambda h, P=P, k=k, q=q, XB=XB, tsl=tsl: h.matmul(
                            P[:], XB[:, k, tsl], win[:, k, SGH + q * 512:SGH + (q + 1) * 512], start=(k == 0), stop=(k == DC - 1))
                        if k == 0:
                            kb.emit("pe", f, reads=[winb, XBb], writes=[Pb], signal=False)
                        else:
                            kb.emit("pe", f, reads=[winb, XBb], pwrites=[Pb], signal=(k == DC - 1))
                    o = V[:, q * 512:(q + 1) * 512]
                    f = lambda h, o=o, P=P: h.activation(out=o, in_=P[:], func=AF.Gelu)
                    if q == 0:
                        kb.emit("act", f, reads=[Pb], writes=[Vb])
                    else:
                        kb.emit("act", f, reads=[Pb], pwrites=[Vb])
                kb.emit("dve", lambda h, S=S, V=V: h.tensor_reduce(S[:, 0:1], V[:], AX.X, ALU.add), reads=[Vb], writes=[Sb])
                kb.emit("dve", lambda h, S=S, V=V: h.scalar_tensor_tensor(vn[:], V[:], 1.0, V[:], ALU.mult, ALU.mult,
                                                                         accum_out=S[:, 1:2]), reads=[Vb], writes=[vnb], pwrites=[Sb])
                kb.emit("dve", lambda h, S=S: h.tensor_scalar(S[:, 2:3], S[:, 0:1], 1.0 / SGH, None, ALU.mult), reads=[Sb], pwrites=[Sb])
                kb.emit("dve", lambda h, S=S: h.tensor_tensor(S[:, 3:4], S[:, 2:3], S[:, 2:3], ALU.mult), reads=[Sb], pwrites=[Sb])
                kb.emit("dve", lambda h, S=S: h.scalar_tensor_tensor(S[:, 4:5], S[:, 1:2], 1.0 / SGH, S[:, 3:4], ALU.mult, ALU.subtract),
                        reads=[Sb], pwrites=[Sb])
                kb.emit("dve", lambda h, S=S: h.tensor_scalar(S[:, 4:5], S[:, 4:5], float(LN_EPS), None, ALU.add), reads=[Sb], pwrites=[Sb])
                kb.emit("act", lambda h, S=S: h.activation(out=S[:, 5:6], in_=S[:, 4:5], func=AF.Sqrt), reads=[Sb], pwrites=[Sb])
                kb.emit("dve", lambda h, S=S: h.reciprocal(S[:, 6:7], S[:, 5:6]), reads=[Sb], pwrites=[Sb])
                kb.emit("dve", lambda h, S=S, V=V: h.tensor_scalar(vn[:], V[:], S[:, 2:3], S[:, 6:7], ALU.subtract, ALU.mult),
                        reads=[Vb, Sb], writes=[vnb])
                for c in range(SGC):
                    g = c // 3
                    P, Pb = cx.ps[5 + si % 2][:, 0:128], cx.psb[5 + si % 2]
                    T1, T1b = t1[si % 2], t1b[si % 2]
                    si += 1
                    kb.emit("pe", lambda h, P=P, c=c, g=g: h.matmul(P, vn[:, c * 128:(c + 1) * 128], wsT[:, 0, g * 128:(g + 1) * 128],
                                                                    start=True, stop=True), reads=[vnb, wsTb], writes=[Pb])
                    kb.emit("dve", lambda h, P=P, c=c, T1=T1: h.scalar_tensor_tensor(T1[:], P, lng[:, c:c + 1], bias2[:, c, :],
                                                                                     ALU.mult, ALU.add),
                            reads=[Pb, bias2b, cb], writes=[T1b])
                    o = uv[:, c, tsl]
                    f = lambda h, o=o, T1=T1, c=c, tsl=tsl: h.tensor_tensor(o, T1[:], ufm[:, c, tsl], ALU.mult)
                    if c == 0 and i == 0:
                        kb.emit("pool", f, reads=[T1b, ufmb[c]], writes=[uvb])
                    else:
                        kb.emit("pool", f, reads=[T1b, ufmb[c]], pwrites=[uvb])
            dst = HS[b][:, 0:SGC * TB].rearrange("p (c t) -> p c t", c=SGC)
            kb.emit("sp", lambda h, dst=dst: h.dma_start(out=dst, in_=uv[:]), reads=[uvb], writes=[cx.dram(HS, b)], dma_buf=uvb)
        kb.barrier()
        kb.recycle()


def _grp(kb, P, Pb, mms, extra_reads):
    n = len(mms)
    for i, (l, r) in enumerate(mms):
        f = lambda h, l=l, r=r, i=i: h.matmul(P, l, r, start=(i == 0), stop=(i == n - 1))
        if i == 0:
            kb.emit("pe", f, reads=extra_reads, writes=[Pb], signal=(n == 1))
        else:
            kb.emit("pe", f, reads=extra_reads, pwrites=[Pb], signal=(i == n - 1))


def phase_mla(kb, cx, XSin, HS, QS, Wm, Wuq, Wukv, qg_d, kvg_d, ropeC_d, ropeS_d, seqs):
    nc = kb.nc
    import contextlib
    SC = 96 ** -0.5
    with contextlib.ExitStack() as es:
        wm = sb(es, nc, "wm", [128, DC, 1216], BF16); wmb = Buf("wm")
        load_weight(kb, cx, Wm, wm, wmb, DC, 1216, "wm")
        wuq = sb(es, nc, "wuq", [128, 6, 1536], BF16); wuqb = Buf("wuq")
        load_weight(kb, cx, Wuq, wuq, wuqb, 6, 1536, "wuq")
        wukv = sb(es, nc, "wukv", [128, 2, 1024], BF16); wukvb = Buf("wukv")
        load_weight(kb, cx, Wukv, wukv, wukvb, 2, 1024, "wukv")
        qg = sb(es, nc, "qg", [128, 6], F32); kvg = sb(es, nc, "kvg", [128, 2], F32)
        onesf = sb(es, nc, "onesf", [128, 64], F32)
        cb = Buf("mconst")
        kb.emit("sp", lambda h: h.dma_start(out=qg[:], in_=qg_d[:, :]), writes=[cb], dma_buf=cb)
        kb.emit("sp", lambda h: h.dma_start(out=kvg[:], in_=kvg_d[:, :]), pwrites=[cb], dma_buf=cb)
        kb.emit("pool", lambda h: h.memset(onesf[:], 1.0), pwrites=[cb])
        SMAX = max(seqs)
        KT = sb(es, nc, "KT", [128, 8, SMAX], BF16); KTb = Buf("KT")
        VA = sb(es, nc, "VA", [128, SMAX // 128, 512], BF16); VAb = Buf("VA")
        X = sb(es, nc, "mX", [128, DC, TB], F32); Xb_ = Buf("mX")
        xb = sb(es, nc, "mxb", [128, DC, TB], BF16); xbb = Buf("mxb")
        cq = sb(es, nc, "cq", [128, 6, TB], BF16); cqb = Buf("cq")
        cqs = sb(es, nc, "cqs", [128, 6, TB], BF16); cqsb = Buf("cqs")
        ckv = sb(es, nc, "ckv", [128, 2, TB], BF16); ckvb = Buf("ckv")
        ckvs = sb(es, nc, "ckvs", [128, 2, TB], BF16); ckvsb = Buf("ckvs")
        rq = sb(es, nc, "rq", [128, TB], F32); rqb = Buf("rq")
        rkv = sb(es, nc, "rkv", [128, TB], F32); rkvb = Buf("rkv")
        rtok = sb(es, nc, "rtok", [128, 2], F32); rtokb = Buf("rtok")
        rc = sb(es, nc, "rc", [96, TB], F32); rs = sb(es, nc, "rs", [96, TB], F32); rcb = Buf("rcs")
        ta = sb(es, nc, "ta", [96, TB], F32); tab = Buf("ta")
        tb2 = sb(es, nc, "tb2", [96, TB], F32); tb2b = Buf("tb2")
        kro = sb(es, nc, "kro", [96, TB], BF16); krob = Buf("kro")
        qblk = sb(es, nc, "qblk", [96, 8, TB], BF16); qblkb = Buf("qblk")
        Q5 = sb(es, nc, "Q5", [128, 8, 2 * TB], BF16); Q5b = Buf("Q5")
        PT = [sb(es, nc, "PT%d" % i, [128, 512], BF16) for i in range(3)]; PTb = [Buf("PT%d" % i) for i in range(3)]
        rcp = [sb(es, nc, "rcp%d" % i, [128, 512], F32) for i in range(2)]; rcpb = [Buf("rcp%d" % i) for i in range(2)]
        oa = [sb(es, nc, "oa%d" % i, [128, 512], BF16) for i in range(2)]; oab = [Buf("oa%d" % i) for i in range(2)]
        zb_ = Buf("mzero")
        kb.emit("pool", lambda h: h.memset(KT[96:128, :, :], 0.0), writes=[zb_])
        kb.emit("pool", lambda h: h.memset(Q5[96:128, :, :], 0.0), pwrites=[zb_])
        ps, psb = cx.ps, cx.psb
        b0 = 0
        pi = 0
        for S in seqs:
            nb = S // TB
            for bl in range(nb):
                b = b0 + bl
                t0 = bl * TB
                def xload(bb):
                    src = XSin[bb].rearrange("p (c t) -> p c t", c=DC)
                    kb.emit("sp", lambda h, src=src: h.dma_start(out=X[:], in_=src), reads=[cx.dram(XSin, bb)], writes=[Xb_], dma_buf=Xb_)

                def xcast():
                    kb.emit("pool", lambda h: h.tensor_copy(xb[:], X[:]), reads=[Xb_], writes=[xbb])

                if bl == 0:
                    xload(b)
                    xcast()
                kb.emit("sp", lambda h, t0=t0: h.dma_start(out=rc[64:96, :], in_=ropeC_d[64:96, t0:t0 + TB]), writes=[rcb], dma_buf=rcb)
                kb.emit("sp", lambda h, t0=t0: h.dma_start(out=rs[64:96, :], in_=ropeS_d[64:96, t0:t0 + TB]), pwrites=[rcb], dma_buf=rcb)
                if bl + 1 < nb:
                    xload(b + 1)
                for j in range(8):
                    P, Pb = ps[j % 2][:, 0:TB], psb[j % 2]
                    _grp(kb, P, Pb, [(wm[:, k, j * 128:(j + 1) * 128], xb[:, k, :]) for k in range(DC)], [wmb, xbb])
                    if j < 6:
                        o1, o2, gsc, w1, w2 = cq[:, j, :], cqs[:, j, :], qg[:, j:j + 1], cqb, cqsb
                    else:
                        o1, o2, gsc, w1, w2 = ckv[:, j - 6, :], ckvs[:, j - 6, :], kvg[:, j - 6:j - 5], ckvb, ckvsb
                    kw1 = dict(writes=[w1]) if j in (0, 6) else dict(pwrites=[w1])
                    kw2 = dict(writes=[w2]) if j in (0, 6) else dict(pwrites=[w2])
                    kb.emit("act", lambda h, o1=o1, P=P, gsc=gsc: h.activation(out=o1, in_=P, func=AF.Copy, scale=gsc), reads=[Pb, cb], **kw1)
                    kb.emit("act", lambda h, o2=o2, P=P: h.activation(out=o2, in_=P, func=AF.Square), reads=[Pb], **kw2)
                Pk, Pkb = ps[2][0:96, 0:TB], psb[2]
                Pks, Pksb = ps[3][0:96, 0:TB], psb[3]
                _grp(kb, Pk, Pkb, [(wm[:, k, 1024:1120], xb[:, k, :]) for k in range(DC)], [wmb, xbb])
                _grp(kb, Pks, Pksb, [(wm[:, k, 1120:1216], xb[:, k, :]) for k in range(DC)], [wmb, xbb])
                if bl + 1 < nb:
                    xcast()
                kb.emit("dve", lambda h, Pk=Pk: h.tensor_tensor(ta[64:96, :], Pk[64:96, :], rc[64:96, :], ALU.mult), reads=[Pkb, rcb], writes=[tab])
                kb.emit("dve", lambda h, Pks=Pks: h.tensor_tensor(tb2[64:96, :], Pks[64:96, :], rs[64:96, :], ALU.mult), reads=[Pksb, rcb], writes=[tb2b])
                kb.emit("dve", lambda h: h.tensor_tensor(kro[64:96, :], ta[64:96, :], tb2[64:96, :], ALU.add), reads=[tab, tb2b], writes=[krob])
                for hd in range(8):
                    kw = dict(writes=[KTb]) if (bl == 0 and hd == 0) else dict(pwrites=[KTb])
                    kb.emit("pool", lambda h, hd=hd, t0=t0: h.tensor_copy(KT[64:96, hd, t0:t0 + TB], kro[64:96, :]), reads=[krob], **kw)
                Pq, Pqb = ps[4][:, 0:TB], psb[4]
                Pv, Pvb = ps[5][:, 0:TB], psb[5]
                _grp(kb, Pq, Pqb, [(cx.ones[:], cqs[:, j, :]) for j in range(6)], [cqsb, cx.onesb])
                _grp(kb, Pv, Pvb, [(cx.ones[:], ckvs[:, j, :]) for j in range(2)], [ckvsb, cx.onesb])
                for (R_, Rb, P, Pb, n) in ((rq, rqb, Pq, Pqb, 768), (rkv, rkvb, Pv, Pvb, 256)):
                    kb.emit("dve", lambda h, R_=R_, P=P, n=n: h.tensor_scalar(R_[:], P, 1.0 / n, float(RMS_EPS), ALU.mult, ALU.add), reads=[Pb], writes=[Rb])
                    kb.emit("act", lambda h, R_=R_: h.activation(out=R_[:], in_=R_[:], func=AF.Sqrt), reads=[Rb], writes=[Rb])
                    kb.emit("dve", lambda h, R_=R_: h.reciprocal(R_[:], R_[:]), reads=[Rb], writes=[Rb])
                for hd in range(8):
                    P, Pb = ps[hd % 2][0:96, 0:TB], psb[hd % 2]
                    P2, P2b = ps[2 + hd % 2][0:96, 0:TB], psb[2 + hd % 2]
                    _grp(kb, P, Pb, [(wuq[:, j, hd * 192:hd * 192 + 96], cq[:, j, :]) for j in range(6)], [wuqb, cqb])
                    _grp(kb, P2, P2b, [(wuq[:, j, hd * 192 + 96:hd * 192 + 192], cq[:, j, :]) for j in range(6)], [wuqb, cqb])
                    kw = dict(writes=[qblkb]) if hd == 0 else dict(pwrites=[qblkb])
                    kb.emit("dve", lambda h, P=P, hd=hd: h.scalar_tensor_tensor(qblk[0:64, hd, :], P[0:64, :], float(SC), rq[0:64, :], ALU.mult, ALU.mult),
                            reads=[Pb, rqb], **kw)
                    kb.emit("dve", lambda h, P=P: h.tensor_tensor(ta[64:96, :], P[64:96, :], rc[64:96, :], ALU.mult), reads=[Pb, rcb], writes=[tab])
                    kb.emit("dve", lambda h, P2=P2: h.tensor_tensor(tb2[64:96, :], P2[64:96, :], rs[64:96, :], ALU.mult), reads=[P2b, rcb], writes=[tb2b])
                    kb.emit("pool", lambda h: h.tensor_tensor(ta[64:96, :], ta[64:96, :], tb2[64:96, :], ALU.add), reads=[tab, tb2b], writes=[tab])
                    kb.emit("dve", lambda h, hd=hd: h.scalar_tensor_tensor(qblk[64:96, hd, :], ta[64:96, :], float(SC), rq[64:96, :], ALU.mult, ALU.mult),
                            reads=[tab, rqb], pwrites=[qblkb])
                kb.emit("sp", lambda h, b=b: h.dma_start(out=QS[b][0:96, :].rearrange("p (c t) -> p c t", c=8), in_=qblk[:]), reads=[qblkb],
                        writes=[cx.dram(QS, b)], dma_buf=qblkb)
                for hd in range(8):
                    P, Pb = ps[4 + hd % 2][0:64, 0:TB], psb[4 + hd % 2]
                    _grp(kb, P, Pb, [(wukv[:, j, hd * 64:(hd + 1) * 64], ckv[:, j, :]) for j in range(2)], [wukvb, ckvb])
                    kb.emit("dve", lambda h, P=P, hd=hd, t0=t0: h.tensor_tensor(KT[0:64, hd, t0:t0 + TB], P, rkv[0:64, :], ALU.mult),
                            reads=[Pb, rkvb], pwrites=[KTb])
                for i in range(TB // 128):
                    tsl = slice(i * 128, (i + 1) * 128)
                    Pr, Prb = ps[6][:, 0:1], psb[6]
                    _grp(kb, Pr, Prb, [(ckvs[:, j, tsl], cx.ones[:, 0:1]) for j in range(2)], [ckvsb, cx.onesb])
                    kb.emit("dve", lambda h, Pr=Pr: h.tensor_scalar(rtok[:, 0:1], Pr, 1.0 / 256, float(RMS_EPS), ALU.mult, ALU.add), reads=[Prb], writes=[rtokb])
                    kb.emit("act", lambda h: h.activation(out=rtok[:, 0:1], in_=rtok[:, 0:1], func=AF.Sqrt), reads=[rtokb], writes=[rtokb])
                    kb.emit("dve", lambda h: h.reciprocal(rtok[:, 1:2], rtok[:, 0:1]), reads=[rtokb], writes=[rtokb])
                    P, Pb = ps[7], psb[7]
                    _grp(kb, P[:], Pb, [(ckv[:, j, tsl], wukv[:, j, 512:1024]) for j in range(2)], [wukvb, ckvb])
                    tix = (t0 + i * 128) // 128
                    kb.emit("dve", lambda h, P=P, tix=tix: h.tensor_scalar(VA[:, tix, :], P[:], rtok[:, 1:2], None, ALU.mult),
                            reads=[Pb, rtokb], **(dict(writes=[VAb]) if tix == 0 else dict(pwrites=[VAb])))
            nkc = S // 128
            for qb in range(S // 512):
                for i in range(2):
                    b = b0 + qb * 2 + i
                    kw = dict(writes=[Q5b]) if i == 0 else dict(pwrites=[Q5b])
                    kb.emit("sp", lambda h, b=b, i=i: h.dma_start(out=Q5[0:96, :, i * TB:(i + 1) * TB], in_=QS[b][0:96, :].rearrange("p (c t) -> p c t", c=8)),
                            reads=[cx.dram(QS, b), zb_], dma_buf=Q5b, **kw)
                for hd in range(8):
                    hp, sub = hd // 2, hd % 2
                    r0 = sub * 64
                    O, Ob = ps[sub], psb[sub]
                    Dn, Dnb = ps[5 + sub], psb[5 + sub]
                    def emit_s(kc, hd=hd):
                        nonlocal pi
                        Sx, Sxb = ps[2 + pi % 3], psb[2 + pi % 3]
                        Pt, Ptb = PT[pi % 3], PTb[pi % 3]
                        pi += 1
                        kb.emit("pe", lambda h, Sx=Sx, hd=hd, kc=kc: h.matmul(Sx[:], KT[:, hd, kc * 128:(kc + 1) * 128], Q5[:, hd, :], start=True, stop=True),
                                reads=[KTb, Q5b, zb_], writes=[Sxb])
                        kb.emit("act", lambda h, Pt=Pt, Sx=Sx: h.activation(out=Pt[:], in_=Sx[:], func=AF.Exp), reads=[Sxb], writes=[Ptb])
                        return Pt, Ptb

                    pend = emit_s(0)
                    for kc in range(nkc):
                        Pt, Ptb = pend
                        if kc + 1 < nkc:
                            pend = emit_s(kc + 1)
                        f = lambda h, O=O, hp=hp, kc=kc, Pt=Pt: h.matmul(O[:], VA[:, kc, hp * 128:(hp + 1) * 128], Pt[:], start=(kc == 0), stop=(kc == nkc - 1))
                        g = lambda h, Dn=Dn, kc=kc, Pt=Pt: h.matmul(Dn[:], cx.ones[:], Pt[:], start=(kc == 0), stop=(kc == nkc - 1))
                        if kc == 0:
                            kb.emit("pe", f, reads=[VAb, Ptb], writes=[Ob], signal=(nkc == 1))
                            kb.emit("pe", g, reads=[cx.onesb, Ptb], writes=[Dnb], signal=(nkc == 1))
                        else:
                            kb.emit("pe", f, reads=[VAb, Ptb], pwrites=[Ob], signal=(kc == nkc - 1))
                            kb.emit("pe", g, reads=[cx.onesb, Ptb], pwrites=[Dnb], signal=(kc == nkc - 1))
                    R_, Rb = rcp[sub], rcpb[sub]
                    OA, OAb = oa[sub], oab[sub]
                    kb.emit("dve", lambda h, R_=R_, Dn=Dn, r0=r0: h.reciprocal(R_[r0:r0 + 64, :], Dn[r0:r0 + 64, :]), reads=[Dnb], writes=[Rb])
                    kb.emit("dve", lambda h, OA=OA, O=O, R_=R_, r0=r0: h.tensor_tensor(OA[r0:r0 + 64, :], O[r0:r0 + 64, :], R_[r0:r0 + 64, :], ALU.mult),
                            reads=[Ob, Rb], writes=[OAb])
                    for i in range(2):
                        b = b0 + qb * 2 + i
                        c0 = hp * TB
                        kb.emit("sp", lambda h, b=b, r0=r0, c0=c0, OA=OA, i=i: h.dma_start(out=HS[b][r0:r0 + 64, c0:c0 + TB], in_=OA[r0:r0 + 64, i * TB:(i + 1) * TB]),
                                reads=[OAb], pwrites=[cx.dram(HS, b)], dma_buf=OAb)
            b0 += nb
        kb.barrier()
        kb.recycle()


def mla_host_layouts(w_in, w_uq, w_ukv):
    z64 = np.zeros((w_in.shape[0], 64), np.float32)
    kr = w_in[:, 1024:1056]
    kr_sw = np.concatenate([kr[:, 16:32], kr[:, 0:16]], axis=1)
    wm = np.concatenate([w_in[:, 0:1024], z64, kr, z64, kr_sw], axis=1)
    cols = []
    for h in range(8):
        nope = w_uq[:, h * 96:h * 96 + 64]
        rope = w_uq[:, h * 96 + 64:h * 96 + 96]
        rope_sw = np.concatenate([rope[:, 16:32], rope[:, 0:16]], axis=1)
        cols += [nope, rope, nope, rope_sw]
    wuq = np.concatenate(cols, axis=1)
    kc = [w_ukv[:, h * 128:h * 128 + 64] for h in range(8)]
    vc = [w_ukv[:, h * 128 + 64:h * 128 + 128] for h in range(8)]
    wukv = np.concatenate(kc + vc, axis=1)
    return np.ascontiguousarray(wm), np.ascontiguousarray(wuq), np.ascontiguousarray(wukv)


def rope_tables_host(smax):
    inv = 1.0 / (10000.0 ** (np.arange(0, 32, 2, dtype=np.float32) / 32))
    ang = np.arange(smax, dtype=np.float32)[None, :] * inv[:, None]
    c = np.cos(ang).astype(np.float32)
    s_ = np.sin(ang).astype(np.float32)
    C = np.zeros((96, smax), np.float32)
    S = np.zeros((96, smax), np.float32)
    C[64:80] = c
    C[80:96] = c
    S[64:80] = -s_
    S[80:96] = s_
    return C, S


def gla_masks_host():
    s = np.arange(128)[:, None]
    t = np.arange(128)[None, :]
    A = (s <= t).astype(np.float32)
    B = (s > t).astype(np.float32)
    C = (s >= t).astype(np.float32)
    Dm = (s < t).astype(np.float32)
    return np.ascontiguousarray(np.concatenate([A, B, C, Dm, np.tile(A, (1, 4)), np.tile(B, (1, 4))], axis=1))


def phase_gla(kb, cx, XSin, HS, Wg, wgf_d, bgf_d, wgb_d, bgb_d, gn_d, cm_d, seqs):
    nc = kb.nc
    import contextlib
    with contextlib.ExitStack() as es:
        wg = sb(es, nc, "wg", [128, DC, 1568], BF16); wgb_ = Buf("wg")
        load_weight(kb, cx, Wg, wg, wgb_, DC, 1568, "wg")
        wgate = [sb(es, nc, "wgate%d" % i, [16, 256], F32) for i in range(2)]
        bgate = [sb(es, nc, "bgate%d" % i, [16, 256], F32) for i in range(2)]
        gn = sb(es, nc, "gn", [128, 2], F32)
        cm = sb(es, nc, "cm", [128, 1536], F32)
        onesf = sb(es, nc, "gonesf", [1, 128], F32)
        cb = Buf("gconst")
        kb.emit("sp", lambda h: h.dma_start(out=cm[:], in_=cm_d[:, :]), writes=[cb], dma_buf=cb)
        for i, (wd_, bd_) in enumerate(((wgf_d, bgf_d), (wgb_d, bgb_d))):
            kb.emit("sp", lambda h, i=i, wd_=wd_: h.dma_start(out=wgate[i][:], in_=wd_[:, :]), pwrites=[cb], dma_buf=cb)
            kb.emit("sp", lambda h, i=i, bd_=bd_: h.dma_start(out=bgate[i][:], in_=bd_[:, :]), pwrites=[cb], dma_buf=cb)
        kb.emit("sp", lambda h: h.dma_start(out=gn[:], in_=gn_d[:, :]), pwrites=[cb], dma_buf=cb)
        kb.emit("pool", lambda h: h.memset(onesf[:], 1.0), pwrites=[cb])
        wgate16 = [sb(es, nc, "wgate16_%d" % i, [16, 256], BF16) for i in range(2)]
        bgate16 = [sb(es, nc, "bgate16_%d" % i, [16, 256], BF16) for i in range(2)]
        cm16 = sb(es, nc, "cm16", [128, 512], BF16)
        cb16 = Buf("gconst16")
        kb.emit("dve", lambda h: h.tensor_copy(cm16[:], cm[:, 0:512]), reads=[cb], writes=[cb16])
        for i in range(2):
            kb.emit("dve", lambda h, i=i: h.tensor_copy(wgate16[i][:], wgate[i][:]), reads=[cb], pwrites=[cb16])
            kb.emit("dve", lambda h, i=i: h.tensor_copy(bgate16[i][:], bgate[i][:]), reads=[cb], pwrites=[cb16])
        SMAX = max(seqs)
        ofwd = sb(es, nc, "ofwd", [128, 4, SMAX], F32); ofwdb = Buf("ofwd")
        X = sb(es, nc, "gX", [128, DC, TB], F32); Xb_ = Buf("gX")
        xb = sb(es, nc, "gxb", [128, DC, TB], BF16); xbb = Buf("gxb")
        zs = sb(es, nc, "zs", [16, 128], BF16); zsb = Buf("zs")
        e1 = sb(es, nc, "e1", [128, 256], F32); e1b = Buf("e1")
        L = sb(es, nc, "L", [128, 256], BF16); Lb = Buf("L")
        eb = sb(es, nc, "eb", [128, 256], F32); ebb = Buf("eb")
        enb = sb(es, nc, "enb", [128, 256], F32); enbb = Buf("enb")
        est = sb(es, nc, "est", [128, 256], F32); estb = Buf("est")
        qin = sb(es, nc, "qin", [128, 256], BF16); qinb = Buf("qin")
        kin = sb(es, nc, "kin", [128, 256], BF16); kinb = Buf("kin")
        kst = sb(es, nc, "kst", [128, 256], BF16); kstb = Buf("kst")
        vbf = sb(es, nc, "vbf", [128, 512], BF16); vbfb = Buf("vbf")
        attm = sb(es, nc, "attm", [128, 512], BF16); attmb = Buf("attm")
        St = [sb(es, nc, "St%d" % i, [128, 256], F32) for i in range(2)]; Stb = [Buf("St%d" % i) for i in range(2)]
        Sbf = [sb(es, nc, "Sbf%d" % i, [128, 256], BF16) for i in range(2)]; Sbfb = [Buf("Sbf%d" % i) for i in range(2)]
        osum = sb(es, nc, "osum", [128, 512], F32); osumb = Buf("osum")
        osq = sb(es, nc, "osq", [128, 512], BF16); osqb = Buf("osq")
        rn = sb(es, nc, "rn", [128, 512], F32); rnb = Buf("rn")
        sr = sb(es, nc, "sr", [128, 512], F32); srb = Buf("sr")
        obk = sb(es, nc, "obk", [128, 4, TB], BF16); obkb = Buf("obk")
        ps, psb = cx.ps, cx.psb
        b0 = 0
        for S in seqs:
            nb = S // TB
            for d in range(2):
                for i in range(2):
                    kb.emit("pool", lambda h, i=i: h.memset(St[i][:], 0.0), writes=[Stb[i]])
                    kb.emit("pool", lambda h, i=i: h.memset(Sbf[i][:], 0.0), writes=[Sbfb[i]])
                blks = range(nb) if d == 0 else range(nb - 1, -1, -1)
                mi, ms, ma = ((0, 128, 512), (256, 384, 1024))[d]
                dcol = 127 if d == 0 else 0
                for bl in blks:
                    b = b0 + bl
                    src = XSin[b].rearrange("p (c t) -> p c t", c=DC)
                    kb.emit("sp", lambda h, src=src: h.dma_start(out=X[:], in_=src), reads=[cx.dram(XSin, b)], writes=[Xb_], dma_buf=Xb_)
                    kb.emit("pool", lambda h: h.tensor_copy(xb[:], X[:]), reads=[Xb_], writes=[xbb])
                    tiles = range(2) if d == 0 else range(1, -1, -1)
                    for ti_, i in enumerate(tiles):
                        tsl = slice(i * 128, (i + 1) * 128)
                        t0 = bl * TB + i * 128
                        rx = [wgb_, xbb]
                        for g4 in range(4):
                            f0 = dict(writes=[psb[0]]) if g4 == 0 else dict(pwrites=[psb[0]])
                            for k in range(DC):
                                kb.emit("pe", lambda h, g4=g4, k=k, tsl=tsl: h.matmul(ps[0][:, g4 * 128:(g4 + 1) * 128], wg[:, k, g4 * 128:(g4 + 1) * 128], xb[:, k, tsl],
                                                                                   start=(k == 0), stop=(k == DC - 1)),
                                        reads=rx, signal=(k == DC - 1), **(f0 if k == 0 else dict(pwrites=[psb[0]])))
                        _grp(kb, ps[1][:, 0:256], psb[1], [(xb[:, k, tsl], wg[:, k, 256:512]) for k in range(DC)], rx)
                        _grp(kb, ps[2][:], psb[2], [(xb[:, k, tsl], wg[:, k, 512:1024]) for k in range(DC)], rx)
                        _grp(kb, ps[3][0:16, 0:128], psb[3], [(wg[:, k, 1536 + 16 * d:1552 + 16 * d], xb[:, k, tsl]) for k in range(DC)], rx)
                        kb.emit("act", lambda h: h.activation(out=zs[:], in_=ps[3][0:16, 0:128], func=AF.Copy), reads=[psb[3]], writes=[zsb])
                        _grp(kb, ps[4][:, 0:256], psb[4], [(zs[:], wgate16[d][:]), (cx.ones[0:1, :], bgate16[d][0:1, :])], [zsb, cb16, cx.onesb])
                        kb.emit("act", lambda h: h.activation(out=e1[:], in_=ps[4][:, 0:256], func=AF.Exp, scale=-1.0), reads=[psb[4]], writes=[e1b])
                        kb.emit("act", lambda h: h.activation(out=L[:], in_=e1[:], func=AF.Ln, bias=1.0), reads=[e1b], writes=[Lb])
                        if "G1" in DBG:
                            continue
                        for pr in range(2):
                            kb.emit("pe", lambda h, pr=pr: h.matmul(ps[5][:, pr * 128:(pr + 1) * 128], L[:, pr * 128:(pr + 1) * 128], cm16[:, mi:mi + 128], start=True, stop=True),
                                    reads=[Lb, cb16], **(dict(writes=[psb[5]]) if pr == 0 else dict(pwrites=[psb[5]])))
                        kb.emit("pe", lambda h: h.matmul(ps[6][:, 0:256], cm16[:, ms:ms + 128], L[:], start=True, stop=True), reads=[Lb, cb16], writes=[psb[6]])
                        kb.emit("act", lambda h: h.activation(out=eb[:], in_=ps[5][:, 0:256], func=AF.Exp, scale=-1.0 / 16), reads=[psb[5]], writes=[ebb])
                        kb.emit("act", lambda h: h.activation(out=enb[:], in_=ps[5][:, 0:256], func=AF.Exp, scale=1.0 / 16), reads=[psb[5]], writes=[enbb])
                        kb.emit("act", lambda h: h.activation(out=est[:], in_=ps[6][:, 0:256], func=AF.Exp, scale=-1.0 / 16), reads=[psb[6]], writes=[estb])
                        kb.emit("dve", lambda h: h.scalar_tensor_tensor(qin[:], ps[0][:, 0:256], 0.125, eb[:], ALU.mult, ALU.mult), reads=[psb[0], ebb], writes=[qinb])
                        kb.emit("dve", lambda h: h.tensor_tensor(kin[:], ps[0][:, 256:512], enb[:], ALU.mult), reads=[psb[0], enbb], writes=[kinb])
                        kb.emit("dve", lambda h: h.tensor_tensor(kst[:], ps[1][:, 0:256], est[:], ALU.mult), reads=[psb[1], estb], writes=[kstb])
                        kb.emit("act", lambda h: h.activation(out=vbf[:], in_=ps[2][:], func=AF.Copy), reads=[psb[2]], writes=[vbfb])
                        if "G2" in DBG:
                            continue
                        attb = (7, 0)
                        ob_ = (3, 1)
                        for hd in range(4):
                            pr, r0, par = hd // 2, (hd % 2) * 64, hd % 2
                            bk = attb[par]
                            kb.emit("pe", lambda h, bk=bk, pr=pr, r0=r0: h.matmul(ps[bk][:, pr * 128:(pr + 1) * 128], kin[r0:r0 + 64, pr * 128:(pr + 1) * 128],
                                                                                  qin[r0:r0 + 64, pr * 128:(pr + 1) * 128], start=True, stop=True),
                                    reads=[kinb, qinb], **(dict(writes=[psb[bk]]) if pr == 0 else dict(pwrites=[psb[bk]])))
                        attm4 = attm[:].rearrange("p (a b t) -> p a b t", a=2, b=2)
                        mk = cm[:, ma:ma + 256].rearrange("p (a t) -> p a t", a=2)
                        for par in range(2):
                            bk = attb[par]
                            kw = dict(writes=[attmb]) if par == 0 else dict(pwrites=[attmb])
                            kb.emit("dve", lambda h, bk=bk, par=par: h.tensor_tensor(attm4[:, :, par, :], ps[bk][:, 0:256].rearrange("p (a t) -> p a t", a=2), mk, ALU.mult),
                                    reads=[psb[bk], cb], **kw)
                        for hd in range(4):
                            pr, r0, par = hd // 2, (hd % 2) * 64, hd % 2
                            bk = ob_[par]
                            O = ps[bk][:, pr * 128:(pr + 1) * 128]
                            kb.emit("pe", lambda h, O=O, hd=hd: h.matmul(O, vbf[:, hd * 128:(hd + 1) * 128], attm[:, hd * 128:(hd + 1) * 128], start=True, stop=False),
                                    reads=[vbfb, attmb], signal=False, **(dict(writes=[psb[bk]]) if pr == 0 else dict(pwrites=[psb[bk]])))
                            kb.emit("pe", lambda h, O=O, hd=hd, pr=pr, r0=r0: h.matmul(O, Sbf[pr][r0:r0 + 64, (hd % 2) * 128:(hd % 2 + 1) * 128],
                                                                                     qin[r0:r0 + 64, pr * 128:(pr + 1) * 128], start=False, stop=True),
                                    reads=[Sbfb[pr], qinb], pwrites=[psb[bk]])
                        if "G3" in DBG:
                            continue
                        for pr in range(2):
                            Dp, Dpb = ps[5 + pr][:, 0:256], psb[5 + pr]
                            kb.emit("pe", lambda h, Dp=Dp, pr=pr: h.matmul(Dp, kst[:, pr * 128:(pr + 1) * 128], vbf[:, pr * 256:(pr + 1) * 256], start=True, stop=True),
                                    reads=[kstb, vbfb], writes=[Dpb])
                            dc = pr * 128 + dcol
                            kb.emit("dve", lambda h, Dp=Dp, pr=pr, dc=dc: h.scalar_tensor_tensor(St[pr][:], St[pr][:], eb[:, dc:dc + 1], Dp, ALU.mult, ALU.add),
                                    reads=[Stb[pr], ebb, Dpb], writes=[Stb[pr]])
                            kb.emit("act", lambda h, pr=pr: h.activation(out=Sbf[pr][:], in_=St[pr][:], func=AF.Copy), reads=[Stb[pr]], writes=[Sbfb[pr]])
                        if "G4" in DBG:
                            continue
                        if d == 0:
                            of4 = ofwd[:].rearrange("p (a b) s -> p a b s", a=2, b=2)
                            for par in range(2):
                                bk = ob_[par]
                                kw = dict(writes=[ofwdb]) if (bl == 0 and i == 0 and par == 0) else dict(pwrites=[ofwdb])
                                kb.emit("act", lambda h, t0=t0, bk=bk, par=par: h.activation(out=of4[:, :, par, t0:t0 + 128], in_=ps[bk][:, 0:256].rearrange("p (a t) -> p a t", a=2),
                                                                                         func=AF.Copy), reads=[psb[bk]], **kw)
                        else:
                            of4 = ofwd[:].rearrange("p (a b) s -> p a b s", a=2, b=2)
                            os4 = osum[:].rearrange("p (a b t) -> p a b t", a=2, b=2)
                            for par in range(2):
                                bk = ob_[par]
                                kw = dict(writes=[osumb]) if par == 0 else dict(pwrites=[osumb])
                                kb.emit("dve", lambda h, t0=t0, bk=bk, par=par: h.tensor_tensor(os4[:, :, par, :], ps[bk][:, 0:256].rearrange("p (a t) -> p a t", a=2),
                                                                                            of4[:, :, par, t0:t0 + 128], ALU.add), reads=[psb[bk], ofwdb], **kw)
                            kb.emit("pool", lambda h: h.tensor_tensor(osq[:], osum[:], osum[:], ALU.mult), reads=[osumb], writes=[osqb])
                            kb.emit("pe", lambda h: h.matmul(ps[7][:], cx.ones[:], osq[:], start=True, stop=True), reads=[osqb, cx.onesb], writes=[psb[7]])
                            kb.emit("dve", lambda h: h.tensor_scalar(rn[:], ps[7][:], 1.0 / 128, float(RMS_EPS), ALU.mult, ALU.add), reads=[psb[7]], writes=[rnb])
                            kb.emit("act", lambda h: h.activation(out=rn[:], in_=rn[:], func=AF.Sqrt), reads=[rnb], writes=[rnb])
                            kb.emit("dve", lambda h: h.reciprocal(rn[:], rn[:]), reads=[rnb], writes=[rnb])
                            kb.emit("dve", lambda h: h.tensor_tensor(osum[:], osum[:], rn[:], ALU.mult), reads=[osumb, rnb], writes=[osumb])
                            for hd in range(4):
                                f0 = dict(writes=[psb[4]]) if hd == 0 else dict(pwrites=[psb[4]])
                                for k in range(DC):
                                    kb.emit("pe", lambda h, hd=hd, k=k, tsl=tsl: h.matmul(ps[4][:, hd * 128:(hd + 1) * 128], wg[:, k, 1024 + hd * 128:1024 + (hd + 1) * 128],
                                                                                       xb[:, k, tsl], start=(k == 0), stop=(k == DC - 1)),
                                            reads=rx, signal=(k == DC - 1), **(f0 if k == 0 else dict(pwrites=[psb[4]])))
                            kb.emit("act", lambda h: h.activation(out=sr[:], in_=ps[4][:], func=AF.Silu), reads=[psb[4]], writes=[srb])
                            kw = dict(writes=[obkb]) if ti_ == 0 else dict(pwrites=[obkb])
                            kb.emit("dve", lambda h, tsl=tsl: h.scalar_tensor_tensor(obk[:, :, tsl], osum[:].rearrange("p (h t) -> p h t", h=4), gn[:, 0:1],
                                                                                    sr[:].rearrange("p (h t) -> p h t", h=4), ALU.mult, ALU.mult),
                                    reads=[osumb, srb, cb], **kw)
                    if d == 1:
                        dst = HS[b][:, 4 * TB:8 * TB].rearrange("p (c t) -> p c t", c=4)
                        kb.emit("sp", lambda h, dst=dst: h.dma_start(out=dst, in_=obk[:]), reads=[obkb], pwrites=[cx.dram(HS, b)], dma_buf=obkb)
            b0 += nb
        kb.barrier()
        kb.recycle()


SEQS = [2048, 2048, 4096, 4096]


def build_full(seqs):
    import contextlib
    T = sum(seqs)
    NB = T // TB
    nc = bass.Bass("TRN2", target_bir_lowering=False)

    def din(name, shape):
        return nc.dram_tensor(name, list(shape), F32, kind="ExternalInput").ap()

    x = din("x", [T, D])
    y = nc.dram_tensor("y", [T, D], F32, kind="ExternalOutput").ap()
    W = {}
    for l in ("l0", "l1"):
        for f in ("ffa", "ffb"):
            W[l + f + "gu"] = din(l + f + "gu", [D, 2 * FH])
            W[l + f + "dn"] = din(l + f + "dn", [FH, D])
        for i in (1, 2, 3):
            W[l + "g%d" % i] = din(l + "g%d" % i, [128, DC])
            W[l + "b%d" % i] = din(l + "b%d" % i, [128, DC])
    wm = din("wm", [D, 1216]); wuq = din("wuq", [768, 1536]); wukv = din("wukv", [256, 1024])
    qg = din("qg", [128, 6]); kvg = din("kvg", [128, 2])
    SMAX = max(seqs)
    rC = din("rC", [96, SMAX]); rS = din("rS", [96, SMAX])
    wg = din("wg", [D, 1568]); wgf = din("wgf", [16, 256]); bgf = din("bgf", [16, 256]); wgb = din("wgb", [16, 256]); bgb = din("bgb", [16, 256])
    gn = din("gn", [128, 2]); cm = din("cm", [128, 1536])
    wout0 = din("wout0", [1024, D])
    swin = din("swin", [D, 6144]); swout = din("swout", [3072, D]); wsT = din("wsT", [128, 1024]); bsbc = din("bsbc", [128, 1024])
    slng = din("slng", [128, 24]); slnb = din("slnb", [128, 24])
    with contextlib.ExitStack() as es:
        kb = KB(nc, es)
        cx = make_ctx(nc, kb, es)
        XA = DS(nc, "XA", NB, DC * TB, F32)
        XB = DS(nc, "XB", NB, DC * TB, F32)
        HS = DS(nc, "HS", NB, 24 * TB, BF16)
        QS = DS(nc, "QS", NB, 8 * TB, BF16)
        phase_f1(kb, cx, es, "tm", x, XA, HS, W["l0ffagu"], NB)
        phase_f2(kb, cx, HS, FHC, W["l0ffadn"], 0.5, W["l0g1"], W["l0b1"], XA, "fm", XB, NB, "a")
        phase_mla(kb, cx, XB, HS, QS, wm, wuq, wukv, qg, kvg, rC, rS, seqs)
        phase_gla(kb, cx, XB, HS, wg, wgf, bgf, wgb, bgb, gn, cm, seqs)
        phase_f2(kb, cx, HS, 8, wout0, 1.0, W["l0g2"], W["l0b2"], XB, "fm", XA, NB, "b")
        phase_f1(kb, cx, es, "fm", XA, None, HS, W["l0ffbgu"], NB)
        phase_f2(kb, cx, HS, FHC, W["l0ffbdn"], 0.5, W["l0g3"], W["l0b3"], XA, "fm", XB, NB, "c")
        phase_f1(kb, cx, es, "fm", XB, None, HS, W["l1ffagu"], NB)
        phase_f2(kb, cx, HS, FHC, W["l1ffadn"], 0.5, W["l1g1"], W["l1b1"], XB, "fm", XA, NB, "d")
        phase_c1(kb, cx, XA, HS, swin, wsT, bsbc, slng, slnb, NB)
        phase_f2(kb, cx, HS, SGC, swout, 1.0, W["l1g2"], W["l1b2"], XA, "fm", XB, NB, "e")
        phase_f1(kb, cx, es, "fm", XB, None, HS, W["l1ffbgu"], NB)
        phase_f2(kb, cx, HS, FHC, W["l1ffbdn"], 0.5, W["l1g3"], W["l1b3"], XB, "tm", y, NB, "f")
        finish(kb, cx)
    return nc


def shared_inputs(p, seqs):
    pc = lambda v, c: np.ascontiguousarray(v.reshape(c, 128).T.astype(np.float32))
    shared = dict(const_inputs())
    for l in ("l0", "l1"):
        for f in ("ffa", "ffb"):
            shared[l + f + "gu"] = p["%s_%s_w_gu" % (l, f)]
            shared[l + f + "dn"] = p["%s_%s_w_down" % (l, f)]
        for i in (1, 2, 3):
            shared[l + "g%d" % i] = pc(p["%s_ln%d_g" % (l, i)], DC)
            shared[l + "b%d" % i] = pc(p["%s_ln%d_b" % (l, i)], DC)
    WM, WUQ, WUKV = mla_host_layouts(p["l0_w_in"], p["l0_mla_w_uq"], p["l0_mla_w_ukv"])
    C, S_ = rope_tables_host(max(seqs))
    shared.update(wm=WM, wuq=WUQ, wukv=WUKV, qg=pc(p["l0_mla_q_norm"], 6), kvg=pc(p["l0_mla_kv_norm"], 2), rC=C, rS=S_,
                  wg=np.ascontiguousarray(p["l0_w_in"][:, 1056:2624]), wgf=p["l0_gla_w_gate_f"], bgf=np.concatenate([p["l0_gla_b_gate_f"].reshape(1, 256), np.zeros((15, 256), np.float32)], 0),
                  wgb=p["l0_gla_w_gate_b"], bgb=np.concatenate([p["l0_gla_b_gate_b"].reshape(1, 256), np.zeros((15, 256), np.float32)], 0), gn=np.concatenate([p["l0_gla_norm"].reshape(128, 1)] * 2, 1), cm=gla_masks_host(),
                  wout0=p["l0_w_out"], swin=p["l1_sgu_w_in"], swout=p["l1_sgu_w_out"],
                  wsT=np.ascontiguousarray(p["l1_sgu_w_s"].transpose(2, 0, 1).reshape(128, 1024)),
                  bsbc=np.ascontiguousarray(np.broadcast_to(p["l1_sgu_b_s"].reshape(1, 1024), (128, 1024))),
                  slng=pc(p["l1_sgu_ln_g"], 24), slnb=pc(p["l1_sgu_ln_b"], 24))
    return {k: np.ascontiguousarray(v, dtype=np.float32) for k, v in shared.items()}


def kernel(**inp):
    p = {k: np.asarray(v) for k, v in inp.items()}
    seqs = SEQS
    nc = build_full(seqs)
    shared = shared_inputs(p, seqs)
    xp, xs = p["x_prompt"], p["x_sample"]
    in_maps = []
    for c in range(N_CORES):
        xc = np.concatenate([xp[2 * c].reshape(-1, D), xp[2 * c + 1].reshape(-1, D), xs[2 * c].reshape(-1, D), xs[2 * c + 1].reshape(-1, D)], axis=0)
        m = dict(shared)
        m["x"] = np.ascontiguousarray(xc)
        in_maps.append(m)
    res = run_bass_kernel_spmd(nc, in_maps, core_ids=list(range(N_CORES)))
    yp = np.empty_like(xp)
    ys = np.empty_like(xs)
    for c in range(N_CORES):
        yc = res.results[c]["y"]
        yp[2 * c] = yc[0:2048]
        yp[2 * c + 1] = yc[2048:4096]
        ys[2 * c] = yc[4096:8192]
        ys[2 * c + 1] = yc[8192:12288]
    return (yp, ys)
```

```python
import numpy as np
import concourse.bass as bass
import concourse.mybir as mybir
from concourse.bass_utils import run_bass_kernel_spmd

F32 = mybir.dt.float32
BF16 = mybir.dt.bfloat16
AF = mybir.ActivationFunctionType
ALU = mybir.AluOpType
AX = mybir.AxisListType

D = 1024
DC = 8
FH = 2816
FHC = 22
TB = 256
ALPHA = 4 ** 0.25
LN_EPS = 1e-5
RMS_EPS = 1e-6
N_CORES = 8
import os
DBG = os.environ.get("KDBG", "")


class Buf:
    _n = 0

    def __init__(self, name=""):
        Buf._n += 1
        self.key = "b%d_%s" % (Buf._n, name)
        self.prev = {}
        self.lw = {}
        self.rd = {}
        self.sem = None
        self.dcnt = 0


def _merge(dst, src):
    for k, (s, v) in src.items():
        if k not in dst or dst[k][1] < v:
            dst[k] = (s, v)


class Eng:
    def __init__(self, key, h, sem):
        self.key = key
        self.h = h
        self.sem = sem
        self.cnt = 0
        self.seen = {}


class KB:
    def __init__(self, nc, es):
        self.nc = nc
        self.es = es
        self.eng = {}
        for key, h in (("pe", nc.tensor), ("act", nc.scalar), ("dve", nc.vector), ("pool", nc.gpsimd), ("sp", nc.sync)):
            sem = es.enter_context(nc.semaphore("sem_" + key))
            self.eng[key] = Eng(key, h, sem)
        self.dma_bufs = []
        self.free_sems = []
        self.nsem = 0
        self.nins = 0

    def dma_sem(self, b):
        if b.sem is None:
            if self.free_sems:
                b.sem, b.dcnt, b.semkey = self.free_sems.pop()
            else:
                self.nsem += 1
                b.semkey = "dsem%d" % self.nsem
                b.sem = self.es.enter_context(self.nc.semaphore(b.semkey))
                b.dcnt = 0
            self.dma_bufs.append(b)
        return b.sem

    def recycle(self):
        for b in self.dma_bufs:
            self.free_sems.append((b.sem, b.dcnt, b.semkey))
            b.sem = None
        self.dma_bufs = []

    def emit(self, eng, fn, reads=(), writes=(), pwrites=(), dma_buf=None, signal=True):
        E = self.eng[eng]
        deps = {}
        for b in reads:
            _merge(deps, b.lw)
        for b in writes:
            p = {}
            _merge(p, b.lw)
            _merge(p, b.rd)
            b.prev = p
            b.lw = {}
            b.rd = {}
        for b in list(writes) + list(pwrites):
            _merge(deps, b.prev)
        for k, (s, v) in deps.items():
            if k == "pe" and eng == "pe":
                continue
            if E.seen.get(k, 0) >= v:
                continue
            E.h.wait_ge(s, v)
            E.seen[k] = v
        ins = fn(E.h)
        self.nins += 1
        if dma_buf is not None:
            sem = self.dma_sem(dma_buf)
            dma_buf.dcnt += 16
            ins.then_inc(sem, 16)
            ev = (dma_buf.semkey, sem, dma_buf.dcnt)
        elif signal:
            E.cnt += 1
            ins.then_inc(E.sem, 1)
            ev = (E.key, E.sem, E.cnt)
        else:
            ev = (E.key, E.sem, E.cnt + 1)
        d = {ev[0]: (ev[1], ev[2])}
        for b in reads:
            _merge(b.rd, d)
        for b in list(writes) + list(pwrites):
            _merge(b.lw, d)
        return ins

    def barrier(self):
        for E in self.eng.values():
            for E2 in self.eng.values():
                if E2 is E or E2.cnt == 0:
                    continue
                if E.seen.get(E2.key, 0) >= E2.cnt:
                    continue
                E.h.wait_ge(E2.sem, E2.cnt)
                E.seen[E2.key] = E2.cnt
            for b in self.dma_bufs:
                if b.dcnt == 0 or E.seen.get(b.semkey, 0) >= b.dcnt:
                    continue
                E.h.wait_ge(b.sem, b.dcnt)
                E.seen[b.semkey] = b.dcnt


class Ctx:
    pass


_sbn = [0]


def sb(es, nc, name, shape, dt):
    _sbn[0] += 1
    return es.enter_context(nc.sbuf_tensor("s%d_%s" % (_sbn[0], name), shape, dt))


def load_weight(kb, cx, W3, dst, dstbuf, KC, N, tag, scale=None):
    nc = kb.nc
    CH = 1536
    i = 0
    first = True
    for k in range(KC):
        for c0 in range(0, N, CH):
            w = min(CH, N - c0)
            st, stb = cx.stage[i % len(cx.stage)]
            kb.emit("sp", lambda h, st=st, k=k, c0=c0, w=w: h.dma_start(out=st[:, 0:w], in_=W3[k * 128:(k + 1) * 128, c0:c0 + w]),
                    writes=[stb], dma_buf=stb)
            e = ("act", "dve", "pool")[i % 3]
            o = dst[:, k, c0:c0 + w]
            if e == "act":
                f = lambda h, st=st, o=o, w=w: h.activation(out=o, in_=st[:, 0:w], func=AF.Copy)
            else:
                f = lambda h, st=st, o=o, w=w: h.tensor_copy(o, st[:, 0:w])
            if first:
                kb.emit(e, f, reads=[stb], writes=[dstbuf])
                first = False
            else:
                kb.emit(e, f, reads=[stb], pwrites=[dstbuf])
            i += 1


def phase_f1(kb, cx, es0, src_mode, Xsrc, XSout, HS, Wgu, NB):
    nc = kb.nc
    import contextlib
    with contextlib.ExitStack() as es:
        wgu = sb(es, nc, "wgu", [128, DC, 2 * FH], BF16)
        wgub = Buf("wgu")
        if "w" not in DBG:
            load_weight(kb, cx, Wgu, wgu, wgub, DC, 2 * FH, "gu")
        NX = 3
        xT = [sb(es, nc, "xT%d" % i, [128, DC, TB], F32) for i in range(NX)]
        xTb = [Buf("xT%d" % i) for i in range(NX)]
        xb = [sb(es, nc, "xb%d" % i, [128, DC, TB], BF16) for i in range(NX)]
        xbb = [Buf("xb%d" % i) for i in range(NX)]
        hh = [sb(es, nc, "hh%d" % i, [128, FHC, TB], BF16) for i in range(2)]
        hhb = [Buf("hh%d" % i) for i in range(2)]
        sg = [sb(es, nc, "sg%d" % i, [128, TB], F32) for i in range(3)]
        sgb = [Buf("sg%d" % i) for i in range(3)]
        if src_mode == "tm":
            xtm = [sb(es, nc, "xtm%d" % i, [128, D], F32) for i in range(4)]
            xtmb = [Buf("xtm%d" % i) for i in range(4)]
        gu_ps = []
        for i in range(2):
            gu_ps.append((cx.ps[i][:, 0:TB], Buf("gps%d" % i), cx.ps[2 + i][:, 0:TB], Buf("ups%d" % i)))
        tp_ps = [(cx.ps[4 + i], Buf("tps%d" % i)) for i in range(4)]
        gi = 0
        si = 0
        ti = 0
        def tm_load(b):
            for i in range(TB // 128):
                t, tb_ = xtm[(2 * b + i) % 4], xtmb[(2 * b + i) % 4]
                r0 = b * TB + i * 128
                kb.emit("sp", lambda h, t=t, r0=r0: h.dma_start(out=t[:], in_=Xsrc[r0:r0 + 128, :]),
                        writes=[tb_], dma_buf=tb_)

        def tm_transpose(b):
            nonlocal ti
            X, Xb_ = xT[b % NX], xTb[b % NX]
            XB, XBb = xb[b % NX], xbb[b % NX]
            tl = [(xtm[(2 * b + i) % 4], xtmb[(2 * b + i) % 4]) for i in range(TB // 128)]
            first = True
            for c2 in range(DC // 2):
                P, Pb = tp_ps[ti % 4]
                ti += 1
                n = 0
                for cc in range(2):
                    c = c2 * 2 + cc
                    for i, (t, tb_) in enumerate(tl):
                        o = P[:, cc * TB + i * 128: cc * TB + (i + 1) * 128]
                        last = (n == 3)
                        f = lambda h, o=o, t=t, c=c: h.transpose(o, t[:, c * 128:(c + 1) * 128], cx.ident[:])
                        if n == 0:
                            kb.emit("pe", f, reads=[tb_, cx.identb], writes=[Pb], signal=last)
                        else:
                            kb.emit("pe", f, reads=[tb_, cx.identb], pwrites=[Pb], signal=last)
                        n += 1
                o32 = X[:, c2 * 2:c2 * 2 + 2, :]
                o16 = XB[:, c2 * 2:c2 * 2 + 2, :]
                pin = P[:, 0:2 * TB].rearrange("p (c t) -> p c t", c=2)
                kw = dict(writes=[Xb_]) if first else dict(pwrites=[Xb_])
                kb.emit("act", lambda h, o32=o32, pin=pin: h.activation(out=o32, in_=pin, func=AF.Copy),
                        reads=[Pb], **kw)
                kw = dict(writes=[XBb]) if first else dict(pwrites=[XBb])
                kb.emit("dve", lambda h, o16=o16, o32=o32: h.tensor_copy(o16, o32), reads=[Xb_], **kw)
                first = False
            dst = XSout[b].rearrange("p (c t) -> p c t", c=DC)
            kb.emit("sp", lambda h, dst=dst, X=X: h.dma_start(out=dst, in_=X[:]), reads=[Xb_],
                    writes=[cx.dram(XSout, b)], dma_buf=Xb_)

        def fm_load(b):
            X, Xb_ = xT[b % NX], xTb[b % NX]
            XB, XBb = xb[b % NX], xbb[b % NX]
            src = Xsrc[b].rearrange("p (c t) -> p c t", c=DC)
            kb.emit("sp", lambda h, src=src, X=X: h.dma_start(out=X[:], in_=src), reads=[cx.dram(Xsrc, b)],
                    writes=[Xb_], dma_buf=Xb_)
            kb.emit("pool", lambda h, X=X, XB=XB: h.tensor_copy(XB[:], X[:]), reads=[Xb_], writes=[XBb])

        if src_mode == "tm":
            tm_load(0)
            tm_transpose(0)
            if NB > 1:
                tm_load(1)
        else:
            fm_load(0)
        for b in range(NB):
            X, Xb_ = xT[b % NX], xTb[b % NX]
            XB, XBb = xb[b % NX], xbb[b % NX]
            if src_mode == "fm" and b + 1 < NB:
                fm_load(b + 1)
            H, Hb = hh[b % 2], hhb[b % 2]
            for j in range(FHC if "c" not in DBG else 0):
                if src_mode == "tm" and j == FHC // 2 and b + 1 < NB:
                    tm_transpose(b + 1)
                    if b + 2 < NB:
                        tm_load(b + 2)
                G, Gb, U, Ub = gu_ps[gi % 2]
                gi += 1
                for k in range(DC):
                    f = lambda h, G=G, j=j, k=k, XB=XB: h.matmul(G, wgu[:, k, j * 128:(j + 1) * 128], XB[:, k, :],
                                                                 start=(k == 0), stop=(k == DC - 1))
                    if k == 0:
                        kb.emit("pe", f, reads=[wgub, XBb], writes=[Gb], signal=False)
                    else:
                        kb.emit("pe", f, reads=[wgub, XBb], pwrites=[Gb], signal=(k == DC - 1))
                for k in range(DC):
                    f = lambda h, U=U, j=j, k=k, XB=XB: h.matmul(U, wgu[:, k, FH + j * 128:FH + (j + 1) * 128],
                                                                 XB[:, k, :], start=(k == 0), stop=(k == DC - 1))
                    if k == 0:
                        kb.emit("pe", f, reads=[wgub, XBb], writes=[Ub], signal=False)
                    else:
                        kb.emit("pe", f, reads=[wgub, XBb], pwrites=[Ub], signal=(k == DC - 1))
                S, Sb = sg[si % 3], sgb[si % 3]
                si += 1
                kb.emit("act", lambda h, S=S, G=G: h.activation(out=S[:], in_=G, func=AF.Silu), reads=[Gb], writes=[Sb])
                o = H[:, j, :]
                f = lambda h, o=o, S=S, U=U: h.tensor_tensor(o, S[:], U, ALU.mult)
                if j == 0:
                    kb.emit("dve", f, reads=[Sb, Ub], writes=[Hb])
                else:
                    kb.emit("dve", f, reads=[Sb, Ub], pwrites=[Hb])
            if src_mode == "tm" and "c" in DBG and b + 1 < NB:
                tm_transpose(b + 1)
                if b + 2 < NB:
                    tm_load(b + 2)
            dst = HS[b][:, 0:FHC * TB].rearrange("p (c t) -> p c t", c=FHC)
            if "h" not in DBG:
                kb.emit("sp", lambda h, dst=dst, H=H: h.dma_start(out=dst, in_=H[:]), reads=[Hb],
                        writes=[cx.dram(HS, b)], dma_buf=Hb)
        kb.barrier()
        kb.recycle()


def phase_f2(kb, cx, AS, KC, Wd, csc, lng, lnb, XSin, dst_mode, Xdst, NB, tagp):
    nc = kb.nc
    import contextlib
    with contextlib.ExitStack() as es:
        wd = sb(es, nc, "wd", [128, KC, D], BF16)
        wdb = Buf("wd")
        load_weight(kb, cx, Wd, wd, wdb, KC, D, "wd")
        gbg = sb(es, nc, "lng", [128, DC], F32)
        gbv = sb(es, nc, "lnb", [128, DC], F32)
        gbb = Buf("lngb")
        kb.emit("sp", lambda h: h.dma_start(out=gbg[:], in_=lng[:, :]), writes=[gbb], dma_buf=gbb)
        kb.emit("sp", lambda h: h.dma_start(out=gbv[:], in_=lnb[:, :]), pwrites=[gbb], dma_buf=gbb)
        A = [sb(es, nc, "A%d" % i, [128, KC, TB], BF16) for i in range(3)]
        Ab = [Buf("A%d" % i) for i in range(3)]
        Z = [sb(es, nc, "Z%d" % i, [128, DC, TB], F32) for i in range(3)]
        Zb = [[Buf("Z%d_%d" % (i, m)) for m in range(DC)] for i in range(3)]
        zb16 = [sb(es, nc, "zb%d" % i, [128, TB], BF16) for i in range(3)]
        zb16b = [Buf("zb%d" % i) for i in range(3)]
        zsq = [sb(es, nc, "zsq%d" % i, [128, TB], BF16) for i in range(3)]
        zsqb = [Buf("zsq%d" % i) for i in range(3)]
        mean = sb(es, nc, "mean", [128, TB], F32)
        meanb = Buf("mean")
        msq = sb(es, nc, "msq", [128, TB], F32)
        msqb = Buf("msq")
        rstd = sb(es, nc, "rstd", [128, TB], F32)
        rstdb = Buf("rstd")
        tt = [sb(es, nc, "tt%d" % i, [128, TB], F32) for i in range(4)]
        ttb = [Buf("tt%d" % i) for i in range(4)]
        if dst_mode == "tm":
            ytm = [sb(es, nc, "ytm%d" % i, [128, D], F32) for i in range(2)]
            ytmb = [Buf("ytm%d" % i) for i in range(2)]
        if dst_mode == "tm":
            ybanks, s1banks, s2banks = (0, 1), (2, 4), (3, 5)
        else:
            ybanks, s1banks, s2banks = (0, 1, 4, 5), (2, 6), (3, 7)
        NY = len(ybanks)
        y_ps = [(cx.ps[i][:, 0:TB], Buf("yps%d" % i)) for i in ybanks]
        s1s = [(cx.ps[i][:, 0:TB], Buf("s1_%d" % i)) for i in s1banks]
        s2s = [(cx.ps[i][:, 0:TB], Buf("s2_%d" % i)) for i in s2banks]
        tp_ps = [(cx.ps[6 + i], Buf("tps%d" % i)) for i in range(2)]
        LAG = 3
        yi = 0
        ti = 0
        eps = LN_EPS / (ALPHA * ALPHA)
        cz = csc / ALPHA
        def load(b):
            Ai, Aib = A[b % 3], Ab[b % 3]
            Zi, Zib = Z[b % 3], Zb[b % 3]
            src = AS[b][:, 0:KC * TB].rearrange("p (c t) -> p c t", c=KC)
            kb.emit("sp", lambda h, src=src, Ai=Ai: h.dma_start(out=Ai[:], in_=src), reads=[cx.dram(AS, b)],
                    writes=[Aib], dma_buf=Aib)
            srcx = XSin[b].rearrange("p (c t) -> p c t", c=DC)
            kb.emit("sp", lambda h, srcx=srcx, Zi=Zi: h.dma_start(out=Zi[:], in_=srcx), reads=[cx.dram(XSin, b)],
                    writes=Zib, dma_buf=Zib[0])

        pend = [None]

        def chunk(b, m):
            nonlocal yi
            Ai, Aib = A[b % 3], Ab[b % 3]
            Zi, Zib = Z[b % 3], Zb[b % 3]
            s1, s2 = s1s[b % 2], s2s[b % 2]
            pend_stats = pend[0]
            if True:
                Y, Yb = y_ps[yi % NY]
                yi += 1
                for k in range(KC):
                    f = lambda h, Y=Y, m=m, k=k, Ai=Ai: h.matmul(Y, wd[:, k, m * 128:(m + 1) * 128], Ai[:, k, :],
                                                                 start=(k == 0), stop=(k == KC - 1))
                    if k == 0:
                        kb.emit("pe", f, reads=[wdb, Aib], writes=[Yb], signal=(KC == 1))
                    else:
                        kb.emit("pe", f, reads=[wdb, Aib], pwrites=[Yb], signal=(k == KC - 1))
                zc = Zi[:, m, :]
                kb.emit("dve", lambda h, zc=zc, Y=Y: h.scalar_tensor_tensor(zc, Y, float(cz), zc, ALU.mult, ALU.add),
                        reads=[Yb, Zib[m]], pwrites=[Zib[m]])
                z16, z16b = zb16[m % 3], zb16b[m % 3]
                zq, zqb = zsq[m % 3], zsqb[m % 3]
                kb.emit("act", lambda h, z16=z16, zc=zc: h.activation(out=z16[:], in_=zc, func=AF.Copy),
                        reads=[Zib[m]], writes=[z16b])
                kb.emit("act", lambda h, zq=zq, zc=zc: h.activation(out=zq[:], in_=zc, func=AF.Square),
                        reads=[Zib[m]], writes=[zqb])
                def stats(z16=z16, z16b=z16b, zq=zq, zqb=zqb, m=m):
                    f1 = lambda h: h.matmul(s1[0], cx.ones[:], z16[:], start=(m == 0), stop=(m == DC - 1))
                    f2_ = lambda h: h.matmul(s2[0], cx.ones[:], zq[:], start=(m == 0), stop=(m == DC - 1))
                    if m == 0:
                        kb.emit("pe", f1, reads=[z16b, cx.onesb], writes=[s1[1]])
                        kb.emit("pe", f2_, reads=[zqb, cx.onesb], writes=[s2[1]])
                    else:
                        kb.emit("pe", f1, reads=[z16b, cx.onesb], pwrites=[s1[1]])
                        kb.emit("pe", f2_, reads=[zqb, cx.onesb], pwrites=[s2[1]])
                if pend_stats is not None:
                    pend_stats()
                if m == DC - 1:
                    stats()
                    pend[0] = None
                else:
                    pend[0] = stats

        def epilogue(b):
            nonlocal ti
            Zi, Zib = Z[b % 3], Zb[b % 3]
            s1, s2 = s1s[b % 2], s2s[b % 2]
            kb.emit("act", lambda h: h.activation(out=mean[:], in_=s1[0], func=AF.Copy, scale=1.0 / D),
                    reads=[s1[1]], writes=[meanb])
            kb.emit("dve", lambda h: h.tensor_tensor(msq[:], mean[:], mean[:], ALU.mult), reads=[meanb], writes=[msqb])
            kb.emit("dve", lambda h: h.scalar_tensor_tensor(rstd[:], s2[0], 1.0 / D, msq[:], ALU.mult, ALU.subtract),
                    reads=[s2[1], msqb], writes=[rstdb])
            kb.emit("dve", lambda h: h.tensor_scalar(rstd[:], rstd[:], float(eps), None, ALU.add), reads=[rstdb], writes=[rstdb])
            kb.emit("act", lambda h: h.activation(out=rstd[:], in_=rstd[:], func=AF.Sqrt), reads=[rstdb], writes=[rstdb])
            kb.emit("dve", lambda h: h.reciprocal(rstd[:], rstd[:]), reads=[rstdb], writes=[rstdb])
            for m in range(DC):
                zc = Zi[:, m, :]
                T1, T1b = tt[m % 4], ttb[m % 4]
                e = "dve" if m % 2 == 0 else "pool"
                kb.emit(e, lambda h, T1=T1, zc=zc: h.tensor_tensor(T1[:], zc, mean[:], ALU.subtract),
                        reads=[Zib[m], meanb], writes=[T1b])
                kb.emit(e, lambda h, T1=T1: h.tensor_tensor(T1[:], T1[:], rstd[:], ALU.mult),
                        reads=[T1b, rstdb], writes=[T1b])
                kb.emit(e, lambda h, T1=T1, zc=zc, m=m: h.tensor_scalar(zc, T1[:], gbg[:, m:m + 1], gbv[:, m:m + 1], ALU.mult, ALU.add),
                        reads=[T1b, gbb], pwrites=[Zib[m]])
            if dst_mode == "fm":
                dst = Xdst[b].rearrange("p (c t) -> p c t", c=DC)
                kb.emit("sp", lambda h, dst=dst, Zi=Zi: h.dma_start(out=dst, in_=Zi[:]), reads=Zib,
                        writes=[cx.dram(Xdst, b)], dma_buf=Zib[0])
            else:
                for i in range(TB // 128):
                    Yt, Ytb = ytm[i % 2], ytmb[i % 2]
                    for q in range(2):
                        P, Pb = tp_ps[ti % 2]
                        ti += 1
                        for cc in range(4):
                            c = q * 4 + cc
                            o = P[:, cc * 128:(cc + 1) * 128]
                            f = lambda h, o=o, c=c, i=i, Zi=Zi: h.transpose(o, Zi[:, c, i * 128:(i + 1) * 128], cx.ident[:])
                            if cc == 0:
                                kb.emit("pe", f, reads=[Zib[c], cx.identb], writes=[Pb], signal=False)
                            else:
                                kb.emit("pe", f, reads=[Zib[c], cx.identb], pwrites=[Pb], signal=(cc == 3))
                        o = Yt[:, q * 512:(q + 1) * 512]
                        if q == 0:
                            kb.emit("act", lambda h, o=o, P=P: h.activation(out=o, in_=P[:], func=AF.Copy), reads=[Pb],
                                    writes=[Ytb])
                        else:
                            kb.emit("dve", lambda h, o=o, P=P: h.tensor_copy(o, P[:]), reads=[Pb], pwrites=[Ytb])
                    r0 = b * TB + i * 128
                    kb.emit("sp", lambda h, Yt=Yt, r0=r0: h.dma_start(out=Xdst[r0:r0 + 128, :], in_=Yt[:]),
                            reads=[Ytb], pwrites=[cx.outbuf], dma_buf=Ytb)
        load(0)
        for b in range(NB):
            if b + 1 < NB:
                load(b + 1)
            for m in range(DC):
                chunk(b, m)
                if m == LAG - 1 and b > 0:
                    epilogue(b - 1)
        epilogue(NB - 1)
        kb.barrier()
        kb.recycle()


class DS:
    def __init__(self, nc, name, NB, W, dt):
        self.ap = nc.dram_tensor(name, [NB * 128, W], dt, kind="Internal").ap()
        self.bufs = [Buf("%s_%d" % (name, b)) for b in range(NB)]

    def __getitem__(self, b):
        return self.ap[b * 128:(b + 1) * 128, :]


def make_ctx(nc, kb, es):
    cx = Ctx()
    cx.dram = lambda ds, b: ds.bufs[b]
    cx.ps = [es.enter_context(nc.psum_tensor("ps%d" % i, [128, 512], F32)) for i in range(8)]
    cx.psb = [Buf("psb%d" % i) for i in range(8)]
    cx.stage = [(sb(es, nc, "stage%d" % i, [128, 1536], F32), Buf("stage%d" % i)) for i in range(2)]
    cx.ident = sb(es, nc, "ident", [128, 128], F32)
    cx.identb = Buf("ident")
    cx.ones = sb(es, nc, "ones", [128, 128], BF16)
    cx.onesb = Buf("ones")
    cx.outbuf = Buf("out")
    ident_d = nc.dram_tensor("c_ident", [128, 128], F32, kind="ExternalInput").ap()
    kb.emit("sp", lambda h: h.dma_start(out=cx.ident[:], in_=ident_d[:, :]), writes=[cx.identb], dma_buf=cx.identb)
    kb.emit("pool", lambda h: h.memset(cx.ones[:], 1.0), writes=[cx.onesb])
    return cx


def finish(kb, cx):
    E = kb.eng["sp"]
    for k, (s, v) in cx.outbuf.lw.items():
        if E.seen.get(k, 0) < v:
            E.h.wait_ge(s, v)
            E.seen[k] = v
    kb.barrier()


def const_inputs():
    return {"c_ident": np.eye(128, dtype=np.float32)}


SGH = 3072
SGC = 24


def phase_c1(kb, cx, XSin, HS, Win, wsT_d, bsbc_d, lng_d, lnb_d, NB):
    nc = kb.nc
    import contextlib
    with contextlib.ExitStack() as es:
        win = sb(es, nc, "win", [128, DC, 2 * SGH], BF16)
        winb = Buf("win")
        load_weight(kb, cx, Win, win, winb, DC, 2 * SGH, "sguin")
        wsT = sb(es, nc, "wsT", [128, 1, 1024], BF16)
        wsTb = Buf("wsT")
        load_weight(kb, cx, wsT_d, wsT, wsTb, 1, 1024, "wsT")
        bsbc = sb(es, nc, "bsbc", [128, 1024], F32)
        lng = sb(es, nc, "slng", [128, SGC], F32)
        lnb = sb(es, nc, "slnb", [128, SGC], F32)
        cb = Buf("sgconst")
        kb.emit("sp", lambda h: h.dma_start(out=bsbc[:], in_=bsbc_d[:, :]), writes=[cb], dma_buf=cb)
        kb.emit("sp", lambda h: h.dma_start(out=lng[:], in_=lng_d[:, :]), pwrites=[cb], dma_buf=cb)
        kb.emit("sp", lambda h: h.dma_start(out=lnb[:], in_=lnb_d[:, :]), pwrites=[cb], dma_buf=cb)
        bias2 = sb(es, nc, "bias2", [128, SGC, 128], F32)
        bias2b = Buf("bias2")
        for g in range(8):
            P, Pb = cx.ps[g % 2][:, 0:128], cx.psb[g % 2]
            kb.emit("pe", lambda h, P=P, g=g: h.matmul(P, cx.ones[:], wsT[:, 0, g * 128:(g + 1) * 128], start=True, stop=True),
                    reads=[wsTb, cx.onesb], writes=[Pb])
            for cc in range(3):
                c = g * 3 + cc
                f = lambda h, P=P, c=c, g=g: h.scalar_tensor_tensor(bias2[:, c, :], P, lnb[:, c:c + 1],
                                                                    bsbc[:, g * 128:(g + 1) * 128], ALU.mult, ALU.add)
                if c == 0:
                    kb.emit("dve", f, reads=[Pb, cb], writes=[bias2b])
                else:
                    kb.emit("dve", f, reads=[Pb, cb], pwrites=[bias2b])
        X = sb(es, nc, "cX", [128, DC, TB], F32)
        Xb_ = Buf("cX")
        xb = [sb(es, nc, "cxb%d" % i, [128, DC, TB], BF16) for i in range(2)]
        xbb = [Buf("cxb%d" % i) for i in range(2)]
        ufm = sb(es, nc, "ufm", [128, SGC, TB], BF16)
        ufmb = [Buf("ufm%d" % c) for c in range(SGC)]
        vtm = [sb(es, nc, "vtm%d" % i, [128, SGH], F32) for i in range(2)]
        vtmb = [Buf("vtm%d" % i) for i in range(2)]
        vn = sb(es, nc, "vn", [128, SGH], BF16)
        vnb = Buf("vn")
        uv = sb(es, nc, "uv", [128, SGC, TB], BF16)
        uvb = Buf("uv")
        st = [sb(es, nc, "sgst%d" % i, [128, 8], F32) for i in range(2)]
        stb = [Buf("sgst%d" % i) for i in range(2)]
        t1 = [sb(es, nc, "sgt%d" % i, [128, 128], F32) for i in range(2)]
        t1b = [Buf("sgt%d" % i) for i in range(2)]
        ui = 0
        vi = 0
        si = 0
        ti = 0
        def load(b):
            XB, XBb = xb[b % 2], xbb[b % 2]
            src = XSin[b].rearrange("p (c t) -> p c t", c=DC)
            kb.emit("sp", lambda h, src=src: h.dma_start(out=X[:], in_=src), reads=[cx.dram(XSin, b)], writes=[Xb_], dma_buf=Xb_)
            kb.emit("pool", lambda h, XB=XB: h.tensor_copy(XB[:], X[:]), reads=[Xb_], writes=[XBb])

        load(0)
        for b in range(NB):
            XB, XBb = xb[b % 2], xbb[b % 2]
            if b + 1 < NB:
                load(b + 1)
            for c in range(SGC):
                P, Pb = cx.ps[ui % 2][:, 0:TB], cx.psb[ui % 2]
                ui += 1
                for k in range(DC):
                    f = lambda h, P=P, k=k, c=c, XB=XB: h.matmul(P, win[:, k, c * 128:(c + 1) * 128], XB[:, k, :],
                                                                 start=(k == 0), stop=(k == DC - 1))
                    if k == 0:
                        kb.emit("pe", f, reads=[winb, XBb], writes=[Pb], signal=False)
                    else:
                        kb.emit("pe", f, reads=[winb, XBb], pwrites=[Pb], signal=(k == DC - 1))
                kb.emit("act", lambda h, c=c, P=P: h.activation(out=ufm[:, c, :], in_=P, func=AF.Gelu), reads=[Pb], writes=[ufmb[c]])
            for i in range(TB // 128):
                tsl = slice(i * 128, (i + 1) * 128)
                V, Vb = vtm[ti % 2], vtmb[ti % 2]
                S, Sb = st[ti % 2], stb[ti % 2]
                ti += 1
                for q in range(6):
                    P, Pb = cx.ps[2 + vi % 3], cx.psb[2 + vi % 3]
                    vi += 1
                    for k in range(DC):
                        f = lambda h, P=P, k=k, q=q, XB=XB, tsl=tsl: h.matmul(
                            P[:], XB[:, k, tsl], win[:, k, SGH + q * 512:SGH + (q + 1) * 512], start=(k == 0), stop=(k == DC - 1))
                        if k == 0:
                            kb.emit("pe", f, reads=[winb, XBb], writes=[Pb], signal=False)
                        else:
                            kb.emit("pe", f, reads=[winb, XBb], pwrites=[Pb], signal=(k == DC - 1))
                    o = V[:, q * 512:(q + 1) * 512]
                    f = lambda h, o=o, P=P: h.activation(out=o, in_=P[:], func=AF.Gelu)
                    if q == 0:
                        kb.emit("act", f, reads=[Pb], writes=[Vb])
                    else:
                        kb.emit("act", f, reads=[Pb], pwrites=[Vb])
                kb.emit("dve", lambda h, S=S, V=V: h.tensor_reduce(S[:, 0:1], V[:], AX.X, ALU.add), reads=[Vb], writes=[Sb])
                kb.emit("dve", lambda h, S=S, V=V: h.scalar_tensor_tensor(vn[:], V[:], 1.0, V[:], ALU.mult, ALU.mult,
                                                                         accum_out=S[:, 1:2]), reads=[Vb], writes=[vnb], pwrites=[Sb])
                kb.emit("dve", lambda h, S=S: h.tensor_scalar(S[:, 2:3], S[:, 0:1], 1.0 / SGH, None, ALU.mult), reads=[Sb], pwrites=[Sb])
                kb.emit("dve", lambda h, S=S: h.tensor_tensor(S[:, 3:4], S[:, 2:3], S[:, 2:3], ALU.mult), reads=[Sb], pwrites=[Sb])
                kb.emit("dve", lambda h, S=S: h.scalar_tensor_tensor(S[:, 4:5], S[:, 1:2], 1.0 / SGH, S[:, 3:4], ALU.mult, ALU.subtract),
                        reads=[Sb], pwrites=[Sb])
                kb.emit("dve", lambda h, S=S: h.tensor_scalar(S[:, 4:5], S[:, 4:5], float(LN_EPS), None, ALU.add), reads=[Sb], pwrites=[Sb])
                kb.emit("act", lambda h, S=S: h.activation(out=S[:, 5:6], in_=S[:, 4:5], func=AF.Sqrt), reads=[Sb], pwrites=[Sb])
                kb.emit("dve", lambda h, S=S: h.reciprocal(S[:, 6:7], S[:, 5:6]), reads=[Sb], pwrites=[Sb])
                kb.emit("dve", lambda h, S=S, V=V: h.tensor_scalar(vn[:], V[:], S[:, 2:3], S[:, 6:7], ALU.subtract, ALU.mult),
                        reads=[Vb, Sb], writes=[vnb])
                for c in range(SGC):
                    g = c // 3
                    P, Pb = cx.ps[5 + si % 2][:, 0:128], cx.psb[5 + si % 2]
                    T1, T1b = t1[si % 2], t1b[si % 2]
                    si += 1
                    kb.emit("pe", lambda h, P=P, c=c, g=g: h.matmul(P, vn[:, c * 128:(c + 1) * 128], wsT[:, 0, g * 128:(g + 1) * 128],
                                                                    start=True, stop=True), reads=[vnb, wsTb], writes=[Pb])
                    kb.emit("dve", lambda h, P=P, c=c, T1=T1: h.scalar_tensor_tensor(T1[:], P, lng[:, c:c + 1], bias2[:, c, :],
                                                                                     ALU.mult, ALU.add),
                            reads=[Pb, bias2b, cb], writes=[T1b])
                    o = uv[:, c, tsl]
                    f = lambda h, o=o, T1=T1, c=c, tsl=tsl: h.tensor_tensor(o, T1[:], ufm[:, c, tsl], ALU.mult)
                    if c == 0 and i == 0:
                        kb.emit("pool", f, reads=[T1b, ufmb[c]], writes=[uvb])
                    else:
                        kb.emit("pool", f, reads=[T1b, ufmb[c]], pwrites=[uvb])
            dst = HS[b][:, 0:SGC * TB].rearrange("p (c t) -> p c t", c=SGC)
            kb.emit("sp", lambda h, dst=dst: h.dma_start(out=dst, in_=uv[:]), reads=[uvb], writes=[cx.dram(HS, b)], dma_buf=uvb)
        kb.barrier()
        kb.recycle()


def _grp(kb, P, Pb, mms, extra_reads):
    n = len(mms)
    for i, (l, r) in enumerate(mms):
        f = lambda h, l=l, r=r, i=i: h.matmul(P, l, r, start=(i == 0), stop=(i == n - 1))
        if i == 0:
            kb.emit("pe", f, reads=extra_reads, writes=[Pb], signal=(n == 1))
        else:
            kb.emit("pe", f, reads=extra_reads, pwrites=[Pb], signal=(i == n - 1))


def phase_mla(kb, cx, XSin, HS, QS, Wm, Wuq, Wukv, qg_d, kvg_d, ropeC_d, ropeS_d, seqs):
    nc = kb.nc
    import contextlib
    SC = 96 ** -0.5
    with contextlib.ExitStack() as es:
        wm = sb(es, nc, "wm", [128, DC, 1216], BF16); wmb = Buf("wm")
        load_weight(kb, cx, Wm, wm, wmb, DC, 1216, "wm")
        wuq = sb(es, nc, "wuq", [128, 6, 1536], BF16); wuqb = Buf("wuq")
        load_weight(kb, cx, Wuq, wuq, wuqb, 6, 1536, "wuq")
        wukv = sb(es, nc, "wukv", [128, 2, 1024], BF16); wukvb = Buf("wukv")
        load_weight(kb, cx, Wukv, wukv, wukvb, 2, 1024, "wukv")
        qg = sb(es, nc, "qg", [128, 6], F32); kvg = sb(es, nc, "kvg", [128, 2], F32)
        onesf = sb(es, nc, "onesf", [128, 64], F32)
        cb = Buf("mconst")
        kb.emit("sp", lambda h: h.dma_start(out=qg[:], in_=qg_d[:, :]), writes=[cb], dma_buf=cb)
        kb.emit("sp", lambda h: h.dma_start(out=kvg[:], in_=kvg_d[:, :]), pwrites=[cb], dma_buf=cb)
        kb.emit("pool", lambda h: h.memset(onesf[:], 1.0), pwrites=[cb])
        SMAX = max(seqs)
        KT = sb(es, nc, "KT", [128, 8, SMAX], BF16); KTb = Buf("KT")
        VA = sb(es, nc, "VA", [128, SMAX // 128, 512], BF16); VAb = Buf("VA")
        X = sb(es, nc, "mX", [128, DC, TB], F32); Xb_ = Buf("mX")
        xb = sb(es, nc, "mxb", [128, DC, TB], BF16); xbb = Buf("mxb")
        cq = sb(es, nc, "cq", [128, 6, TB], BF16); cqb = Buf("cq")
        cqs = sb(es, nc, "cqs", [128, 6, TB], BF16); cqsb = Buf("cqs")
        ckv = sb(es, nc, "ckv", [128, 2, TB], BF16); ckvb = Buf("ckv")
        ckvs = sb(es, nc, "ckvs", [128, 2, TB], BF16); ckvsb = Buf("ckvs")
        rq = sb(es, nc, "rq", [128, TB], F32); rqb = Buf("rq")
        rkv = sb(es, nc, "rkv", [128, TB], F32); rkvb = Buf("rkv")
        rtok = sb(es, nc, "rtok", [128, 2], F32); rtokb = Buf("rtok")
        rc = sb(es, nc, "rc", [96, TB], F32); rs = sb(es, nc, "rs", [96, TB], F32); rcb = Buf("rcs")
        ta = sb(es, nc, "ta", [96, TB], F32); tab = Buf("ta")
        tb2 = sb(es, nc, "tb2", [96, TB], F32); tb2b = Buf("tb2")
        kro = sb(es, nc, "kro", [96, TB], BF16); krob = Buf("kro")
        qblk = sb(es, nc, "qblk", [96, 8, TB], BF16); qblkb = Buf("qblk")
        Q5 = sb(es, nc, "Q5", [128, 8, 2 * TB], BF16); Q5b = Buf("Q5")
        PT = [sb(es, nc, "PT%d" % i, [128, 512], BF16) for i in range(3)]; PTb = [Buf("PT%d" % i) for i in range(3)]
        rcp = [sb(es, nc, "rcp%d" % i, [128, 512], F32) for i in range(2)]; rcpb = [Buf("rcp%d" % i) for i in range(2)]
        oa = [sb(es, nc, "oa%d" % i, [128, 512], BF16) for i in range(2)]; oab = [Buf("oa%d" % i) for i in range(2)]
        zb_ = Buf("mzero")
        kb.emit("pool", lambda h: h.memset(KT[96:128, :, :], 0.0), writes=[zb_])
        kb.emit("pool", lambda h: h.memset(Q5[96:128, :, :], 0.0), pwrites=[zb_])
        ps, psb = cx.ps, cx.psb
        b0 = 0
        pi = 0
        for S in seqs:
            nb = S // TB
            for bl in range(nb):
                b = b0 + bl
                t0 = bl * TB
                def xload(bb):
                    src = XSin[bb].rearrange("p (c t) -> p c t", c=DC)
                    kb.emit("sp", lambda h, src=src: h.dma_start(out=X[:], in_=src), reads=[cx.dram(XSin, bb)], writes=[Xb_], dma_buf=Xb_)

                def xcast():
                    kb.emit("pool", lambda h: h.tensor_copy(xb[:], X[:]), reads=[Xb_], writes=[xbb])

                if bl == 0:
                    xload(b)
                    xcast()
                kb.emit("sp", lambda h, t0=t0: h.dma_start(out=rc[64:96, :], in_=ropeC_d[64:96, t0:t0 + TB]), writes=[rcb], dma_buf=rcb)
                kb.emit("sp", lambda h, t0=t0: h.dma_start(out=rs[64:96, :], in_=ropeS_d[64:96, t0:t0 + TB]), pwrites=[rcb], dma_buf=rcb)
                if bl + 1 < nb:
                    xload(b + 1)
                for j in range(8):
                    P, Pb = ps[j % 2][:, 0:TB], psb[j % 2]
                    _grp(kb, P, Pb, [(wm[:, k, j * 128:(j + 1) * 128], xb[:, k, :]) for k in range(DC)], [wmb, xbb])
                    if j < 6:
                        o1, o2, gsc, w1, w2 = cq[:, j, :], cqs[:, j, :], qg[:, j:j + 1], cqb, cqsb
                    else:
                        o1, o2, gsc, w1, w2 = ckv[:, j - 6, :], ckvs[:, j - 6, :], kvg[:, j - 6:j - 5], ckvb, ckvsb
                    kw1 = dict(writes=[w1]) if j in (0, 6) else dict(pwrites=[w1])
                    kw2 = dict(writes=[w2]) if j in (0, 6) else dict(pwrites=[w2])
                    kb.emit("act", lambda h, o1=o1, P=P, gsc=gsc: h.activation(out=o1, in_=P, func=AF.Copy, scale=gsc), reads=[Pb, cb], **kw1)
                    kb.emit("act", lambda h, o2=o2, P=P: h.activation(out=o2, in_=P, func=AF.Square), reads=[Pb], **kw2)
                Pk, Pkb = ps[2][0:96, 0:TB], psb[2]
                Pks, Pksb = ps[3][0:96, 0:TB], psb[3]
                _grp(kb, Pk, Pkb, [(wm[:, k, 1024:1120], xb[:, k, :]) for k in range(DC)], [wmb, xbb])
                _grp(kb, Pks, Pksb, [(wm[:, k, 1120:1216], xb[:, k, :]) for k in range(DC)], [wmb, xbb])
                if bl + 1 < nb:
                    xcast()
                kb.emit("dve", lambda h, Pk=Pk: h.tensor_tensor(ta[64:96, :], Pk[64:96, :], rc[64:96, :], ALU.mult), reads=[Pkb, rcb], writes=[tab])
                kb.emit("dve", lambda h, Pks=Pks: h.tensor_tensor(tb2[64:96, :], Pks[64:96, :], rs[64:96, :], ALU.mult), reads=[Pksb, rcb], writes=[tb2b])
                kb.emit("dve", lambda h: h.tensor_tensor(kro[64:96, :], ta[64:96, :], tb2[64:96, :], ALU.add), reads=[tab, tb2b], writes=[krob])
                for hd in range(8):
                    kw = dict(writes=[KTb]) if (bl == 0 and hd == 0) else dict(pwrites=[KTb])
                    kb.emit("pool", lambda h, hd=hd, t0=t0: h.tensor_copy(KT[64:96, hd, t0:t0 + TB], kro[64:96, :]), reads=[krob], **kw)
                Pq, Pqb = ps[4][:, 0:TB], psb[4]
                Pv, Pvb = ps[5][:, 0:TB], psb[5]
                _grp(kb, Pq, Pqb, [(cx.ones[:], cqs[:, j, :]) for j in range(6)], [cqsb, cx.onesb])
                _grp(kb, Pv, Pvb, [(cx.ones[:], ckvs[:, j, :]) for j in range(2)], [ckvsb, cx.onesb])
                for (R_, Rb, P, Pb, n) in ((rq, rqb, Pq, Pqb, 768), (rkv, rkvb, Pv, Pvb, 256)):
                    kb.emit("dve", lambda h, R_=R_, P=P, n=n: h.tensor_scalar(R_[:], P, 1.0 / n, float(RMS_EPS), ALU.mult, ALU.add), reads=[Pb], writes=[Rb])
                    kb.emit("act", lambda h, R_=R_: h.activation(out=R_[:], in_=R_[:], func=AF.Sqrt), reads=[Rb], writes=[Rb])
                    kb.emit("dve", lambda h, R_=R_: h.reciprocal(R_[:], R_[:]), reads=[Rb], writes=[Rb])
                for hd in range(8):
                    P, Pb = ps[hd % 2][0:96, 0:TB], psb[hd % 2]
                    P2, P2b = ps[2 + hd % 2][0:96, 0:TB], psb[2 + hd % 2]
                    _grp(kb, P, Pb, [(wuq[:, j, hd * 192:hd * 192 + 96], cq[:, j, :]) for j in range(6)], [wuqb, cqb])
                    _grp(kb, P2, P2b, [(wuq[:, j, hd * 192 + 96:hd * 192 + 192], cq[:, j, :]) for j in range(6)], [wuqb, cqb])
                    kw = dict(writes=[qblkb]) if hd == 0 else dict(pwrites=[qblkb])
                    kb.emit("dve", lambda h, P=P, hd=hd: h.scalar_tensor_tensor(qblk[0:64, hd, :], P[0:64, :], float(SC), rq[0:64, :], ALU.mult, ALU.mult),
                            reads=[Pb, rqb], **kw)
                    kb.emit("dve", lambda h, P=P: h.tensor_tensor(ta[64:96, :], P[64:96, :], rc[64:96, :], ALU.mult), reads=[Pb, rcb], writes=[tab])
                    kb.emit("dve", lambda h, P2=P2: h.tensor_tensor(tb2[64:96, :], P2[64:96, :], rs[64:96, :], ALU.mult), reads=[P2b, rcb], writes=[tb2b])
                    kb.emit("pool", lambda h: h.tensor_tensor(ta[64:96, :], ta[64:96, :], tb2[64:96, :], ALU.add), reads=[tab, tb2b], writes=[tab])
                    kb.emit("dve", lambda h, hd=hd: h.scalar_tensor_tensor(qblk[64:96, hd, :], ta[64:96, :], float(SC), rq[64:96, :], ALU.mult, ALU.mult),
                            reads=[tab, rqb], pwrites=[qblkb])
                kb.emit("sp", lambda h, b=b: h.dma_start(out=QS[b][0:96, :].rearrange("p (c t) -> p c t", c=8), in_=qblk[:]), reads=[qblkb],
                        writes=[cx.dram(QS, b)], dma_buf=qblkb)
                for hd in range(8):
                    P, Pb = ps[4 + hd % 2][0:64, 0:TB], psb[4 + hd % 2]
                    _grp(kb, P, Pb, [(wukv[:, j, hd * 64:(hd + 1) * 64], ckv[:, j, :]) for j in range(2)], [wukvb, ckvb])
                    kb.emit("dve", lambda h, P=P, hd=hd, t0=t0: h.tensor_tensor(KT[0:64, hd, t0:t0 + TB], P, rkv[0:64, :], ALU.mult),
                            reads=[Pb, rkvb], pwrites=[KTb])
                for i in range(TB // 128):
                    tsl = slice(i * 128, (i + 1) * 128)
                    Pr, Prb = ps[6][:, 0:1], psb[6]
                    _grp(kb, Pr, Prb, [(ckvs[:, j, tsl], cx.ones[:, 0:1]) for j in range(2)], [ckvsb, cx.onesb])
                    kb.emit("dve", lambda h, Pr=Pr: h.tensor_scalar(rtok[:, 0:1], Pr, 1.0 / 256, float(RMS_EPS), ALU.mult, ALU.add), reads=[Prb], writes=[rtokb])
                    kb.emit("act", lambda h: h.activation(out=rtok[:, 0:1], in_=rtok[:, 0:1], func=AF.Sqrt), reads=[rtokb], writes=[rtokb])
                    kb.emit("dve", lambda h: h.reciprocal(rtok[:, 1:2], rtok[:, 0:1]), reads=[rtokb], writes=[rtokb])
                    P, Pb = ps[7], psb[7]
                    _grp(kb, P[:], Pb, [(ckv[:, j, tsl], wukv[:, j, 512:1024]) for j in range(2)], [wukvb, ckvb])
                    tix = (t0 + i * 128) // 128
                    kb.emit("dve", lambda h, P=P, tix=tix: h.tensor_scalar(VA[:, tix, :], P[:], rtok[:, 1:2], None, ALU.mult),
                            reads=[Pb, rtokb], **(dict(writes=[VAb]) if tix == 0 else dict(pwrites=[VAb])))
            nkc = S // 128
            for qb in range(S // 512):
                for i in range(2):
                    b = b0 + qb * 2 + i
                    kw = dict(writes=[Q5b]) if i == 0 else dict(pwrites=[Q5b])
                    kb.emit("sp", lambda h, b=b, i=i: h.dma_start(out=Q5[0:96, :, i * TB:(i + 1) * TB], in_=QS[b][0:96, :].rearrange("p (c t) -> p c t", c=8)),
                            reads=[cx.dram(QS, b), zb_], dma_buf=Q5b, **kw)
                for hd in range(8):
                    hp, sub = hd // 2, hd % 2
                    r0 = sub * 64
                    O, Ob = ps[sub], psb[sub]
                    Dn, Dnb = ps[5 + sub], psb[5 + sub]
                    def emit_s(kc, hd=hd):
                        nonlocal pi
                        Sx, Sxb = ps[2 + pi % 3], psb[2 + pi % 3]
                        Pt, Ptb = PT[pi % 3], PTb[pi % 3]
                        pi += 1
                        kb.emit("pe", lambda h, Sx=Sx, hd=hd, kc=kc: h.matmul(Sx[:], KT[:, hd, kc * 128:(kc + 1) * 128], Q5[:, hd, :], start=True, stop=True),
                                reads=[KTb, Q5b, zb_], writes=[Sxb])
                        kb.emit("act", lambda h, Pt=Pt, Sx=Sx: h.activation(out=Pt[:], in_=Sx[:], func=AF.Exp), reads=[Sxb], writes=[Ptb])
                        return Pt, Ptb

                    pend = emit_s(0)
                    for kc in range(nkc):
                        Pt, Ptb = pend
                        if kc + 1 < nkc:
                            pend = emit_s(kc + 1)
                        f = lambda h, O=O, hp=hp, kc=kc, Pt=Pt: h.matmul(O[:], VA[:, kc, hp * 128:(hp + 1) * 128], Pt[:], start=(kc == 0), stop=(kc == nkc - 1))
                        g = lambda h, Dn=Dn, kc=kc, Pt=Pt: h.matmul(Dn[:], cx.ones[:], Pt[:], start=(kc == 0), stop=(kc == nkc - 1))
                        if kc == 0:
                            kb.emit("pe", f, reads=[VAb, Ptb], writes=[Ob], signal=(nkc == 1))
                            kb.emit("pe", g, reads=[cx.onesb, Ptb], writes=[Dnb], signal=(nkc == 1))
                        else:
                            kb.emit("pe", f, reads=[VAb, Ptb], pwrites=[Ob], signal=(kc == nkc - 1))
                            kb.emit("pe", g, reads=[cx.onesb, Ptb], pwrites=[Dnb], signal=(kc == nkc - 1))
                    R_, Rb = rcp[sub], rcpb[sub]
                    OA, OAb = oa[sub], oab[sub]
                    kb.emit("dve", lambda h, R_=R_, Dn=Dn, r0=r0: h.reciprocal(R_[r0:r0 + 64, :], Dn[r0:r0 + 64, :]), reads=[Dnb], writes=[Rb])
                    kb.emit("dve", lambda h, OA=OA, O=O, R_=R_, r0=r0: h.tensor_tensor(OA[r0:r0 + 64, :], O[r0:r0 + 64, :], R_[r0:r0 + 64, :], ALU.mult),
                            reads=[Ob, Rb], writes=[OAb])
                    for i in range(2):
                        b = b0 + qb * 2 + i
                        c0 = hp * TB
                        kb.emit("sp", lambda h, b=b, r0=r0, c0=c0, OA=OA, i=i: h.dma_start(out=HS[b][r0:r0 + 64, c0:c0 + TB], in_=OA[r0:r0 + 64, i * TB:(i + 1) * TB]),
                                reads=[OAb], pwrites=[cx.dram(HS, b)], dma_buf=OAb)
            b0 += nb
        kb.barrier()
        kb.recycle()


def mla_host_layouts(w_in, w_uq, w_ukv):
    z64 = np.zeros((w_in.shape[0], 64), np.float32)
    kr = w_in[:, 1024:1056]
    kr_sw = np.concatenate([kr[:, 16:32], kr[:, 0:16]], axis=1)
    wm = np.concatenate([w_in[:, 0:1024], z64, kr, z64, kr_sw], axis=1)
    cols = []
    for h in range(8):
        nope = w_uq[:, h * 96:h * 96 + 64]
        rope = w_uq[:, h * 96 + 64:h * 96 + 96]
        rope_sw = np.concatenate([rope[:, 16:32], rope[:, 0:16]], axis=1)
        cols += [nope, rope, nope, rope_sw]
    wuq = np.concatenate(cols, axis=1)
    kc = [w_ukv[:, h * 128:h * 128 + 64] for h in range(8)]
    vc = [w_ukv[:, h * 128 + 64:h * 128 + 128] for h in range(8)]
    wukv = np.concatenate(kc + vc, axis=1)
    return np.ascontiguousarray(wm), np.ascontiguousarray(wuq), np.ascontiguousarray(wukv)


def rope_tables_host(smax):
    inv = 1.0 / (10000.0 ** (np.arange(0, 32, 2, dtype=np.float32) / 32))
    ang = np.arange(smax, dtype=np.float32)[None, :] * inv[:, None]
    c = np.cos(ang).astype(np.float32)
    s_ = np.sin(ang).astype(np.float32)
    C = np.zeros((96, smax), np.float32)
    S = np.zeros((96, smax), np.float32)
    C[64:80] = c
    C[80:96] = c
    S[64:80] = -s_
    S[80:96] = s_
    return C, S


def gla_masks_host():
    s = np.arange(128)[:, None]
    t = np.arange(128)[None, :]
    A = (s <= t).astype(np.float32)
    B = (s > t).astype(np.float32)
    C = (s >= t).astype(np.float32)
    Dm = (s < t).astype(np.float32)
    return np.ascontiguousarray(np.concatenate([A, B, C, Dm, np.tile(A, (1, 4)), np.tile(B, (1, 4))], axis=1))


def phase_gla(kb, cx, XSin, HS, Wg, wgf_d, bgf_d, wgb_d, bgb_d, gn_d, cm_d, seqs):
    nc = kb.nc
    import contextlib
    with contextlib.ExitStack() as es:
        wg = sb(es, nc, "wg", [128, DC, 1568], BF16); wgb_ = Buf("wg")
        load_weight(kb, cx, Wg, wg, wgb_, DC, 1568, "wg")
        wgate = [sb(es, nc, "wgate%d" % i, [16, 256], F32) for i in range(2)]
        bgate = [sb(es, nc, "bgate%d" % i, [16, 256], F32) for i in range(2)]
        gn = sb(es, nc, "gn", [128, 2], F32)
        cm = sb(es, nc, "cm", [128, 1536], F32)
        onesf = sb(es, nc, "gonesf", [1, 128], F32)
        cb = Buf("gconst")
        kb.emit("sp", lambda h: h.dma_start(out=cm[:], in_=cm_d[:, :]), writes=[cb], dma_buf=cb)
        for i, (wd_, bd_) in enumerate(((wgf_d, bgf_d), (wgb_d, bgb_d))):
            kb.emit("sp", lambda h, i=i, wd_=wd_: h.dma_start(out=wgate[i][:], in_=wd_[:, :]), pwrites=[cb], dma_buf=cb)
            kb.emit("sp", lambda h, i=i, bd_=bd_: h.dma_start(out=bgate[i][:], in_=bd_[:, :]), pwrites=[cb], dma_buf=cb)
        kb.emit("sp", lambda h: h.dma_start(out=gn[:], in_=gn_d[:, :]), pwrites=[cb], dma_buf=cb)
        kb.emit("pool", lambda h: h.memset(onesf[:], 1.0), pwrites=[cb])
        wgate16 = [sb(es, nc, "wgate16_%d" % i, [16, 256], BF16) for i in range(2)]
        bgate16 = [sb(es, nc, "bgate16_%d" % i, [16, 256], BF16) for i in range(2)]
        cm16 = sb(es, nc, "cm16", [128, 512], BF16)
        cb16 = Buf("gconst16")
        kb.emit("dve", lambda h: h.tensor_copy(cm16[:], cm[:, 0:512]), reads=[cb], writes=[cb16])
        for i in range(2):
            kb.emit("dve", lambda h, i=i: h.tensor_copy(wgate16[i][:], wgate[i][:]), reads=[cb], pwrites=[cb16])
            kb.emit("dve", lambda h, i=i: h.tensor_copy(bgate16[i][:], bgate[i][:]), reads=[cb], pwrites=[cb16])
        SMAX = max(seqs)
        ofwd = sb(es, nc, "ofwd", [128, 4, SMAX], F32); ofwdb = Buf("ofwd")
        X = sb(es, nc, "gX", [128, DC, TB], F32); Xb_ = Buf("gX")
        xb = sb(es, nc, "gxb", [128, DC, TB], BF16); xbb = Buf("gxb")
        zs = sb(es, nc, "zs", [16, 128], BF16); zsb = Buf("zs")
        e1 = sb(es, nc, "e1", [128, 256], F32); e1b = Buf("e1")
        L = sb(es, nc, "L", [128, 256], BF16); Lb = Buf("L")
        eb = sb(es, nc, "eb", [128, 256], F32); ebb = Buf("eb")
        enb = sb(es, nc, "enb", [128, 256], F32); enbb = Buf("enb")
        est = sb(es, nc, "est", [128, 256], F32); estb = Buf("est")
        qin = sb(es, nc, "qin", [128, 256], BF16); qinb = Buf("qin")
        kin = sb(es, nc, "kin", [128, 256], BF16); kinb = Buf("kin")
        kst = sb(es, nc, "kst", [128, 256], BF16); kstb = Buf("kst")
        vbf = sb(es, nc, "vbf", [128, 512], BF16); vbfb = Buf("vbf")
        attm = sb(es, nc, "attm", [128, 512], BF16); attmb = Buf("attm")
        St = [sb(es, nc, "St%d" % i, [128, 256], F32) for i in range(2)]; Stb = [Buf("St%d" % i) for i in range(2)]
        Sbf = [sb(es, nc, "Sbf%d" % i, [128, 256], BF16) for i in range(2)]; Sbfb = [Buf("Sbf%d" % i) for i in range(2)]
        osum = sb(es, nc, "osum", [128, 512], F32); osumb = Buf("osum")
        osq = sb(es, nc, "osq", [128, 512], BF16); osqb = Buf("osq")
        rn = sb(es, nc, "rn", [128, 512], F32); rnb = Buf("rn")
        sr = sb(es, nc, "sr", [128, 512], F32); srb = Buf("sr")
        obk = sb(es, nc, "obk", [128, 4, TB], BF16); obkb = Buf("obk")
        ps, psb = cx.ps, cx.psb
        b0 = 0
        for S in seqs:
            nb = S // TB
            for d in range(2):
                for i in range(2):
                    kb.emit("pool", lambda h, i=i: h.memset(St[i][:], 0.0), writes=[Stb[i]])
                    kb.emit("pool", lambda h, i=i: h.memset(Sbf[i][:], 0.0), writes=[Sbfb[i]])
                blks = range(nb) if d == 0 else range(nb - 1, -1, -1)
                mi, ms, ma = ((0, 128, 512), (256, 384, 1024))[d]
                dcol = 127 if d == 0 else 0
                for bl in blks:
                    b = b0 + bl
                    src = XSin[b].rearrange("p (c t) -> p c t", c=DC)
                    kb.emit("sp", lambda h, src=src: h.dma_start(out=X[:], in_=src), reads=[cx.dram(XSin, b)], writes=[Xb_], dma_buf=Xb_)
                    kb.emit("pool", lambda h: h.tensor_copy(xb[:], X[:]), reads=[Xb_], writes=[xbb])
                    tiles = range(2) if d == 0 else range(1, -1, -1)
                    for ti_, i in enumerate(tiles):
                        tsl = slice(i * 128, (i + 1) * 128)
                        t0 = bl * TB + i * 128
                        rx = [wgb_, xbb]
                        for g4 in range(4):
                            f0 = dict(writes=[psb[0]]) if g4 == 0 else dict(pwrites=[psb[0]])
                            for k in range(DC):
                                kb.emit("pe", lambda h, g4=g4, k=k, tsl=tsl: h.matmul(ps[0][:, g4 * 128:(g4 + 1) * 128], wg[:, k, g4 * 128:(g4 + 1) * 128], xb[:, k, tsl],
                                                                                   start=(k == 0), stop=(k == DC - 1)),
                                        reads=rx, signal=(k == DC - 1), **(f0 if k == 0 else dict(pwrites=[psb[0]])))
                        _grp(kb, ps[1][:, 0:256], psb[1], [(xb[:, k, tsl], wg[:, k, 256:512]) for k in range(DC)], rx)
                        _grp(kb, ps[2][:], psb[2], [(xb[:, k, tsl], wg[:, k, 512:1024]) for k in range(DC)], rx)
                        _grp(kb, ps[3][0:16, 0:128], psb[3], [(wg[:, k, 1536 + 16 * d:1552 + 16 * d], xb[:, k, tsl]) for k in range(DC)], rx)
                        kb.emit("act", lambda h: h.activation(out=zs[:], in_=ps[3][0:16, 0:128], func=AF.Copy), reads=[psb[3]], writes=[zsb])
                        _grp(kb, ps[4][:, 0:256], psb[4], [(zs[:], wgate16[d][:]), (cx.ones[0:1, :], bgate16[d][0:1, :])], [zsb, cb16, cx.onesb])
                        kb.emit("act", lambda h: h.activation(out=e1[:], in_=ps[4][:, 0:256], func=AF.Exp, scale=-1.0), reads=[psb[4]], writes=[e1b])
                        kb.emit("act", lambda h: h.activation(out=L[:], in_=e1[:], func=AF.Ln, bias=1.0), reads=[e1b], writes=[Lb])
                        if "G1" in DBG:
                            continue
                        for pr in range(2):
                            kb.emit("pe", lambda h, pr=pr: h.matmul(ps[5][:, pr * 128:(pr + 1) * 128], L[:, pr * 128:(pr + 1) * 128], cm16[:, mi:mi + 128], start=True, stop=True),
                                    reads=[Lb, cb16], **(dict(writes=[psb[5]]) if pr == 0 else dict(pwrites=[psb[5]])))
                        kb.emit("pe", lambda h: h.matmul(ps[6][:, 0:256], cm16[:, ms:ms + 128], L[:], start=True, stop=True), reads=[Lb, cb16], writes=[psb[6]])
                        kb.emit("act", lambda h: h.activation(out=eb[:], in_=ps[5][:, 0:256], func=AF.Exp, scale=-1.0 / 16), reads=[psb[5]], writes=[ebb])
                        kb.emit("act", lambda h: h.activation(out=enb[:], in_=ps[5][:, 0:256], func=AF.Exp, scale=1.0 / 16), reads=[psb[5]], writes=[enbb])
                        kb.emit("act", lambda h: h.activation(out=est[:], in_=ps[6][:, 0:256], func=AF.Exp, scale=-1.0 / 16), reads=[psb[6]], writes=[estb])
                        kb.emit("dve", lambda h: h.scalar_tensor_tensor(qin[:], ps[0][:, 0:256], 0.125, eb[:], ALU.mult, ALU.mult), reads=[psb[0], ebb], writes=[qinb])
                        kb.emit("dve", lambda h: h.tensor_tensor(kin[:], ps[0][:, 256:512], enb[:], ALU.mult), reads=[psb[0], enbb], writes=[kinb])
                        kb.emit("dve", lambda h: h.tensor_tensor(kst[:], ps[1][:, 0:256], est[:], ALU.mult), reads=[psb[1], estb], writes=[kstb])
                        kb.emit("act", lambda h: h.activation(out=vbf[:], in_=ps[2][:], func=AF.Copy), reads=[psb[2]], writes=[vbfb])
                        if "G2" in DBG:
                            continue
                        attb = (7, 0)
                        ob_ = (3, 1)
                        for hd in range(4):
                            pr, r0, par = hd // 2, (hd % 2) * 64, hd % 2
                            bk = attb[par]
                            kb.emit("pe", lambda h, bk=bk, pr=pr, r0=r0: h.matmul(ps[bk][:, pr * 128:(pr + 1) * 128], kin[r0:r0 + 64, pr * 128:(pr + 1) * 128],
                                                                                  qin[r0:r0 + 64, pr * 128:(pr + 1) * 128], start=True, stop=True),
                                    reads=[kinb, qinb], **(dict(writes=[psb[bk]]) if pr == 0 else dict(pwrites=[psb[bk]])))
                        attm4 = attm[:].rearrange("p (a b t) -> p a b t", a=2, b=2)
                        mk = cm[:, ma:ma + 256].rearrange("p (a t) -> p a t", a=2)
                        for par in range(2):
                            bk = attb[par]
                            kw = dict(writes=[attmb]) if par == 0 else dict(pwrites=[attmb])
                            kb.emit("dve", lambda h, bk=bk, par=par: h.tensor_tensor(attm4[:, :, par, :], ps[bk][:, 0:256].rearrange("p (a t) -> p a t", a=2), mk, ALU.mult),
                                    reads=[psb[bk], cb], **kw)
                        for hd in range(4):
                            pr, r0, par = hd // 2, (hd % 2) * 64, hd % 2
                            bk = ob_[par]
                            O = ps[bk][:, pr * 128:(pr + 1) * 128]
                            kb.emit("pe", lambda h, O=O, hd=hd: h.matmul(O, vbf[:, hd * 128:(hd + 1) * 128], attm[:, hd * 128:(hd + 1) * 128], start=True, stop=False),
                                    reads=[vbfb, attmb], signal=False, **(dict(writes=[psb[bk]]) if pr == 0 else dict(pwrites=[psb[bk]])))
                            kb.emit("pe", lambda h, O=O, hd=hd, pr=pr, r0=r0: h.matmul(O, Sbf[pr][r0:r0 + 64, (hd % 2) * 128:(hd % 2 + 1) * 128],
                                                                                     qin[r0:r0 + 64, pr * 128:(pr + 1) * 128], start=False, stop=True),
                                    reads=[Sbfb[pr], qinb], pwrites=[psb[bk]])
                        if "G3" in DBG:
                            continue
                        for pr in range(2):
                            Dp, Dpb = ps[5 + pr][:, 0:256], psb[5 + pr]
                            kb.emit("pe", lambda h, Dp=Dp, pr=pr: h.matmul(Dp, kst[:, pr * 128:(pr + 1) * 128], vbf[:, pr * 256:(pr + 1) * 256], start=True, stop=True),
                                    reads=[kstb, vbfb], writes=[Dpb])
                            dc = pr * 128 + dcol
                            kb.emit("dve", lambda h, Dp=Dp, pr=pr, dc=dc: h.scalar_tensor_tensor(St[pr][:], St[pr][:], eb[:, dc:dc + 1], Dp, ALU.mult, ALU.add),
                                    reads=[Stb[pr], ebb, Dpb], writes=[Stb[pr]])
                            kb.emit("act", lambda h, pr=pr: h.activation(out=Sbf[pr][:], in_=St[pr][:], func=AF.Copy), reads=[Stb[pr]], writes=[Sbfb[pr]])
                        if "G4" in DBG:
                            continue
                        if d == 0:
                            of4 = ofwd[:].rearrange("p (a b) s -> p a b s", a=2, b=2)
                            for par in range(2):
                                bk = ob_[par]
                                kw = dict(writes=[ofwdb]) if (bl == 0 and i == 0 and par == 0) else dict(pwrites=[ofwdb])
                                kb.emit("act", lambda h, t0=t0, bk=bk, par=par: h.activation(out=of4[:, :, par, t0:t0 + 128], in_=ps[bk][:, 0:256].rearrange("p (a t) -> p a t", a=2),
                                                                                         func=AF.Copy), reads=[psb[bk]], **kw)
                        else:
                            of4 = ofwd[:].rearrange("p (a b) s -> p a b s", a=2, b=2)
                            os4 = osum[:].rearrange("p (a b t) -> p a b t", a=2, b=2)
                            for par in range(2):
                                bk = ob_[par]
                                kw = dict(writes=[osumb]) if par == 0 else dict(pwrites=[osumb])
                                kb.emit("dve", lambda h, t0=t0, bk=bk, par=par: h.tensor_tensor(os4[:, :, par, :], ps[bk][:, 0:256].rearrange("p (a t) -> p a t", a=2),
                                                                                            of4[:, :, par, t0:t0 + 128], ALU.add), reads=[psb[bk], ofwdb], **kw)
                            kb.emit("pool", lambda h: h.tensor_tensor(osq[:], osum[:], osum[:], ALU.mult), reads=[osumb], writes=[osqb])
                            kb.emit("pe", lambda h: h.matmul(ps[7][:], cx.ones[:], osq[:], start=True, stop=True), reads=[osqb, cx.onesb], writes=[psb[7]])
                            kb.emit("dve", lambda h: h.tensor_scalar(rn[:], ps[7][:], 1.0 / 128, float(RMS_EPS), ALU.mult, ALU.add), reads=[psb[7]], writes=[rnb])
                            kb.emit("act", lambda h: h.activation(out=rn[:], in_=rn[:], func=AF.Sqrt), reads=[rnb], writes=[rnb])
                            kb.emit("dve", lambda h: h.reciprocal(rn[:], rn[:]), reads=[rnb], writes=[rnb])
                            kb.emit("dve", lambda h: h.tensor_tensor(osum[:], osum[:], rn[:], ALU.mult), reads=[osumb, rnb], writes=[osumb])
                            for hd in range(4):
                                f0 = dict(writes=[psb[4]]) if hd == 0 else dict(pwrites=[psb[4]])
                                for k in range(DC):
                                    kb.emit("pe", lambda h, hd=hd, k=k, tsl=tsl: h.matmul(ps[4][:, hd * 128:(hd + 1) * 128], wg[:, k, 1024 + hd * 128:1024 + (hd + 1) * 128],
                                                                                       xb[:, k, tsl], start=(k == 0), stop=(k == DC - 1)),
                                            reads=rx, signal=(k == DC - 1), **(f0 if k == 0 else dict(pwrites=[psb[4]])))
                            kb.emit("act", lambda h: h.activation(out=sr[:], in_=ps[4][:], func=AF.Silu), reads=[psb[4]], writes=[srb])
                            kw = dict(writes=[obkb]) if ti_ == 0 else dict(pwrites=[obkb])
                            kb.emit("dve", lambda h, tsl=tsl: h.scalar_tensor_tensor(obk[:, :, tsl], osum[:].rearrange("p (h t) -> p h t", h=4), gn[:, 0:1],
                                                                                    sr[:].rearrange("p (h t) -> p h t", h=4), ALU.mult, ALU.mult),
                                    reads=[osumb, srb, cb], **kw)
                    if d == 1:
                        dst = HS[b][:, 4 * TB:8 * TB].rearrange("p (c t) -> p c t", c=4)
                        kb.emit("sp", lambda h, dst=dst: h.dma_start(out=dst, in_=obk[:]), reads=[obkb], pwrites=[cx.dram(HS, b)], dma_buf=obkb)
            b0 += nb
        kb.barrier()
        kb.recycle()


SEQS = [2048, 2048, 4096, 4096]


def build_full(seqs):
    import contextlib
    T = sum(seqs)
    NB = T // TB
    nc = bass.Bass("TRN2", target_bir_lowering=False)

    def din(name, shape):
        return nc.dram_tensor(name, list(shape), F32, kind="ExternalInput").ap()

    x = din("x", [T, D])
    y = nc.dram_tensor("y", [T, D], F32, kind="ExternalOutput").ap()
    W = {}
    for l in ("l0", "l1"):
        for f in ("ffa", "ffb"):
            W[l + f + "gu"] = din(l + f + "gu", [D, 2 * FH])
            W[l + f + "dn"] = din(l + f + "dn", [FH, D])
        for i in (1, 2, 3):
            W[l + "g%d" % i] = din(l + "g%d" % i, [128, DC])
            W[l + "b%d" % i] = din(l + "b%d" % i, [128, DC])
    wm = din("wm", [D, 1216]); wuq = din("wuq", [768, 1536]); wukv = din("wukv", [256, 1024])
    qg = din("qg", [128, 6]); kvg = din("kvg", [128, 2])
    SMAX = max(seqs)
    rC = din("rC", [96, SMAX]); rS = din("rS", [96, SMAX])
    wg = din("wg", [D, 1568]); wgf = din("wgf", [16, 256]); bgf = din("bgf", [16, 256]); wgb = din("wgb", [16, 256]); bgb = din("bgb", [16, 256])
    gn = din("gn", [128, 2]); cm = din("cm", [128, 1536])
    wout0 = din("wout0", [1024, D])
    swin = din("swin", [D, 6144]); swout = din("swout", [3072, D]); wsT = din("wsT", [128, 1024]); bsbc = din("bsbc", [128, 1024])
    slng = din("slng", [128, 24]); slnb = din("slnb", [128, 24])
    with contextlib.ExitStack() as es:
        kb = KB(nc, es)
        cx = make_ctx(nc, kb, es)
        XA = DS(nc, "XA", NB, DC * TB, F32)
        XB = DS(nc, "XB", NB, DC * TB, F32)
        HS = DS(nc, "HS", NB, 24 * TB, BF16)
        QS = DS(nc, "QS", NB, 8 * TB, BF16)
        phase_f1(kb, cx, es, "tm", x, XA, HS, W["l0ffagu"], NB)
        phase_f2(kb, cx, HS, FHC, W["l0ffadn"], 0.5, W["l0g1"], W["l0b1"], XA, "fm", XB, NB, "a")
        phase_mla(kb, cx, XB, HS, QS, wm, wuq, wukv, qg, kvg, rC, rS, seqs)
        phase_gla(kb, cx, XB, HS, wg, wgf, bgf, wgb, bgb, gn, cm, seqs)
        phase_f2(kb, cx, HS, 8, wout0, 1.0, W["l0g2"], W["l0b2"], XB, "fm", XA, NB, "b")
        phase_f1(kb, cx, es, "fm", XA, None, HS, W["l0ffbgu"], NB)
        phase_f2(kb, cx, HS, FHC, W["l0ffbdn"], 0.5, W["l0g3"], W["l0b3"], XA, "fm", XB, NB, "c")
        phase_f1(kb, cx, es, "fm", XB, None, HS, W["l1ffagu"], NB)
        phase_f2(kb, cx, HS, FHC, W["l1ffadn"], 0.5, W["l1g1"], W["l1b1"], XB, "fm", XA, NB, "d")
        phase_c1(kb, cx, XA, HS, swin, wsT, bsbc, slng, slnb, NB)
        phase_f2(kb, cx, HS, SGC, swout, 1.0, W["l1g2"], W["l1b2"], XA, "fm", XB, NB, "e")
        phase_f1(kb, cx, es, "fm", XB, None, HS, W["l1ffbgu"], NB)
        phase_f2(kb, cx, HS, FHC, W["l1ffbdn"], 0.5, W["l1g3"], W["l1b3"], XB, "tm", y, NB, "f")
        finish(kb, cx)
    return nc


def shared_inputs(p, seqs):
    pc = lambda v, c: np.ascontiguousarray(v.reshape(c, 128).T.astype(np.float32))
    shared = dict(const_inputs())
    for l in ("l0", "l1"):
        for f in ("ffa", "ffb"):
            shared[l + f + "gu"] = p["%s_%s_w_gu" % (l, f)]
            shared[l + f + "dn"] = p["%s_%s_w_down" % (l, f)]
        for i in (1, 2, 3):
            shared[l + "g%d" % i] = pc(p["%s_ln%d_g" % (l, i)], DC)
            shared[l + "b%d" % i] = pc(p["%s_ln%d_b" % (l, i)], DC)
    WM, WUQ, WUKV = mla_host_layouts(p["l0_w_in"], p["l0_mla_w_uq"], p["l0_mla_w_ukv"])
    C, S_ = rope_tables_host(max(seqs))
    shared.update(wm=WM, wuq=WUQ, wukv=WUKV, qg=pc(p["l0_mla_q_norm"], 6), kvg=pc(p["l0_mla_kv_norm"], 2), rC=C, rS=S_,
                  wg=np.ascontiguousarray(p["l0_w_in"][:, 1056:2624]), wgf=p["l0_gla_w_gate_f"], bgf=np.concatenate([p["l0_gla_b_gate_f"].reshape(1, 256), np.zeros((15, 256), np.float32)], 0),
                  wgb=p["l0_gla_w_gate_b"], bgb=np.concatenate([p["l0_gla_b_gate_b"].reshape(1, 256), np.zeros((15, 256), np.float32)], 0), gn=np.concatenate([p["l0_gla_norm"].reshape(128, 1)] * 2, 1), cm=gla_masks_host(),
                  wout0=p["l0_w_out"], swin=p["l1_sgu_w_in"], swout=p["l1_sgu_w_out"],
                  wsT=np.ascontiguousarray(p["l1_sgu_w_s"].transpose(2, 0, 1).reshape(128, 1024)),
                  bsbc=np.ascontiguousarray(np.broadcast_to(p["l1_sgu_b_s"].reshape(1, 1024), (128, 1024))),
                  slng=pc(p["l1_sgu_ln_g"], 24), slnb=pc(p["l1_sgu_ln_b"], 24))
    return {k: np.ascontiguousarray(v, dtype=np.float32) for k, v in shared.items()}


def kernel(**inp):
    p = {k: np.asarray(v) for k, v in inp.items()}
    seqs = SEQS
    nc = build_full(seqs)
    shared = shared_inputs(p, seqs)
    xp, xs = p["x_prompt"], p["x_sample"]
    in_maps = []
    for c in range(N_CORES):
        xc = np.concatenate([xp[2 * c].reshape(-1, D), xp[2 * c + 1].reshape(-1, D), xs[2 * c].reshape(-1, D), xs[2 * c + 1].reshape(-1, D)], axis=0)
        m = dict(shared)
        m["x"] = np.ascontiguousarray(xc)
        in_maps.append(m)
    res = run_bass_kernel_spmd(nc, in_maps, core_ids=list(range(N_CORES)))
    yp = np.empty_like(xp)
    ys = np.empty_like(xs)
    for c in range(N_CORES):
        yc = res.results[c]["y"]
        yp[2 * c] = yc[0:2048]
        yp[2 * c + 1] = yc[2048:4096]
        ys[2 * c] = yc[4096:8192]
        ys[2 * c + 1] = yc[8192:12288]
    return (yp, ys)
```

```python
import numpy as np
import concourse.bass as bass
import concourse.mybir as mybir
from concourse.bass_utils import run_bass_kernel_spmd

F32 = mybir.dt.float32
BF16 = mybir.dt.bfloat16
AF = mybir.ActivationFunctionType
ALU = mybir.AluOpType
AX = mybir.AxisListType

D = 1024
DC = 8
FH = 2816
FHC = 22
TB = 256
ALPHA = 4 ** 0.25
LN_EPS = 1e-5
RMS_EPS = 1e-6
N_CORES = 8
import os
DBG = os.environ.get("KDBG", "")


class Buf:
    _n = 0

    def __init__(self, name=""):
        Buf._n += 1
        self.key = "b%d_%s" % (Buf._n, name)
        self.prev = {}
        self.lw = {}
        self.rd = {}
        self.sem = None
        self.dcnt = 0


def _merge(dst, src):
    for k, (s, v) in src.items():
        if k not in dst or dst[k][1] < v:
            dst[k] = (s, v)


class Eng:
    def __init__(self, key, h, sem):
        self.key = key
        self.h = h
        self.sem = sem
        self.cnt = 0
        self.seen = {}


class KB:
    def __init__(self, nc, es):
        self.nc = nc
        self.es = es
        self.eng = {}
        for key, h in (("pe", nc.tensor), ("act", nc.scalar), ("dve", nc.vector), ("pool", nc.gpsimd), ("sp", nc.sync)):
            sem = es.enter_context(nc.semaphore("sem_" + key))
            self.eng[key] = Eng(key, h, sem)
        self.dma_bufs = []
        self.free_sems = []
        self.nsem = 0
        self.nins = 0

    def dma_sem(self, b):
        if b.sem is None:
            if self.free_sems:
                b.sem, b.dcnt, b.semkey = self.free_sems.pop()
            else:
                self.nsem += 1
                b.semkey = "dsem%d" % self.nsem
                b.sem = self.es.enter_context(self.nc.semaphore(b.semkey))
                b.dcnt = 0
            self.dma_bufs.append(b)
        return b.sem

    def recycle(self):
        for b in self.dma_bufs:
            self.free_sems.append((b.sem, b.dcnt, b.semkey))
            b.sem = None
        self.dma_bufs = []

    def emit(self, eng, fn, reads=(), writes=(), pwrites=(), dma_buf=None, signal=True):
        E = self.eng[eng]
        deps = {}
        for b in reads:
            _merge(deps, b.lw)
        for b in writes:
            p = {}
            _merge(p, b.lw)
            _merge(p, b.rd)
            b.prev = p
            b.lw = {}
            b.rd = {}
        for b in list(writes) + list(pwrites):
            _merge(deps, b.prev)
        for k, (s, v) in deps.items():
            if k == "pe" and eng == "pe":
                continue
            if E.seen.get(k, 0) >= v:
                continue
            E.h.wait_ge(s, v)
            E.seen[k] = v
        ins = fn(E.h)
        self.nins += 1
        if dma_buf is not None:
            sem = self.dma_sem(dma_buf)
            dma_buf.dcnt += 16
            ins.then_inc(sem, 16)
            ev = (dma_buf.semkey, sem, dma_buf.dcnt)
        elif signal:
            E.cnt += 1
            ins.then_inc(E.sem, 1)
            ev = (E.key, E.sem, E.cnt)
        else:
            ev = (E.key, E.sem, E.cnt + 1)
        d = {ev[0]: (ev[1], ev[2])}
        for b in reads:
            _merge(b.rd, d)
        for b in list(writes) + list(pwrites):
            _merge(b.lw, d)
        return ins

    def barrier(self):
        for E in self.eng.values():
            for E2 in self.eng.values():
                if E2 is E or E2.cnt == 0:
                    continue
                if E.seen.get(E2.key, 0) >= E2.cnt:
                    continue
                E.h.wait_ge(E2.sem, E2.cnt)
                E.seen[E2.key] = E2.cnt
            for b in self.dma_bufs:
                if b.dcnt == 0 or E.seen.get(b.semkey, 0) >= b.dcnt:
                    continue
                E.h.wait_ge(b.sem, b.dcnt)
                E.seen[b.semkey] = b.dcnt


class Ctx:
    pass


_sbn = [0]


def sb(es, nc, name, shape, dt):
    _sbn[0] += 1
    return es.enter_context(nc.sbuf_tensor("s%d_%s" % (_sbn[0], name), shape, dt))


def load_weight(kb, cx, W3, dst, dstbuf, KC, N, tag, scale=None):
    nc = kb.nc
    CH = 1536
    i = 0
    first = True
    for k in range(KC):
        for c0 in range(0, N, CH):
            w = min(CH, N - c0)
            st, stb = cx.stage[i % len(cx.stage)]
            kb.emit("sp", lambda h, st=st, k=k, c0=c0, w=w: h.dma_start(out=st[:, 0:w], in_=W3[k * 128:(k + 1) * 128, c0:c0 + w]),
                    writes=[stb], dma_buf=stb)
            e = ("act", "dve", "pool")[i % 3]
            o = dst[:, k, c0:c0 + w]
            if e == "act":
                f = lambda h, st=st, o=o, w=w: h.activation(out=o, in_=st[:, 0:w], func=AF.Copy)
            else:
                f = lambda h, st=st, o=o, w=w: h.tensor_copy(o, st[:, 0:w])
            if first:
                kb.emit(e, f, reads=[stb], writes=[dstbuf])
                first = False
            else:
                kb.emit(e, f, reads=[stb], pwrites=[dstbuf])
            i += 1


def phase_f1(kb, cx, es0, src_mode, Xsrc, XSout, HS, Wgu, NB):
    nc = kb.nc
    import contextlib
    with contextlib.ExitStack() as es:
        wgu = sb(es, nc, "wgu", [128, DC, 2 * FH], BF16)
        wgub = Buf("wgu")
        if "w" not in DBG:
            load_weight(kb, cx, Wgu, wgu, wgub, DC, 2 * FH, "gu")
        NX = 3
        xT = [sb(es, nc, "xT%d" % i, [128, DC, TB], F32) for i in range(NX)]
        xTb = [Buf("xT%d" % i) for i in range(NX)]
        xb = [sb(es, nc, "xb%d" % i, [128, DC, TB], BF16) for i in range(NX)]
        xbb = [Buf("xb%d" % i) for i in range(NX)]
        hh = [sb(es, nc, "hh%d" % i, [128, FHC, TB], BF16) for i in range(2)]
        hhb = [Buf("hh%d" % i) for i in range(2)]
        sg = [sb(es, nc, "sg%d" % i, [128, TB], F32) for i in range(3)]
        sgb = [Buf("sg%d" % i) for i in range(3)]
        if src_mode == "tm":
            xtm = [sb(es, nc, "xtm%d" % i, [128, D], F32) for i in range(4)]
            xtmb = [Buf("xtm%d" % i) for i in range(4)]
        gu_ps = []
        for i in range(2):
            gu_ps.append((cx.ps[i][:, 0:TB], Buf("gps%d" % i), cx.ps[2 + i][:, 0:TB], Buf("ups%d" % i)))
        tp_ps = [(cx.ps[4 + i], Buf("tps%d" % i)) for i in range(4)]
        gi = 0
        si = 0
        ti = 0
        def tm_load(b):
            for i in range(TB // 128):
                t, tb_ = xtm[(2 * b + i) % 4], xtmb[(2 * b + i) % 4]
                r0 = b * TB + i * 128
                kb.emit("sp", lambda h, t=t, r0=r0: h.dma_start(out=t[:], in_=Xsrc[r0:r0 + 128, :]),
                        writes=[tb_], dma_buf=tb_)

        def tm_transpose(b):
            nonlocal ti
            X, Xb_ = xT[b % NX], xTb[b % NX]
            XB, XBb = xb[b % NX], xbb[b % NX]
            tl = [(xtm[(2 * b + i) % 4], xtmb[(2 * b + i) % 4]) for i in range(TB // 128)]
            first = True
            for c2 in range(DC // 2):
                P, Pb = tp_ps[ti % 4]
                ti += 1
                n = 0
                for cc in range(2):
                    c = c2 * 2 + cc
                    for i, (t, tb_) in enumerate(tl):
                        o = P[:, cc * TB + i * 128: cc * TB + (i + 1) * 128]
                        last = (n == 3)
                        f = lambda h, o=o, t=t, c=c: h.transpose(o, t[:, c * 128:(c + 1) * 128], cx.ident[:])
                        if n == 0:
                            kb.emit("pe", f, reads=[tb_, cx.identb], writes=[Pb], signal=last)
                        else:
                            kb.emit("pe", f, reads=[tb_, cx.identb], pwrites=[Pb], signal=last)
                        n += 1
                o32 = X[:, c2 * 2:c2 * 2 + 2, :]
                o16 = XB[:, c2 * 2:c2 * 2 + 2, :]
                pin = P[:, 0:2 * TB].rearrange("p (c t) -> p c t", c=2)
                kw = dict(writes=[Xb_]) if first else dict(pwrites=[Xb_])
                kb.emit("act", lambda h, o32=o32, pin=pin: h.activation(out=o32, in_=pin, func=AF.Copy),
                        reads=[Pb], **kw)
                kw = dict(writes=[XBb]) if first else dict(pwrites=[XBb])
                kb.emit("dve", lambda h, o16=o16, o32=o32: h.tensor_copy(o16, o32), reads=[Xb_], **kw)
                first = False
            dst = XSout[b].rearrange("p (c t) -> p c t", c=DC)
            kb.emit("sp", lambda h, dst=dst, X=X: h.dma_start(out=dst, in_=X[:]), reads=[Xb_],
                    writes=[cx.dram(XSout, b)], dma_buf=Xb_)

        def fm_load(b):
            X, Xb_ = xT[b % NX], xTb[b % NX]
            XB, XBb = xb[b % NX], xbb[b % NX]
            src = Xsrc[b].rearrange("p (c t) -> p c t", c=DC)
            kb.emit("sp", lambda h, src=src, X=X: h.dma_start(out=X[:], in_=src), reads=[cx.dram(Xsrc, b)],
                    writes=[Xb_], dma_buf=Xb_)
            kb.emit("pool", lambda h, X=X, XB=XB: h.tensor_copy(XB[:], X[:]), reads=[Xb_], writes=[XBb])

        if src_mode == "tm":
            tm_load(0)
            tm_transpose(0)
            if NB > 1:
                tm_load(1)
        else:
            fm_load(0)
        for b in range(NB):
            X, Xb_ = xT[b % NX], xTb[b % NX]
            XB, XBb = xb[b % NX], xbb[b % NX]
            if src_mode == "fm" and b + 1 < NB:
                fm_load(b + 1)
            H, Hb = hh[b % 2], hhb[b % 2]
            for j in range(FHC if "c" not in DBG else 0):
                if src_mode == "tm" and j == FHC // 2 and b + 1 < NB:
                    tm_transpose(b + 1)
                    if b + 2 < NB:
                        tm_load(b + 2)
                G, Gb, U, Ub = gu_ps[gi % 2]
                gi += 1
                for k in range(DC):
                    f = lambda h, G=G, j=j, k=k, XB=XB: h.matmul(G, wgu[:, k, j * 128:(j + 1) * 128], XB[:, k, :],
                                                                 start=(k == 0), stop=(k == DC - 1))
                    if k == 0:
                        kb.emit("pe", f, reads=[wgub, XBb], writes=[Gb], signal=False)
                    else:
                        kb.emit("pe", f, reads=[wgub, XBb], pwrites=[Gb], signal=(k == DC - 1))
                for k in range(DC):
                    f = lambda h, U=U, j=j, k=k, XB=XB: h.matmul(U, wgu[:, k, FH + j * 128:FH + (j + 1) * 128],
                                                                 XB[:, k, :], start=(k == 0), stop=(k == DC - 1))
                    if k == 0:
                        kb.emit("pe", f, reads=[wgub, XBb], writes=[Ub], signal=False)
                    else:
                        kb.emit("pe", f, reads=[wgub, XBb], pwrites=[Ub], signal=(k == DC - 1))
                S, Sb = sg[si % 3], sgb[si % 3]
                si += 1
                kb.emit("act", lambda h, S=S, G=G: h.activation(out=S[:], in_=G, func=AF.Silu), reads=[Gb], writes=[Sb])
                o = H[:, j, :]
                f = lambda h, o=o, S=S, U=U: h.tensor_tensor(o, S[:], U, ALU.mult)
                if j == 0:
                    kb.emit("dve", f, reads=[Sb, Ub], writes=[Hb])
                else:
                    kb.emit("dve", f, reads=[Sb, Ub], pwrites=[Hb])
            if src_mode == "tm" and "c" in DBG and b + 1 < NB:
                tm_transpose(b + 1)
                if b + 2 < NB:
                    tm_load(b + 2)
            dst = HS[b][:, 0:FHC * TB].rearrange("p (c t) -> p c t", c=FHC)
            if "h" not in DBG:
                kb.emit("sp", lambda h, dst=dst, H=H: h.dma_start(out=dst, in_=H[:]), reads=[Hb],
                        writes=[cx.dram(HS, b)], dma_buf=Hb)
        kb.barrier()
        kb.recycle()


def phase_f2(kb, cx, AS, KC, Wd, csc, lng, lnb, XSin, dst_mode, Xdst, NB, tagp):
    nc = kb.nc
    import contextlib
    with contextlib.ExitStack() as es:
        wd = sb(es, nc, "wd", [128, KC, D], BF16)
        wdb = Buf("wd")
        load_weight(kb, cx, Wd, wd, wdb, KC, D, "wd")
        gbg = sb(es, nc, "lng", [128, DC], F32)
        gbv = sb(es, nc, "lnb", [128, DC], F32)
        gbb = Buf("lngb")
        kb.emit("sp", lambda h: h.dma_start(out=gbg[:], in_=lng[:, :]), writes=[gbb], dma_buf=gbb)
        kb.emit("sp", lambda h: h.dma_start(out=gbv[:], in_=lnb[:, :]), pwrites=[gbb], dma_buf=gbb)
        A = [sb(es, nc, "A%d" % i, [128, KC, TB], BF16) for i in range(3)]
        Ab = [Buf("A%d" % i) for i in range(3)]
        Z = [sb(es, nc, "Z%d" % i, [128, DC, TB], F32) for i in range(3)]
        Zb = [[Buf("Z%d_%d" % (i, m)) for m in range(DC)] for i in range(3)]
        zb16 = [sb(es, nc, "zb%d" % i, [128, TB], BF16) for i in range(3)]
        zb16b = [Buf("zb%d" % i) for i in range(3)]
        zsq = [sb(es, nc, "zsq%d" % i, [128, TB], BF16) for i in range(3)]
        zsqb = [Buf("zsq%d" % i) for i in range(3)]
        mean = sb(es, nc, "mean", [128, TB], F32)
        meanb = Buf("mean")
        msq = sb(es, nc, "msq", [128, TB], F32)
        msqb = Buf("msq")
        rstd = sb(es, nc, "rstd", [128, TB], F32)
        rstdb = Buf("rstd")
        tt = [sb(es, nc, "tt%d" % i, [128, TB], F32) for i in range(4)]
        ttb = [Buf("tt%d" % i) for i in range(4)]
        if dst_mode == "tm":
            ytm = [sb(es, nc, "ytm%d" % i, [128, D], F32) for i in range(2)]
            ytmb = [Buf("ytm%d" % i) for i in range(2)]
        if dst_mode == "tm":
            ybanks, s1banks, s2banks = (0, 1), (2, 4), (3, 5)
        else:
            ybanks, s1banks, s2banks = (0, 1, 4, 5), (2, 6), (3, 7)
        NY = len(ybanks)
        y_ps = [(cx.ps[i][:, 0:TB], Buf("yps%d" % i)) for i in ybanks]
        s1s = [(cx.ps[i][:, 0:TB], Buf("s1_%d" % i)) for i in s1banks]
        s2s = [(cx.ps[i][:, 0:TB], Buf("s2_%d" % i)) for i in s2banks]
        tp_ps = [(cx.ps[6 + i], Buf("tps%d" % i)) for i in range(2)]
        LAG = 3
        yi = 0
        ti = 0
        eps = LN_EPS / (ALPHA * ALPHA)
        cz = csc / ALPHA
        def load(b):
            Ai, Aib = A[b % 3], Ab[b % 3]
            Zi, Zib = Z[b % 3], Zb[b % 3]
            src = AS[b][:, 0:KC * TB].rearrange("p (c t) -> p c t", c=KC)
            kb.emit("sp", lambda h, src=src, Ai=Ai: h.dma_start(out=Ai[:], in_=src), reads=[cx.dram(AS, b)],
                    writes=[Aib], dma_buf=Aib)
            srcx = XSin[b].rearrange("p (c t) -> p c t", c=DC)
            kb.emit("sp", lambda h, srcx=srcx, Zi=Zi: h.dma_start(out=Zi[:], in_=srcx), reads=[cx.dram(XSin, b)],
                    writes=Zib, dma_buf=Zib[0])

        pend = [None]

        def chunk(b, m):
            nonlocal yi
            Ai, Aib = A[b % 3], Ab[b % 3]
            Zi, Zib = Z[b % 3], Zb[b % 3]
            s1, s2 = s1s[b % 2], s2s[b % 2]
            pend_stats = pend[0]
            if True:
                Y, Yb = y_ps[yi % NY]
                yi += 1
                for k in range(KC):
                    f = lambda h, Y=Y, m=m, k=k, Ai=Ai: h.matmul(Y, wd[:, k, m * 128:(m + 1) * 128], Ai[:, k, :],
                                                                 start=(k == 0), stop=(k == KC - 1))
                    if k == 0:
                        kb.emit("pe", f, reads=[wdb, Aib], writes=[Yb], signal=(KC == 1))
                    else:
                        kb.emit("pe", f, reads=[wdb, Aib], pwrites=[Yb], signal=(k == KC - 1))
                zc = Zi[:, m, :]
                kb.emit("dve", lambda h, zc=zc, Y=Y: h.scalar_tensor_tensor(zc, Y, float(cz), zc, ALU.mult, ALU.add),
                        reads=[Yb, Zib[m]], pwrites=[Zib[m]])
                z16, z16b = zb16[m % 3], zb16b[m % 3]
                zq, zqb = zsq[m % 3], zsqb[m % 3]
                kb.emit("act", lambda h, z16=z16, zc=zc: h.activation(out=z16[:], in_=zc, func=AF.Copy),
                        reads=[Zib[m]], writes=[z16b])
                kb.emit("act", lambda h, zq=zq, zc=zc: h.activation(out=zq[:], in_=zc, func=AF.Square),
                        reads=[Zib[m]], writes=[zqb])
                def stats(z16=z16, z16b=z16b, zq=zq, zqb=zqb, m=m):
                    f1 = lambda h: h.matmul(s1[0], cx.ones[:], z16[:], start=(m == 0), stop=(m == DC - 1))
                    f2_ = lambda h: h.matmul(s2[0], cx.ones[:], zq[:], start=(m == 0), stop=(m == DC - 1))
                    if m == 0:
                        kb.emit("pe", f1, reads=[z16b, cx.onesb], writes=[s1[1]])
                        kb.emit("pe", f2_, reads=[zqb, cx.onesb], writes=[s2[1]])
                    else:
                        kb.emit("pe", f1, reads=[z16b, cx.onesb], pwrites=[s1[1]])
                        kb.emit("pe", f2_, reads=[zqb, cx.onesb], pwrites=[s2[1]])
                if pend_stats is not None:
                    pend_stats()
                if m == DC - 1:
                    stats()
                    pend[0] = None
                else:
                    pend[0] = stats

        def epilogue_pieces(b):
            pieces = []
            Zi, Zib = Z[b % 3], Zb[b % 3]
            s1, s2 = s1s[b % 2], s2s[b % 2]

            def p_stats():
                kb.emit("act", lambda h: h.activation(out=mean[:], in_=s1[0], func=AF.Copy, scale=1.0 / D),
                        reads=[s1[1]], writes=[meanb])
                kb.emit("dve", lambda h: h.tensor_tensor(msq[:], mean[:], mean[:], ALU.mult), reads=[meanb], writes=[msqb])
                kb.emit("dve", lambda h: h.scalar_tensor_tensor(rstd[:], s2[0], 1.0 / D, msq[:], ALU.mult, ALU.subtract),
                        reads=[s2[1], msqb], writes=[rstdb])
                kb.emit("dve", lambda h: h.tensor_scalar(rstd[:], rstd[:], float(eps), None, ALU.add), reads=[rstdb], writes=[rstdb])
                kb.emit("act", lambda h: h.activation(out=rstd[:], in_=rstd[:], func=AF.Sqrt), reads=[rstdb], writes=[rstdb])
                kb.emit("dve", lambda h: h.reciprocal(rstd[:], rstd[:]), reads=[rstdb], writes=[rstdb])
            pieces.append(p_stats)

            def mk_norm(m):
                def p_norm():
                    zc = Zi[:, m, :]
                    T1, T1b = tt[m % 4], ttb[m % 4]
                    e = "dve" if m % 2 == 0 else "pool"
                    kb.emit(e, lambda h: h.tensor_tensor(T1[:], zc, mean[:], ALU.subtract), reads=[Zib[m], meanb], writes=[T1b])
                    kb.emit(e, lambda h: h.tensor_tensor(T1[:], T1[:], rstd[:], ALU.mult), reads=[T1b, rstdb], writes=[T1b])
                    kb.emit(e, lambda h: h.tensor_scalar(zc, T1[:], gbg[:, m:m + 1], gbv[:, m:m + 1], ALU.mult, ALU.add),
                            reads=[T1b, gbb], pwrites=[Zib[m]])
                return p_norm
            for m0 in range(0, DC, 2):
                pieces.append(lambda m0=m0: (mk_norm(m0)(), mk_norm(m0 + 1)()))
            pieces.append(lambda: epilogue_store(b))
            return pieces

        def epilogue_store(b):
            nonlocal ti
            Zi, Zib = Z[b % 3], Zb[b % 3]
            if True:
                pass
            if dst_mode == "fm":
                dst = Xdst[b].rearrange("p (c t) -> p c t", c=DC)
                kb.emit("sp", lambda h, dst=dst, Zi=Zi: h.dma_start(out=dst, in_=Zi[:]), reads=Zib,
                        writes=[cx.dram(Xdst, b)], dma_buf=Zib[0])
            else:
                for i in range(TB // 128):
                    Yt, Ytb = ytm[i % 2], ytmb[i % 2]
                    for q in range(2):
                        P, Pb = tp_ps[ti % 2]
                        ti += 1
                        for cc in range(4):
                            c = q * 4 + cc
                            o = P[:, cc * 128:(cc + 1) * 128]
                            f = lambda h, o=o, c=c, i=i, Zi=Zi: h.transpose(o, Zi[:, c, i * 128:(i + 1) * 128], cx.ident[:])
                            if cc == 0:
                                kb.emit("pe", f, reads=[Zib[c], cx.identb], writes=[Pb], signal=False)
                            else:
                                kb.emit("pe", f, reads=[Zib[c], cx.identb], pwrites=[Pb], signal=(cc == 3))
                        o = Yt[:, q * 512:(q + 1) * 512]
                        if q == 0:
                            kb.emit("act", lambda h, o=o, P=P: h.activation(out=o, in_=P[:], func=AF.Copy), reads=[Pb],
                                    writes=[Ytb])
                        else:
                            kb.emit("dve", lambda h, o=o, P=P: h.tensor_copy(o, P[:]), reads=[Pb], pwrites=[Ytb])
                    r0 = b * TB + i * 128
                    kb.emit("sp", lambda h, Yt=Yt, r0=r0: h.dma_start(out=Xdst[r0:r0 + 128, :], in_=Yt[:]),
                            reads=[Ytb], pwrites=[cx.outbuf], dma_buf=Ytb)
        load(0)
        for b in range(NB):
            if b + 1 < NB:
                load(b + 1)
            pcs = epilogue_pieces(b - 1) if b > 0 else []
            for m in range(DC):
                chunk(b, m)
                if m >= 1 and pcs:
                    pcs.pop(0)()
            while pcs:
                pcs.pop(0)()
        for pc_ in epilogue_pieces(NB - 1):
            pc_()
        kb.barrier()
        kb.recycle()


class DS:
    def __init__(self, nc, name, NB, W, dt):
        self.ap = nc.dram_tensor(name, [NB * 128, W], dt, kind="Internal").ap()
        self.bufs = [Buf("%s_%d" % (name, b)) for b in range(NB)]

    def __getitem__(self, b):
        return self.ap[b * 128:(b + 1) * 128, :]


def make_ctx(nc, kb, es):
    cx = Ctx()
    cx.dram = lambda ds, b: ds.bufs[b]
    cx.ps = [es.enter_context(nc.psum_tensor("ps%d" % i, [128, 512], F32)) for i in range(8)]
    cx.psb = [Buf("psb%d" % i) for i in range(8)]
    cx.stage = [(sb(es, nc, "stage%d" % i, [128, 1536], F32), Buf("stage%d" % i)) for i in range(2)]
    cx.ident = sb(es, nc, "ident", [128, 128], F32)
    cx.identb = Buf("ident")
    cx.ones = sb(es, nc, "ones", [128, 128], BF16)
    cx.onesb = Buf("ones")
    cx.outbuf = Buf("out")
    ident_d = nc.dram_tensor("c_ident", [128, 128], F32, kind="ExternalInput").ap()
    kb.emit("sp", lambda h: h.dma_start(out=cx.ident[:], in_=ident_d[:, :]), writes=[cx.identb], dma_buf=cx.identb)
    kb.emit("pool", lambda h: h.memset(cx.ones[:], 1.0), writes=[cx.onesb])
    return cx


def finish(kb, cx):
    E = kb.eng["sp"]
    for k, (s, v) in cx.outbuf.lw.items():
        if E.seen.get(k, 0) < v:
            E.h.wait_ge(s, v)
            E.seen[k] = v
    kb.barrier()


def const_inputs():
    return {"c_ident": np.eye(128, dtype=np.float32)}


SGH = 3072
SGC = 24


def phase_c1(kb, cx, XSin, HS, Win, wsT_d, bsbc_d, lng_d, lnb_d, NB):
    nc = kb.nc
    import contextlib
    with contextlib.ExitStack() as es:
        win = sb(es, nc, "win", [128, DC, 2 * SGH], BF16)
        winb = Buf("win")
        load_weight(kb, cx, Win, win, winb, DC, 2 * SGH, "sguin")
        wsT = sb(es, nc, "wsT", [128, 1, 1024], BF16)
        wsTb = Buf("wsT")
        load_weight(kb, cx, wsT_d, wsT, wsTb, 1, 1024, "wsT")
        bsbc = sb(es, nc, "bsbc", [128, 1024], F32)
        lng = sb(es, nc, "slng", [128, SGC], F32)
        lnb = sb(es, nc, "slnb", [128, SGC], F32)
        cb = Buf("sgconst")
        kb.emit("sp", lambda h: h.dma_start(out=bsbc[:], in_=bsbc_d[:, :]), writes=[cb], dma_buf=cb)
        kb.emit("sp", lambda h: h.dma_start(out=lng[:], in_=lng_d[:, :]), pwrites=[cb], dma_buf=cb)
        kb.emit("sp", lambda h: h.dma_start(out=lnb[:], in_=lnb_d[:, :]), pwrites=[cb], dma_buf=cb)
        bias2 = sb(es, nc, "bias2", [128, SGC, 128], F32)
        bias2b = Buf("bias2")
        for g in range(8):
            P, Pb = cx.ps[g % 2][:, 0:128], cx.psb[g % 2]
            kb.emit("pe", lambda h, P=P, g=g: h.matmul(P, cx.ones[:], wsT[:, 0, g * 128:(g + 1) * 128], start=True, stop=True),
                    reads=[wsTb, cx.onesb], writes=[Pb])
            for cc in range(3):
                c = g * 3 + cc
                f = lambda h, P=P, c=c, g=g: h.scalar_tensor_tensor(bias2[:, c, :], P, lnb[:, c:c + 1],
                                                                    bsbc[:, g * 128:(g + 1) * 128], ALU.mult, ALU.add)
                if c == 0:
                    kb.emit("dve", f, reads=[Pb, cb], writes=[bias2b])
                else:
                    kb.emit("dve", f, reads=[Pb, cb], pwrites=[bias2b])
        X = sb(es, nc, "cX", [128, DC, TB], F32)
        Xb_ = Buf("cX")
        xb = [sb(es, nc, "cxb%d" % i, [128, DC, TB], BF16) for i in range(2)]
        xbb = [Buf("cxb%d" % i) for i in range(2)]
        ufm = sb(es, nc, "ufm", [128, SGC, TB], BF16)
        ufmb = [Buf("ufm%d" % c) for c in range(SGC)]
        vtm = [sb(es, nc, "vtm%d" % i, [128, SGH], F32) for i in range(2)]
        vtmb = [Buf("vtm%d" % i) for i in range(2)]
        vn = sb(es, nc, "vn", [128, SGH], BF16)
        vnb = Buf("vn")
        uv = sb(es, nc, "uv", [128, SGC, TB], BF16)
        uvb = Buf("uv")
        st = [sb(es, nc, "sgst%d" % i, [128, 8], F32) for i in range(2)]
        stb = [Buf("sgst%d" % i) for i in range(2)]
        t1 = [sb(es, nc, "sgt%d" % i, [128, 128], F32) for i in range(2)]
        t1b = [Buf("sgt%d" % i) for i in range(2)]
        ui = 0
        vi = 0
        si = 0
        ti = 0
        def load(b):
            XB, XBb = xb[b % 2], xbb[b % 2]
            src = XSin[b].rearrange("p (c t) -> p c t", c=DC)
            kb.emit("sp", lambda h, src=src: h.dma_start(out=X[:], in_=src), reads=[cx.dram(XSin, b)], writes=[Xb_], dma_buf=Xb_)
            kb.emit("pool", lambda h, XB=XB: h.tensor_copy(XB[:], X[:]), reads=[Xb_], writes=[XBb])

        load(0)
        for b in range(NB):
            XB, XBb = xb[b % 2], xbb[b % 2]
            if b + 1 < NB:
                load(b + 1)
            for c in range(SGC):
                P, Pb = cx.ps[ui % 2][:, 0:TB], cx.psb[ui % 2]
                ui += 1
                for k in range(DC):
                    f = lambda h, P=P, k=k, c=c, XB=XB: h.matmul(P, win[:, k, c * 128:(c + 1) * 128], XB[:, k, :],
                                                                 start=(k == 0), stop=(k == DC - 1))
                    if k == 0:
                        kb.emit("pe", f, reads=[winb, XBb], writes=[Pb], signal=False)
                    else:
                        kb.emit("pe", f, reads=[winb, XBb], pwrites=[Pb], signal=(k == DC - 1))
                kb.emit("act", lambda h, c=c, P=P: h.activation(out=ufm[:, c, :], in_=P, func=AF.Gelu), reads=[Pb], writes=[ufmb[c]])
            for i in range(TB // 128):
                tsl = slice(i * 128, (i + 1) * 128)
                V, Vb = vtm[ti % 2], vtmb[ti % 2]
                S, Sb = st[ti % 2], stb[ti % 2]
                ti += 1
                for q in range(6):
                    P, Pb = cx.ps[2 + vi % 3], cx.psb[2 + vi % 3]
                    vi += 1
                    for k in range(DC):
                        f = lambda h, P=P, k=k, q=q, XB=XB, tsl=tsl: h.matmul(
                            P[:], XB[:, k, tsl], win[:, k, SGH + q * 512:SGH + (q + 1) * 512], start=(k == 0), stop=(k == DC - 1))
                        if k == 0:
                            kb.emit("pe", f, reads=[winb, XBb], writes=[Pb], signal=False)
                        else:
                            kb.emit("pe", f, reads=[winb, XBb], pwrites=[Pb], signal=(k == DC - 1))
                    o = V[:, q * 512:(q + 1) * 512]
                    f = lambda h, o=o, P=P: h.activation(out=o, in_=P[:], func=AF.Gelu)
                    if q == 0:
                        kb.emit("act", f, reads=[Pb], writes=[Vb])
                    else:
                        kb.emit("act", f, reads=[Pb], pwrites=[Vb])
                kb.emit("dve", lambda h, S=S, V=V: h.tensor_reduce(S[:, 0:1], V[:], AX.X, ALU.add), reads=[Vb], writes=[Sb])
                kb.emit("dve", lambda h, S=S, V=V: h.scalar_tensor_tensor(vn[:], V[:], 1.0, V[:], ALU.mult, ALU.mult,
                                                                         accum_out=S[:, 1:2]), reads=[Vb], writes=[vnb], pwrites=[Sb])
                kb.emit("dve", lambda h, S=S: h.tensor_scalar(S[:, 2:3], S[:, 0:1], 1.0 / SGH, None, ALU.mult), reads=[Sb], pwrites=[Sb])
                kb.emit("dve", lambda h, S=S: h.tensor_tensor(S[:, 3:4], S[:, 2:3], S[:, 2:3], ALU.mult), reads=[Sb], pwrites=[Sb])
                kb.emit("dve", lambda h, S=S: h.scalar_tensor_tensor(S[:, 4:5], S[:, 1:2], 1.0 / SGH, S[:, 3:4], ALU.mult, ALU.subtract),
                        reads=[Sb], pwrites=[Sb])
                kb.emit("dve", lambda h, S=S: h.tensor_scalar(S[:, 4:5], S[:, 4:5], float(LN_EPS), None, ALU.add), reads=[Sb], pwrites=[Sb])
                kb.emit("act", lambda h, S=S: h.activation(out=S[:, 5:6], in_=S[:, 4:5], func=AF.Sqrt), reads=[Sb], pwrites=[Sb])
                kb.emit("dve", lambda h, S=S: h.reciprocal(S[:, 6:7], S[:, 5:6]), reads=[Sb], pwrites=[Sb])
                kb.emit("dve", lambda h, S=S, V=V: h.tensor_scalar(vn[:], V[:], S[:, 2:3], S[:, 6:7], ALU.subtract, ALU.mult),
                        reads=[Vb, Sb], writes=[vnb])
                for c in range(SGC):
                    g = c // 3
                    P, Pb = cx.ps[5 + si % 2][:, 0:128], cx.psb[5 + si % 2]
                    T1, T1b = t1[si % 2], t1b[si % 2]
                    si += 1
                    kb.emit("pe", lambda h, P=P, c=c, g=g: h.matmul(P, vn[:, c * 128:(c + 1) * 128], wsT[:, 0, g * 128:(g + 1) * 128],
                                                                    start=True, stop=True), reads=[vnb, wsTb], writes=[Pb])
                    kb.emit("dve", lambda h, P=P, c=c, T1=T1: h.scalar_tensor_tensor(T1[:], P, lng[:, c:c + 1], bias2[:, c, :],
                                                                                     ALU.mult, ALU.add),
                            reads=[Pb, bias2b, cb], writes=[T1b])
                    o = uv[:, c, tsl]
                    f = lambda h, o=o, T1=T1, c=c, tsl=tsl: h.tensor_tensor(o, T1[:], ufm[:, c, tsl], ALU.mult)
                    if c == 0 and i == 0:
                        kb.emit("pool", f, reads=[T1b, ufmb[c]], writes=[uvb])
                    else:
                        kb.emit("pool", f, reads=[T1b, ufmb[c]], pwrites=[uvb])
            dst = HS[b][:, 0:SGC * TB].rearrange("p (c t) -> p c t", c=SGC)
            kb.emit("sp", lambda h, dst=dst: h.dma_start(out=dst, in_=uv[:]), reads=[uvb], writes=[cx.dram(HS, b)], dma_buf=uvb)
        kb.barrier()
        kb.recycle()


def _grp(kb, P, Pb, mms, extra_reads):
    n = len(mms)
    for i, (l, r) in enumerate(mms):
        f = lambda h, l=l, r=r, i=i: h.matmul(P, l, r, start=(i == 0), stop=(i == n - 1))
        if i == 0:
            kb.emit("pe", f, reads=extra_reads, writes=[Pb], signal=(n == 1))
        else:
            kb.emit("pe", f, reads=extra_reads, pwrites=[Pb], signal=(i == n - 1))


def phase_mla(kb, cx, XSin, HS, QS, Wm, Wuq, Wukv, qg_d, kvg_d, ropeC_d, ropeS_d, seqs):
    nc = kb.nc
    import contextlib
    SC = 96 ** -0.5
    with contextlib.ExitStack() as es:
        wm = sb(es, nc, "wm", [128, DC, 1216], BF16); wmb = Buf("wm")
        load_weight(kb, cx, Wm, wm, wmb, DC, 1216, "wm")
        wuq = sb(es, nc, "wuq", [128, 6, 1536], BF16); wuqb = Buf("wuq")
        load_weight(kb, cx, Wuq, wuq, wuqb, 6, 1536, "wuq")
        wukv = sb(es, nc, "wukv", [128, 2, 1024], BF16); wukvb = Buf("wukv")
        load_weight(kb, cx, Wukv, wukv, wukvb, 2, 1024, "wukv")
        qg = sb(es, nc, "qg", [128, 6], F32); kvg = sb(es, nc, "kvg", [128, 2], F32)
        onesf = sb(es, nc, "onesf", [128, 64], F32)
        cb = Buf("mconst")
        kb.emit("sp", lambda h: h.dma_start(out=qg[:], in_=qg_d[:, :]), writes=[cb], dma_buf=cb)
        kb.emit("sp", lambda h: h.dma_start(out=kvg[:], in_=kvg_d[:, :]), pwrites=[cb], dma_buf=cb)
        kb.emit("pool", lambda h: h.memset(onesf[:], 1.0), pwrites=[cb])
        SMAX = max(seqs)
        KT = sb(es, nc, "KT", [128, 8, SMAX], BF16); KTb = Buf("KT")
        VA = sb(es, nc, "VA", [128, SMAX // 128, 512], BF16); VAb = Buf("VA")
        X = sb(es, nc, "mX", [128, DC, TB], F32); Xb_ = Buf("mX")
        xb = sb(es, nc, "mxb", [128, DC, TB], BF16); xbb = Buf("mxb")
        cq = sb(es, nc, "cq", [128, 6, TB], BF16); cqb = Buf("cq")
        cqs = sb(es, nc, "cqs", [128, 6, TB], BF16); cqsb = Buf("cqs")
        ckv = sb(es, nc, "ckv", [128, 2, TB], BF16); ckvb = Buf("ckv")
        ckvs = sb(es, nc, "ckvs", [128, 2, TB], BF16); ckvsb = Buf("ckvs")
        rq = sb(es, nc, "rq", [128, TB], F32); rqb = Buf("rq")
        rkv = sb(es, nc, "rkv", [128, TB], F32); rkvb = Buf("rkv")
        rtok = sb(es, nc, "rtok", [128, 2], F32); rtokb = Buf("rtok")
        rc = sb(es, nc, "rc", [96, TB], F32); rs = sb(es, nc, "rs", [96, TB], F32); rcb = Buf("rcs")
        ta = sb(es, nc, "ta", [96, TB], F32); tab = Buf("ta")
        tb2 = sb(es, nc, "tb2", [96, TB], F32); tb2b = Buf("tb2")
        kro = sb(es, nc, "kro", [96, TB], BF16); krob = Buf("kro")
        qblk = sb(es, nc, "qblk", [96, 8, TB], BF16); qblkb = Buf("qblk")
        Q5 = sb(es, nc, "Q5", [128, 8, 2 * TB], BF16); Q5b = Buf("Q5")
        PT = [sb(es, nc, "PT%d" % i, [128, 512], BF16) for i in range(3)]; PTb = [Buf("PT%d" % i) for i in range(3)]
        rcp = [sb(es, nc, "rcp%d" % i, [128, 512], F32) for i in range(2)]; rcpb = [Buf("rcp%d" % i) for i in range(2)]
        oa = [sb(es, nc, "oa%d" % i, [128, 512], BF16) for i in range(2)]; oab = [Buf("oa%d" % i) for i in range(2)]
        zb_ = Buf("mzero")
        kb.emit("pool", lambda h: h.memset(KT[96:128, :, :], 0.0), writes=[zb_])
        kb.emit("pool", lambda h: h.memset(Q5[96:128, :, :], 0.0), pwrites=[zb_])
        ps, psb = cx.ps, cx.psb
        b0 = 0
        pi = 0
        for S in seqs:
            nb = S // TB
            for bl in range(nb):
                b = b0 + bl
                t0 = bl * TB
                def xload(bb):
                    src = XSin[bb].rearrange("p (c t) -> p c t", c=DC)
                    kb.emit("sp", lambda h, src=src: h.dma_start(out=X[:], in_=src), reads=[cx.dram(XSin, bb)], writes=[Xb_], dma_buf=Xb_)

                def xcast():
                    kb.emit("pool", lambda h: h.tensor_copy(xb[:], X[:]), reads=[Xb_], writes=[xbb])

                if bl == 0:
                    xload(b)
                    xcast()
                kb.emit("sp", lambda h, t0=t0: h.dma_start(out=rc[64:96, :], in_=ropeC_d[64:96, t0:t0 + TB]), writes=[rcb], dma_buf=rcb)
                kb.emit("sp", lambda h, t0=t0: h.dma_start(out=rs[64:96, :], in_=ropeS_d[64:96, t0:t0 + TB]), pwrites=[rcb], dma_buf=rcb)
                if bl + 1 < nb:
                    xload(b + 1)
                for j in range(8):
                    P, Pb = ps[j % 2][:, 0:TB], psb[j % 2]
                    _grp(kb, P, Pb, [(wm[:, k, j * 128:(j + 1) * 128], xb[:, k, :]) for k in range(DC)], [wmb, xbb])
                    if j < 6:
                        o1, o2, gsc, w1, w2 = cq[:, j, :], cqs[:, j, :], qg[:, j:j + 1], cqb, cqsb
                    else:
                        o1, o2, gsc, w1, w2 = ckv[:, j - 6, :], ckvs[:, j - 6, :], kvg[:, j - 6:j - 5], ckvb, ckvsb
                    kw1 = dict(writes=[w1]) if j in (0, 6) else dict(pwrites=[w1])
                    kw2 = dict(writes=[w2]) if j in (0, 6) else dict(pwrites=[w2])
                    kb.emit("act", lambda h, o1=o1, P=P, gsc=gsc: h.activation(out=o1, in_=P, func=AF.Copy, scale=gsc), reads=[Pb, cb], **kw1)
                    kb.emit("act", lambda h, o2=o2, P=P: h.activation(out=o2, in_=P, func=AF.Square), reads=[Pb], **kw2)
                Pk, Pkb = ps[2][0:96, 0:TB], psb[2]
                Pks, Pksb = ps[3][0:96, 0:TB], psb[3]
                _grp(kb, Pk, Pkb, [(wm[:, k, 1024:1120], xb[:, k, :]) for k in range(DC)], [wmb, xbb])
                _grp(kb, Pks, Pksb, [(wm[:, k, 1120:1216], xb[:, k, :]) for k in range(DC)], [wmb, xbb])
                if bl + 1 < nb:
                    xcast()
                kb.emit("dve", lambda h, Pk=Pk: h.tensor_tensor(ta[64:96, :], Pk[64:96, :], rc[64:96, :], ALU.mult), reads=[Pkb, rcb], writes=[tab])
                kb.emit("dve", lambda h, Pks=Pks: h.tensor_tensor(tb2[64:96, :], Pks[64:96, :], rs[64:96, :], ALU.mult), reads=[Pksb, rcb], writes=[tb2b])
                kb.emit("dve", lambda h: h.tensor_tensor(kro[64:96, :], ta[64:96, :], tb2[64:96, :], ALU.add), reads=[tab, tb2b], writes=[krob])
                for hd in range(8):
                    kw = dict(writes=[KTb]) if (bl == 0 and hd == 0) else dict(pwrites=[KTb])
                    kb.emit("pool", lambda h, hd=hd, t0=t0: h.tensor_copy(KT[64:96, hd, t0:t0 + TB], kro[64:96, :]), reads=[krob], **kw)
                Pq, Pqb = ps[4][:, 0:TB], psb[4]
                Pv, Pvb = ps[5][:, 0:TB], psb[5]
                _grp(kb, Pq, Pqb, [(cx.ones[:], cqs[:, j, :]) for j in range(6)], [cqsb, cx.onesb])
                _grp(kb, Pv, Pvb, [(cx.ones[:], ckvs[:, j, :]) for j in range(2)], [ckvsb, cx.onesb])
                for (R_, Rb, P, Pb, n) in ((rq, rqb, Pq, Pqb, 768), (rkv, rkvb, Pv, Pvb, 256)):
                    kb.emit("dve", lambda h, R_=R_, P=P, n=n: h.tensor_scalar(R_[:], P, 1.0 / n, float(RMS_EPS), ALU.mult, ALU.add), reads=[Pb], writes=[Rb])
                    kb.emit("act", lambda h, R_=R_: h.activation(out=R_[:], in_=R_[:], func=AF.Sqrt), reads=[Rb], writes=[Rb])
                    kb.emit("dve", lambda h, R_=R_: h.reciprocal(R_[:], R_[:]), reads=[Rb], writes=[Rb])
                for hd in range(8):
                    P, Pb = ps[hd % 2][0:96, 0:TB], psb[hd % 2]
                    P2, P2b = ps[2 + hd % 2][0:96, 0:TB], psb[2 + hd % 2]
                    _grp(kb, P, Pb, [(wuq[:, j, hd * 192:hd * 192 + 96], cq[:, j, :]) for j in range(6)], [wuqb, cqb])
                    _grp(kb, P2, P2b, [(wuq[:, j, hd * 192 + 96:hd * 192 + 192], cq[:, j, :]) for j in range(6)], [wuqb, cqb])
                    kw = dict(writes=[qblkb]) if hd == 0 else dict(pwrites=[qblkb])
                    kb.emit("dve", lambda h, P=P, hd=hd: h.scalar_tensor_tensor(qblk[0:64, hd, :], P[0:64, :], float(SC), rq[0:64, :], ALU.mult, ALU.mult),
                            reads=[Pb, rqb], **kw)
                    kb.emit("dve", lambda h, P=P: h.tensor_tensor(ta[64:96, :], P[64:96, :], rc[64:96, :], ALU.mult), reads=[Pb, rcb], writes=[tab])
                    kb.emit("dve", lambda h, P2=P2: h.tensor_tensor(tb2[64:96, :], P2[64:96, :], rs[64:96, :], ALU.mult), reads=[P2b, rcb], writes=[tb2b])
                    kb.emit("pool", lambda h: h.tensor_tensor(ta[64:96, :], ta[64:96, :], tb2[64:96, :], ALU.add), reads=[tab, tb2b], writes=[tab])
                    kb.emit("dve", lambda h, hd=hd: h.scalar_tensor_tensor(qblk[64:96, hd, :], ta[64:96, :], float(SC), rq[64:96, :], ALU.mult, ALU.mult),
                            reads=[tab, rqb], pwrites=[qblkb])
                kb.emit("sp", lambda h, b=b: h.dma_start(out=QS[b][0:96, :].rearrange("p (c t) -> p c t", c=8), in_=qblk[:]), reads=[qblkb],
                        writes=[cx.dram(QS, b)], dma_buf=qblkb)
                for hd in range(8):
                    P, Pb = ps[4 + hd % 2][0:64, 0:TB], psb[4 + hd % 2]
                    _grp(kb, P, Pb, [(wukv[:, j, hd * 64:(hd + 1) * 64], ckv[:, j, :]) for j in range(2)], [wukvb, ckvb])
                    kb.emit("dve", lambda h, P=P, hd=hd, t0=t0: h.tensor_tensor(KT[0:64, hd, t0:t0 + TB], P, rkv[0:64, :], ALU.mult),
                            reads=[Pb, rkvb], pwrites=[KTb])
                for i in range(TB // 128):
                    tsl = slice(i * 128, (i + 1) * 128)
                    Pr, Prb = ps[6][:, 0:1], psb[6]
                    _grp(kb, Pr, Prb, [(ckvs[:, j, tsl], cx.ones[:, 0:1]) for j in range(2)], [ckvsb, cx.onesb])
                    kb.emit("dve", lambda h, Pr=Pr: h.tensor_scalar(rtok[:, 0:1], Pr, 1.0 / 256, float(RMS_EPS), ALU.mult, ALU.add), reads=[Prb], writes=[rtokb])
                    kb.emit("act", lambda h: h.activation(out=rtok[:, 0:1], in_=rtok[:, 0:1], func=AF.Sqrt), reads=[rtokb], writes=[rtokb])
                    kb.emit("dve", lambda h: h.reciprocal(rtok[:, 1:2], rtok[:, 0:1]), reads=[rtokb], writes=[rtokb])
                    P, Pb = ps[7], psb[7]
                    _grp(kb, P[:], Pb, [(ckv[:, j, tsl], wukv[:, j, 512:1024]) for j in range(2)], [wukvb, ckvb])
                    tix = (t0 + i * 128) // 128
                    kb.emit("dve", lambda h, P=P, tix=tix: h.tensor_scalar(VA[:, tix, :], P[:], rtok[:, 1:2], None, ALU.mult),
                            reads=[Pb, rtokb], **(dict(writes=[VAb]) if tix == 0 else dict(pwrites=[VAb])))
            nkc = S // 128
            for qb in range(S // 512):
                for i in range(2):
                    b = b0 + qb * 2 + i
                    kw = dict(writes=[Q5b]) if i == 0 else dict(pwrites=[Q5b])
                    kb.emit("sp", lambda h, b=b, i=i: h.dma_start(out=Q5[0:96, :, i * TB:(i + 1) * TB], in_=QS[b][0:96, :].rearrange("p (c t) -> p c t", c=8)),
                            reads=[cx.dram(QS, b), zb_], dma_buf=Q5b, **kw)
                for hd in range(8):
                    hp, sub = hd // 2, hd % 2
                    r0 = sub * 64
                    O, Ob = ps[sub], psb[sub]
                    Dn, Dnb = ps[5 + sub], psb[5 + sub]
                    def emit_s(kc, hd=hd):
                        nonlocal pi
                        Sx, Sxb = ps[2 + pi % 3], psb[2 + pi % 3]
                        Pt, Ptb = PT[pi % 3], PTb[pi % 3]
                        pi += 1
                        kb.emit("pe", lambda h, Sx=Sx, hd=hd, kc=kc: h.matmul(Sx[:], KT[:, hd, kc * 128:(kc + 1) * 128], Q5[:, hd, :], start=True, stop=True),
                                reads=[KTb, Q5b, zb_], writes=[Sxb])
                        kb.emit("act", lambda h, Pt=Pt, Sx=Sx: h.activation(out=Pt[:], in_=Sx[:], func=AF.Exp), reads=[Sxb], writes=[Ptb])
                        return Pt, Ptb

                    pend = emit_s(0)
                    for kc in range(nkc):
                        Pt, Ptb = pend
                        if kc + 1 < nkc:
                            pend = emit_s(kc + 1)
                        f = lambda h, O=O, hp=hp, kc=kc, Pt=Pt: h.matmul(O[:], VA[:, kc, hp * 128:(hp + 1) * 128], Pt[:], start=(kc == 0), stop=(kc == nkc - 1))
                        g = lambda h, Dn=Dn, kc=kc, Pt=Pt: h.matmul(Dn[:], cx.ones[:], Pt[:], start=(kc == 0), stop=(kc == nkc - 1))
                        if kc == 0:
                            kb.emit("pe", f, reads=[VAb, Ptb], writes=[Ob], signal=(nkc == 1))
                            kb.emit("pe", g, reads=[cx.onesb, Ptb], writes=[Dnb], signal=(nkc == 1))
                        else:
                            kb.emit("pe", f, reads=[VAb, Ptb], pwrites=[Ob], signal=(kc == nkc - 1))
                            kb.emit("pe", g, reads=[cx.onesb, Ptb], pwrites=[Dnb], signal=(kc == nkc - 1))
                    R_, Rb = rcp[sub], rcpb[sub]
                    OA, OAb = oa[sub], oab[sub]
                    kb.emit("dve", lambda h, R_=R_, Dn=Dn, r0=r0: h.reciprocal(R_[r0:r0 + 64, :], Dn[r0:r0 + 64, :]), reads=[Dnb], writes=[Rb])
                    kb.emit("dve", lambda h, OA=OA, O=O, R_=R_, r0=r0: h.tensor_tensor(OA[r0:r0 + 64, :], O[r0:r0 + 64, :], R_[r0:r0 + 64, :], ALU.mult),
                            reads=[Ob, Rb], writes=[OAb])
                    for i in range(2):
                        b = b0 + qb * 2 + i
                        c0 = hp * TB
                        kb.emit("sp", lambda h, b=b, r0=r0, c0=c0, OA=OA, i=i: h.dma_start(out=HS[b][r0:r0 + 64, c0:c0 + TB], in_=OA[r0:r0 + 64, i * TB:(i + 1) * TB]),
                                reads=[OAb], pwrites=[cx.dram(HS, b)], dma_buf=OAb)
            b0 += nb
        kb.barrier()
        kb.recycle()


def mla_host_layouts(w_in, w_uq, w_ukv):
    z64 = np.zeros((w_in.shape[0], 64), np.float32)
    kr = w_in[:, 1024:1056]
    kr_sw = np.concatenate([kr[:, 16:32], kr[:, 0:16]], axis=1)
    wm = np.concatenate([w_in[:, 0:1024], z64, kr, z64, kr_sw], axis=1)
    cols = []
    for h in range(8):
        nope = w_uq[:, h * 96:h * 96 + 64]
        rope = w_uq[:, h * 96 + 64:h * 96 + 96]
        rope_sw = np.concatenate([rope[:, 16:32], rope[:, 0:16]], axis=1)
        cols += [nope, rope, nope, rope_sw]
    wuq = np.concatenate(cols, axis=1)
    kc = [w_ukv[:, h * 128:h * 128 + 64] for h in range(8)]
    vc = [w_ukv[:, h * 128 + 64:h * 128 + 128] for h in range(8)]
    wukv = np.concatenate(kc + vc, axis=1)
    return np.ascontiguousarray(wm), np.ascontiguousarray(wuq), np.ascontiguousarray(wukv)


def rope_tables_host(smax):
    inv = 1.0 / (10000.0 ** (np.arange(0, 32, 2, dtype=np.float32) / 32))
    ang = np.arange(smax, dtype=np.float32)[None, :] * inv[:, None]
    c = np.cos(ang).astype(np.float32)
    s_ = np.sin(ang).astype(np.float32)
    C = np.zeros((96, smax), np.float32)
    S = np.zeros((96, smax), np.float32)
    C[64:80] = c
    C[80:96] = c
    S[64:80] = -s_
    S[80:96] = s_
    return C, S


def gla_masks_host():
    s = np.arange(128)[:, None]
    t = np.arange(128)[None, :]
    A = (s <= t).astype(np.float32)
    B = (s > t).astype(np.float32)
    C = (s >= t).astype(np.float32)
    Dm = (s < t).astype(np.float32)
    return np.ascontiguousarray(np.concatenate([A, B, C, Dm, np.tile(A, (1, 4)), np.tile(B, (1, 4))], axis=1))


def phase_gla(kb, cx, XSin, HS, Wg, wgf_d, bgf_d, wgb_d, bgb_d, gn_d, cm_d, seqs):
    nc = kb.nc
    import contextlib
    with contextlib.ExitStack() as es:
        wg = sb(es, nc, "wg", [128, DC, 1568], BF16); wgb_ = Buf("wg")
        load_weight(kb, cx, Wg, wg, wgb_, DC, 1568, "wg")
        wgate = [sb(es, nc, "wgate%d" % i, [16, 256], F32) for i in range(2)]
        bgate = [sb(es, nc, "bgate%d" % i, [16, 256], F32) for i in range(2)]
        gn = sb(es, nc, "gn", [128, 2], F32)
        cm = sb(es, nc, "cm", [128, 1536], F32)
        onesf = sb(es, nc, "gonesf", [1, 128], F32)
        cb = Buf("gconst")
        kb.emit("sp", lambda h: h.dma_start(out=cm[:], in_=cm_d[:, :]), writes=[cb], dma_buf=cb)
        for i, (wd_, bd_) in enumerate(((wgf_d, bgf_d), (wgb_d, bgb_d))):
            kb.emit("sp", lambda h, i=i, wd_=wd_: h.dma_start(out=wgate[i][:], in_=wd_[:, :]), pwrites=[cb], dma_buf=cb)
            kb.emit("sp", lambda h, i=i, bd_=bd_: h.dma_start(out=bgate[i][:], in_=bd_[:, :]), pwrites=[cb], dma_buf=cb)
        kb.emit("sp", lambda h: h.dma_start(out=gn[:], in_=gn_d[:, :]), pwrites=[cb], dma_buf=cb)
        kb.emit("pool", lambda h: h.memset(onesf[:], 1.0), pwrites=[cb])
        wgate16 = [sb(es, nc, "wgate16_%d" % i, [16, 256], BF16) for i in range(2)]
        bgate16 = [sb(es, nc, "bgate16_%d" % i, [16, 256], BF16) for i in range(2)]
        cm16 = sb(es, nc, "cm16", [128, 512], BF16)
        cb16 = Buf("gconst16")
        kb.emit("dve", lambda h: h.tensor_copy(cm16[:], cm[:, 0:512]), reads=[cb], writes=[cb16])
        for i in range(2):
            kb.emit("dve", lambda h, i=i: h.tensor_copy(wgate16[i][:], wgate[i][:]), reads=[cb], pwrites=[cb16])
            kb.emit("dve", lambda h, i=i: h.tensor_copy(bgate16[i][:], bgate[i][:]), reads=[cb], pwrites=[cb16])
        SMAX = max(seqs)
        ofwd = sb(es, nc, "ofwd", [128, 4, SMAX], F32); ofwdb = Buf("ofwd")
        X = sb(es, nc, "gX", [128, DC, TB], F32); Xb_ = Buf("gX")
        xb = sb(es, nc, "gxb", [128, DC, TB], BF16); xbb = Buf("gxb")
        zs = sb(es, nc, "zs", [16, 128], BF16); zsb = Buf("zs")
        e1 = sb(es, nc, "e1", [128, 256], F32); e1b = Buf("e1")
        L = sb(es, nc, "L", [128, 256], BF16); Lb = Buf("L")
        eb = sb(es, nc, "eb", [128, 256], F32); ebb = Buf("eb")
        enb = sb(es, nc, "enb", [128, 256], F32); enbb = Buf("enb")
        est = sb(es, nc, "est", [128, 256], F32); estb = Buf("est")
        qin = sb(es, nc, "qin", [128, 256], BF16); qinb = Buf("qin")
        kin = sb(es, nc, "kin", [128, 256], BF16); kinb = Buf("kin")
        kst = sb(es, nc, "kst", [128, 256], BF16); kstb = Buf("kst")
        vbf = sb(es, nc, "vbf", [128, 512], BF16); vbfb = Buf("vbf")
        attm = sb(es, nc, "attm", [128, 512], BF16); attmb = Buf("attm")
        St = [sb(es, nc, "St%d" % i, [128, 256], F32) for i in range(2)]; Stb = [Buf("St%d" % i) for i in range(2)]
        Sbf = [sb(es, nc, "Sbf%d" % i, [128, 256], BF16) for i in range(2)]; Sbfb = [Buf("Sbf%d" % i) for i in range(2)]
        osum = sb(es, nc, "osum", [128, 512], F32); osumb = Buf("osum")
        osq = sb(es, nc, "osq", [128, 512], BF16); osqb = Buf("osq")
        rn = sb(es, nc, "rn", [128, 512], F32); rnb = Buf("rn")
        sr = sb(es, nc, "sr", [128, 512], F32); srb = Buf("sr")
        obk = sb(es, nc, "obk", [128, 4, TB], BF16); obkb = Buf("obk")
        ps, psb = cx.ps, cx.psb
        b0 = 0
        for S in seqs:
            nb = S // TB
            for d in range(2):
                for i in range(2):
                    kb.emit("pool", lambda h, i=i: h.memset(St[i][:], 0.0), writes=[Stb[i]])
                    kb.emit("pool", lambda h, i=i: h.memset(Sbf[i][:], 0.0), writes=[Sbfb[i]])
                blks = range(nb) if d == 0 else range(nb - 1, -1, -1)
                mi, ms, ma = ((0, 128, 512), (256, 384, 1024))[d]
                dcol = 127 if d == 0 else 0
                for bl in blks:
                    b = b0 + bl
                    src = XSin[b].rearrange("p (c t) -> p c t", c=DC)
                    kb.emit("sp", lambda h, src=src: h.dma_start(out=X[:], in_=src), reads=[cx.dram(XSin, b)], writes=[Xb_], dma_buf=Xb_)
                    kb.emit("pool", lambda h: h.tensor_copy(xb[:], X[:]), reads=[Xb_], writes=[xbb])
                    tiles = range(2) if d == 0 else range(1, -1, -1)
                    for ti_, i in enumerate(tiles):
                        tsl = slice(i * 128, (i + 1) * 128)
                        t0 = bl * TB + i * 128
                        rx = [wgb_, xbb]
                        for g4 in range(4):
                            f0 = dict(writes=[psb[0]]) if g4 == 0 else dict(pwrites=[psb[0]])
                            for k in range(DC):
                                kb.emit("pe", lambda h, g4=g4, k=k, tsl=tsl: h.matmul(ps[0][:, g4 * 128:(g4 + 1) * 128], wg[:, k, g4 * 128:(g4 + 1) * 128], xb[:, k, tsl],
                                                                                   start=(k == 0), stop=(k == DC - 1)),
                                        reads=rx, signal=(k == DC - 1), **(f0 if k == 0 else dict(pwrites=[psb[0]])))
                        _grp(kb, ps[1][:, 0:256], psb[1], [(xb[:, k, tsl], wg[:, k, 256:512]) for k in range(DC)], rx)
                        _grp(kb, ps[2][:], psb[2], [(xb[:, k, tsl], wg[:, k, 512:1024]) for k in range(DC)], rx)
                        _grp(kb, ps[3][0:16, 0:128], psb[3], [(wg[:, k, 1536 + 16 * d:1552 + 16 * d], xb[:, k, tsl]) for k in range(DC)], rx)
                        kb.emit("act", lambda h: h.activation(out=zs[:], in_=ps[3][0:16, 0:128], func=AF.Copy), reads=[psb[3]], writes=[zsb])
                        _grp(kb, ps[4][:, 0:256], psb[4], [(zs[:], wgate16[d][:]), (cx.ones[0:1, :], bgate16[d][0:1, :])], [zsb, cb16, cx.onesb])
                        kb.emit("act", lambda h: h.activation(out=e1[:], in_=ps[4][:, 0:256], func=AF.Exp, scale=-1.0), reads=[psb[4]], writes=[e1b])
                        kb.emit("act", lambda h: h.activation(out=L[:], in_=e1[:], func=AF.Ln, bias=1.0), reads=[e1b], writes=[Lb])
                        if "G1" in DBG:
                            continue
                        for pr in range(2):
                            kb.emit("pe", lambda h, pr=pr: h.matmul(ps[5][:, pr * 128:(pr + 1) * 128], L[:, pr * 128:(pr + 1) * 128], cm16[:, mi:mi + 128], start=True, stop=True),
                                    reads=[Lb, cb16], **(dict(writes=[psb[5]]) if pr == 0 else dict(pwrites=[psb[5]])))
                        kb.emit("pe", lambda h: h.matmul(ps[6][:, 0:256], cm16[:, ms:ms + 128], L[:], start=True, stop=True), reads=[Lb, cb16], writes=[psb[6]])
                        kb.emit("act", lambda h: h.activation(out=eb[:], in_=ps[5][:, 0:256], func=AF.Exp, scale=-1.0 / 16), reads=[psb[5]], writes=[ebb])
                        kb.emit("act", lambda h: h.activation(out=enb[:], in_=ps[5][:, 0:256], func=AF.Exp, scale=1.0 / 16), reads=[psb[5]], writes=[enbb])
                        kb.emit("act", lambda h: h.activation(out=est[:], in_=ps[6][:, 0:256], func=AF.Exp, scale=-1.0 / 16), reads=[psb[6]], writes=[estb])
                        kb.emit("dve", lambda h: h.scalar_tensor_tensor(qin[:], ps[0][:, 0:256], 0.125, eb[:], ALU.mult, ALU.mult), reads=[psb[0], ebb], writes=[qinb])
                        kb.emit("dve", lambda h: h.tensor_tensor(kin[:], ps[0][:, 256:512], enb[:], ALU.mult), reads=[psb[0], enbb], writes=[kinb])
                        kb.emit("dve", lambda h: h.tensor_tensor(kst[:], ps[1][:, 0:256], est[:], ALU.mult), reads=[psb[1], estb], writes=[kstb])
                        kb.emit("act", lambda h: h.activation(out=vbf[:], in_=ps[2][:], func=AF.Copy), reads=[psb[2]], writes=[vbfb])
                        if "G2" in DBG:
                            continue
                        attb = (7, 0)
                        ob_ = (3, 1)
                        for hd in range(4):
                            pr, r0, par = hd // 2, (hd % 2) * 64, hd % 2
                            bk = attb[par]
                            kb.emit("pe", lambda h, bk=bk, pr=pr, r0=r0: h.matmul(ps[bk][:, pr * 128:(pr + 1) * 128], kin[r0:r0 + 64, pr * 128:(pr + 1) * 128],
                                                                                  qin[r0:r0 + 64, pr * 128:(pr + 1) * 128], start=True, stop=True),
                                    reads=[kinb, qinb], **(dict(writes=[psb[bk]]) if pr == 0 else dict(pwrites=[psb[bk]])))
                        attm4 = attm[:].rearrange("p (a b t) -> p a b t", a=2, b=2)
                        mk = cm[:, ma:ma + 256].rearrange("p (a t) -> p a t", a=2)
                        for par in range(2):
                            bk = attb[par]
                            kw = dict(writes=[attmb]) if par == 0 else dict(pwrites=[attmb])
                            kb.emit("dve", lambda h, bk=bk, par=par: h.tensor_tensor(attm4[:, :, par, :], ps[bk][:, 0:256].rearrange("p (a t) -> p a t", a=2), mk, ALU.mult),
                                    reads=[psb[bk], cb], **kw)
                        for hd in range(4):
                            pr, r0, par = hd // 2, (hd % 2) * 64, hd % 2
                            bk = ob_[par]
                            O = ps[bk][:, pr * 128:(pr + 1) * 128]
                            kb.emit("pe", lambda h, O=O, hd=hd: h.matmul(O, vbf[:, hd * 128:(hd + 1) * 128], attm[:, hd * 128:(hd + 1) * 128], start=True, stop=False),
                                    reads=[vbfb, attmb], signal=False, **(dict(writes=[psb[bk]]) if pr == 0 else dict(pwrites=[psb[bk]])))
                            kb.emit("pe", lambda h, O=O, hd=hd, pr=pr, r0=r0: h.matmul(O, Sbf[pr][r0:r0 + 64, (hd % 2) * 128:(hd % 2 + 1) * 128],
                                                                                     qin[r0:r0 + 64, pr * 128:(pr + 1) * 128], start=False, stop=True),
                                    reads=[Sbfb[pr], qinb], pwrites=[psb[bk]])
                        if "G3" in DBG:
                            continue
                        for pr in range(2):
                            Dp, Dpb = ps[5 + pr][:, 0:256], psb[5 + pr]
                            kb.emit("pe", lambda h, Dp=Dp, pr=pr: h.matmul(Dp, kst[:, pr * 128:(pr + 1) * 128], vbf[:, pr * 256:(pr + 1) * 256], start=True, stop=True),
                                    reads=[kstb, vbfb], writes=[Dpb])
                            dc = pr * 128 + dcol
                            kb.emit("dve", lambda h, Dp=Dp, pr=pr, dc=dc: h.scalar_tensor_tensor(St[pr][:], St[pr][:], eb[:, dc:dc + 1], Dp, ALU.mult, ALU.add),
                                    reads=[Stb[pr], ebb, Dpb], writes=[Stb[pr]])
                            kb.emit("act", lambda h, pr=pr: h.activation(out=Sbf[pr][:], in_=St[pr][:], func=AF.Copy), reads=[Stb[pr]], writes=[Sbfb[pr]])
                        if "G4" in DBG:
                            continue
                        if d == 0:
                            of4 = ofwd[:].rearrange("p (a b) s -> p a b s", a=2, b=2)
                            for par in range(2):
                                bk = ob_[par]
                                kw = dict(writes=[ofwdb]) if (bl == 0 and i == 0 and par == 0) else dict(pwrites=[ofwdb])
                                kb.emit("act", lambda h, t0=t0, bk=bk, par=par: h.activation(out=of4[:, :, par, t0:t0 + 128], in_=ps[bk][:, 0:256].rearrange("p (a t) -> p a t", a=2),
                                                                                         func=AF.Copy), reads=[psb[bk]], **kw)
                        else:
                            of4 = ofwd[:].rearrange("p (a b) s -> p a b s", a=2, b=2)
                            os4 = osum[:].rearrange("p (a b t) -> p a b t", a=2, b=2)
                            for par in range(2):
                                bk = ob_[par]
                                kw = dict(writes=[osumb]) if par == 0 else dict(pwrites=[osumb])
                                kb.emit("dve", lambda h, t0=t0, bk=bk, par=par: h.tensor_tensor(os4[:, :, par, :], ps[bk][:, 0:256].rearrange("p (a t) -> p a t", a=2),
                                                                                            of4[:, :, par, t0:t0 + 128], ALU.add), reads=[psb[bk], ofwdb], **kw)
                            kb.emit("pool", lambda h: h.tensor_tensor(osq[:], osum[:], osum[:], ALU.mult), reads=[osumb], writes=[osqb])
                            kb.emit("pe", lambda h: h.matmul(ps[7][:], cx.ones[:], osq[:], start=True, stop=True), reads=[osqb, cx.onesb], writes=[psb[7]])
                            kb.emit("dve", lambda h: h.tensor_scalar(rn[:], ps[7][:], 1.0 / 128, float(RMS_EPS), ALU.mult, ALU.add), reads=[psb[7]], writes=[rnb])
                            kb.emit("act", lambda h: h.activation(out=rn[:], in_=rn[:], func=AF.Sqrt), reads=[rnb], writes=[rnb])
                            kb.emit("dve", lambda h: h.reciprocal(rn[:], rn[:]), reads=[rnb], writes=[rnb])
                            kb.emit("dve", lambda h: h.tensor_tensor(osum[:], osum[:], rn[:], ALU.mult), reads=[osumb, rnb], writes=[osumb])
                            for hd in range(4):
                                f0 = dict(writes=[psb[4]]) if hd == 0 else dict(pwrites=[psb[4]])
                                for k in range(DC):
                                    kb.emit("pe", lambda h, hd=hd, k=k, tsl=tsl: h.matmul(ps[4][:, hd * 128:(hd + 1) * 128], wg[:, k, 1024 + hd * 128:1024 + (hd + 1) * 128],
                                                                                       xb[:, k, tsl], start=(k == 0), stop=(k == DC - 1)),
                                            reads=rx, signal=(k == DC - 1), **(f0 if k == 0 else dict(pwrites=[psb[4]])))
                            kb.emit("act", lambda h: h.activation(out=sr[:], in_=ps[4][:], func=AF.Silu), reads=[psb[4]], writes=[srb])
                            kw = dict(writes=[obkb]) if ti_ == 0 else dict(pwrites=[obkb])
                            kb.emit("dve", lambda h, tsl=tsl: h.scalar_tensor_tensor(obk[:, :, tsl], osum[:].rearrange("p (h t) -> p h t", h=4), gn[:, 0:1],
                                                                                    sr[:].rearrange("p (h t) -> p h t", h=4), ALU.mult, ALU.mult),
                                    reads=[osumb, srb, cb], **kw)
                    if d == 1:
                        dst = HS[b][:, 4 * TB:8 * TB].rearrange("p (c t) -> p c t", c=4)
                        kb.emit("sp", lambda h, dst=dst: h.dma_start(out=dst, in_=obk[:]), reads=[obkb], pwrites=[cx.dram(HS, b)], dma_buf=obkb)
            b0 += nb
        kb.barrier()
        kb.recycle()


SEQS = [2048, 2048, 4096, 4096]


def build_full(seqs):
    import contextlib
    T = sum(seqs)
    NB = T // TB
    nc = bass.Bass("TRN2", target_bir_lowering=False)

    def din(name, shape):
        return nc.dram_tensor(name, list(shape), F32, kind="ExternalInput").ap()

    x = din("x", [T, D])
    y = nc.dram_tensor("y", [T, D], F32, kind="ExternalOutput").ap()
    W = {}
    for l in ("l0", "l1"):
        for f in ("ffa", "ffb"):
            W[l + f + "gu"] = din(l + f + "gu", [D, 2 * FH])
            W[l + f + "dn"] = din(l + f + "dn", [FH, D])
        for i in (1, 2, 3):
            W[l + "g%d" % i] = din(l + "g%d" % i, [128, DC])
            W[l + "b%d" % i] = din(l + "b%d" % i, [128, DC])
    wm = din("wm", [D, 1216]); wuq = din("wuq", [768, 1536]); wukv = din("wukv", [256, 1024])
    qg = din("qg", [128, 6]); kvg = din("kvg", [128, 2])
    SMAX = max(seqs)
    rC = din("rC", [96, SMAX]); rS = din("rS", [96, SMAX])
    wg = din("wg", [D, 1568]); wgf = din("wgf", [16, 256]); bgf = din("bgf", [16, 256]); wgb = din("wgb", [16, 256]); bgb = din("bgb", [16, 256])
    gn = din("gn", [128, 2]); cm = din("cm", [128, 1536])
    wout0 = din("wout0", [1024, D])
    swin = din("swin", [D, 6144]); swout = din("swout", [3072, D]); wsT = din("wsT", [128, 1024]); bsbc = din("bsbc", [128, 1024])
    slng = din("slng", [128, 24]); slnb = din("slnb", [128, 24])
    with contextlib.ExitStack() as es:
        kb = KB(nc, es)
        cx = make_ctx(nc, kb, es)
        XA = DS(nc, "XA", NB, DC * TB, F32)
        XB = DS(nc, "XB", NB, DC * TB, F32)
        HS = DS(nc, "HS", NB, 24 * TB, BF16)
        QS = DS(nc, "QS", NB, 8 * TB, BF16)
        phase_f1(kb, cx, es, "tm", x, XA, HS, W["l0ffagu"], NB)
        phase_f2(kb, cx, HS, FHC, W["l0ffadn"], 0.5, W["l0g1"], W["l0b1"], XA, "fm", XB, NB, "a")
        phase_mla(kb, cx, XB, HS, QS, wm, wuq, wukv, qg, kvg, rC, rS, seqs)
        phase_gla(kb, cx, XB, HS, wg, wgf, bgf, wgb, bgb, gn, cm, seqs)
        phase_f2(kb, cx, HS, 8, wout0, 1.0, W["l0g2"], W["l0b2"], XB, "fm", XA, NB, "b")
        phase_f1(kb, cx, es, "fm", XA, None, HS, W["l0ffbgu"], NB)
        phase_f2(kb, cx, HS, FHC, W["l0ffbdn"], 0.5, W["l0g3"], W["l0b3"], XA, "fm", XB, NB, "c")
        phase_f1(kb, cx, es, "fm", XB, None, HS, W["l1ffagu"], NB)
        phase_f2(kb, cx, HS, FHC, W["l1ffadn"], 0.5, W["l1g1"], W["l1b1"], XB, "fm", XA, NB, "d")
        phase_c1(kb, cx, XA, HS, swin, wsT, bsbc, slng, slnb, NB)
        phase_f2(kb, cx, HS, SGC, swout, 1.0, W["l1g2"], W["l1b2"], XA, "fm", XB, NB, "e")
        phase_f1(kb, cx, es, "fm", XB, None, HS, W["l1ffbgu"], NB)
        phase_f2(kb, cx, HS, FHC, W["l1ffbdn"], 0.5, W["l1g3"], W["l1b3"], XB, "tm", y, NB, "f")
        finish(kb, cx)
    return nc


def shared_inputs(p, seqs):
    pc = lambda v, c: np.ascontiguousarray(v.reshape(c, 128).T.astype(np.float32))
    shared = dict(const_inputs())
    for l in ("l0", "l1"):
        for f in ("ffa", "ffb"):
            shared[l + f + "gu"] = p["%s_%s_w_gu" % (l, f)]
            shared[l + f + "dn"] = p["%s_%s_w_down" % (l, f)]
        for i in (1, 2, 3):
            shared[l + "g%d" % i] = pc(p["%s_ln%d_g" % (l, i)], DC)
            shared[l + "b%d" % i] = pc(p["%s_ln%d_b" % (l, i)], DC)
    WM, WUQ, WUKV = mla_host_layouts(p["l0_w_in"], p["l0_mla_w_uq"], p["l0_mla_w_ukv"])
    C, S_ = rope_tables_host(max(seqs))
    shared.update(wm=WM, wuq=WUQ, wukv=WUKV, qg=pc(p["l0_mla_q_norm"], 6), kvg=pc(p["l0_mla_kv_norm"], 2), rC=C, rS=S_,
                  wg=np.ascontiguousarray(p["l0_w_in"][:, 1056:2624]), wgf=p["l0_gla_w_gate_f"], bgf=np.concatenate([p["l0_gla_b_gate_f"].reshape(1, 256), np.zeros((15, 256), np.float32)], 0),
                  wgb=p["l0_gla_w_gate_b"], bgb=np.concatenate([p["l0_gla_b_gate_b"].reshape(1, 256), np.zeros((15, 256), np.float32)], 0), gn=np.concatenate([p["l0_gla_norm"].reshape(128, 1)] * 2, 1), cm=gla_masks_host(),
                  wout0=p["l0_w_out"], swin=p["l1_sgu_w_in"], swout=p["l1_sgu_w_out"],
                  wsT=np.ascontiguousarray(p["l1_sgu_w_s"].transpose(2, 0, 1).reshape(128, 1024)),
                  bsbc=np.ascontiguousarray(np.broadcast_to(p["l1_sgu_b_s"].reshape(1, 1024), (128, 1024))),
                  slng=pc(p["l1_sgu_ln_g"], 24), slnb=pc(p["l1_sgu_ln_b"], 24))
    return {k: np.ascontiguousarray(v, dtype=np.float32) for k, v in shared.items()}


def kernel(**inp):
    p = {k: np.asarray(v) for k, v in inp.items()}
    seqs = SEQS
    nc = build_full(seqs)
    shared = shared_inputs(p, seqs)
    xp, xs = p["x_prompt"], p["x_sample"]
    in_maps = []
    for c in range(N_CORES):
        xc = np.concatenate([xp[2 * c].reshape(-1, D), xp[2 * c + 1].reshape(-1, D), xs[2 * c].reshape(-1, D), xs[2 * c + 1].reshape(-1, D)], axis=0)
        m = dict(shared)
        m["x"] = np.ascontiguousarray(xc)
        in_maps.append(m)
    res = run_bass_kernel_spmd(nc, in_maps, core_ids=list(range(N_CORES)))
    yp = np.empty_like(xp)
    ys = np.empty_like(xs)
    for c in range(N_CORES):
        yc = res.results[c]["y"]
        yp[2 * c] = yc[0:2048]
        yp[2 * c + 1] = yc[2048:4096]
        ys[2 * c] = yc[4096:8192]
        ys[2 * c + 1] = yc[8192:12288]
    return (yp, ys)
```
